# Optimizing a Trainium2 kernel written in Bass

```python
import jax, jax.numpy as jnp
from jax import lax
import numpy as np

D_MODEL = 1024
BATCH = 4
SEQ = 4096
DEPTH = 1
DEC_BATCH = 32
DEC_SEQ = 4
PAST_LEN = 8192
PAGE_SIZE = 128

N_HEADS = 8
HEAD_DIM = 64
N_KV_HEADS = 2
GROUP = N_HEADS // N_KV_HEADS
ROT_DIM = HEAD_DIM // 4
ROPE_THETA = 500000.0
CMP_BLOCK = 32
CMP_STRIDE = 16
CMP_HIDDEN = 128
SEL_BLOCK = 64
N_SEL = 16
WINDOW = 512
FORCED_SCORE = 1.0e4
CONV_CH = 512
CONV_WIDTH = 31
N_MEM = 256
X_HEADS = 4
X_HEAD_DIM = 128
D_FF = 4 * D_MODEL
QUERY_BLOCK = 128
EPS = 1e-6
Q_W = N_HEADS * HEAD_DIM
KV_W = N_KV_HEADS * HEAD_DIM
X_W = X_HEADS * X_HEAD_DIM
IN_W = Q_W + 6 * KV_W + 3 * N_HEADS + 2 * CONV_CH + 2 * D_MODEL

kernel_name = 'nsa_conformer_gated_hybrid_step'


def rms_norm(x, g):
    xf = x.astype(jnp.float32)
    y = xf * lax.rsqrt(jnp.mean(xf * xf, axis=-1, keepdims=True) + EPS)
    return (y * g.astype(jnp.float32)).astype(x.dtype)


def layer_norm(x, g, b):
    xf = x.astype(jnp.float32)
    xc = xf - jnp.mean(xf, axis=-1, keepdims=True)
    var = jnp.mean(xc * xc, axis=-1, keepdims=True)
    return (xc * lax.rsqrt(var + EPS) * g.astype(jnp.float32) + b.astype(jnp.float32)).astype(x.dtype)


def rope(x, pos):
    half = ROT_DIM // 2
    inv = ROPE_THETA ** (-jnp.arange(half, dtype=jnp.float32) / half)
    ang = pos.astype(jnp.float32)[:, None] * inv[None, :]
    cos = jnp.cos(ang)[:, None, :]
    sin = jnp.sin(ang)[:, None, :]
    xr = x[..., :ROT_DIM].astype(jnp.float32)
    x1, x2 = xr[..., :half], xr[..., half:]
    rot = jnp.concatenate([x1 * cos - x2 * sin, x2 * cos + x1 * sin], axis=-1)
    return jnp.concatenate([rot.astype(x.dtype), x[..., ROT_DIM:]], axis=-1)


def masked_softmax(s, mask):
    s = s.astype(jnp.float32)
    m = jnp.max(jnp.where(mask, s, -jnp.inf), axis=-1, keepdims=True)
    m = jnp.where(jnp.isfinite(m), m, 0.0)
    e = jnp.where(mask, jnp.exp(s - m), 0.0)
    return e / jnp.maximum(jnp.sum(e, axis=-1, keepdims=True), 1e-30)


def split_points():
    sizes = [Q_W] + [KV_W] * 6 + [3 * N_HEADS, 2 * CONV_CH, 2 * D_MODEL]
    return [int(v) for v in np.cumsum(sizes)[:-1]]


def mixer_inputs(h, w_in, pos):
    bsz, L = h.shape[0], h.shape[1]
    q, k_cmp, v_cmp, k_slc, v_slc, k_win, v_win, gates, glu_in, merge = jnp.split(h @ w_in, split_points(), axis=-1)
    kvh = lambda a: a.reshape(bsz, L, N_KV_HEADS, HEAD_DIM)
    q = rope(q.reshape(bsz, L, N_HEADS, HEAD_DIM), pos).reshape(bsz, L, N_KV_HEADS, GROUP, HEAD_DIM)
    k_cmp = rope(kvh(k_cmp), pos)
    k_slc = rope(kvh(k_slc), pos)
    k_win = rope(kvh(k_win), pos)
    gates = gates.reshape(bsz, L, N_KV_HEADS, GROUP, 3)
    a, b = jnp.split(glu_in, 2, axis=-1)
    glu = a * jax.nn.sigmoid(b)
    return q, k_cmp, kvh(v_cmp), k_slc, kvh(v_slc), k_win, kvh(v_win), gates, glu, merge


def compress(k, pos_emb, w1, w2):
    bsz, L = k.shape[0], k.shape[1]
    nc = (L - CMP_BLOCK) // CMP_STRIDE + 1
    r = CMP_BLOCK // CMP_STRIDE
    chunks = k[:, :(nc + r - 1) * CMP_STRIDE].reshape(bsz, nc + r - 1, CMP_STRIDE, N_KV_HEADS, HEAD_DIM)
    blocks = jnp.concatenate([chunks[:, i:i + nc] for i in range(r)], axis=2)
    blocks = blocks + pos_emb[None, None, :, None, :]
    flat = blocks.transpose(0, 1, 3, 2, 4).reshape(bsz, nc, N_KV_HEADS, CMP_BLOCK * HEAD_DIM)
    return jax.nn.gelu(flat @ w1) @ w2


def cmp_to_sel(nc, ns):
    start = np.arange(nc)[:, None] * CMP_STRIDE
    sel0 = np.arange(ns)[None, :] * SEL_BLOCK
    ov = np.clip(np.minimum(start + CMP_BLOCK, sel0 + SEL_BLOCK) - np.maximum(start, sel0), 0, None)
    return jnp.asarray(ov / CMP_BLOCK, dtype=jnp.float32)


def to_sel_blocks(k):
    bsz, L = k.shape[0], k.shape[1]
    ns = -(-L // SEL_BLOCK)
    k = jnp.pad(k, ((0, 0), (0, ns * SEL_BLOCK - L), (0, 0), (0, 0)))
    return k.reshape(bsz, ns, SEL_BLOCK, N_KV_HEADS, HEAD_DIM).transpose(0, 3, 1, 2, 4)


def gather_pages(pool, page_table):
    g = pool[page_table]
    return g.reshape(g.shape[0], g.shape[1] * g.shape[2], N_KV_HEADS, HEAD_DIM)


def nsa_attend(q, t, kc, vc, ks, vs, kw, vw, pw, gates):
    scale = HEAD_DIM ** -0.5
    bsz, nq = q.shape[0], q.shape[1]
    nc, ns = kc.shape[1], ks.shape[2]
    c_end = jnp.arange(nc, dtype=jnp.int32) * CMP_STRIDE + (CMP_BLOCK - 1)
    c_mask = (c_end[None, :] <= t[:, None])[None, :, None, None, :]
    p_c = masked_softmax(jnp.einsum('bqhgd,bchd->bqhgc', q, kc) * scale, c_mask)
    o_c = jnp.einsum('bqhgc,bchd->bqhgd', p_c.astype(vc.dtype), vc)
    imp = jnp.einsum('bqhgc,cn->bqhn', p_c, cmp_to_sel(nc, ns))
    j = jnp.arange(ns, dtype=jnp.int32)[None, :]
    cur = (t // SEL_BLOCK)[:, None]
    forced = (j == 0) | (j == cur) | (j == cur - 1)
    eligible = j * SEL_BLOCK <= t[:, None]
    imp = jnp.where(forced[None, :, None, :], FORCED_SCORE, imp)
    imp = jnp.where(eligible[None, :, None, :], imp, -1.0)
    n_top = min(N_SEL, ns)
    _, idx = lax.top_k(imp, n_top)
    bi = jnp.arange(bsz)[:, None, None, None]
    hi = jnp.arange(N_KV_HEADS)[None, None, :, None]
    n_keys = n_top * SEL_BLOCK
    k_sel = ks[bi, hi, idx].reshape(bsz, nq, N_KV_HEADS, n_keys, HEAD_DIM)
    v_sel = vs[bi, hi, idx].reshape(bsz, nq, N_KV_HEADS, n_keys, HEAD_DIM)
    sel_pos = (idx[..., None] * SEL_BLOCK + jnp.arange(SEL_BLOCK, dtype=jnp.int32)).reshape(bsz, nq, N_KV_HEADS, n_keys)
    s_mask = (sel_pos <= t[None, :, None, None])[:, :, :, None, :]
    p_s = masked_softmax(jnp.einsum('bqhgd,bqhkd->bqhgk', q, k_sel) * scale, s_mask)
    o_s = jnp.einsum('bqhgk,bqhkd->bqhgd', p_s.astype(v_sel.dtype), v_sel)
    dt = t[:, None] - pw[None, :]
    w_mask = ((dt >= 0) & (dt < WINDOW) & (pw[None, :] >= 0))[None, :, None, None, :]
    p_w = masked_softmax(jnp.einsum('bqhgd,bkhd->bqhgk', q, kw) * scale, w_mask)
    o_w = jnp.einsum('bqhgk,bkhd->bqhgd', p_w.astype(vw.dtype), vw)
    g = jax.nn.sigmoid(gates.astype(jnp.float32)).astype(q.dtype)
    return g[..., 0:1] * o_c + g[..., 1:2] * o_s + g[..., 2:3] * o_w


def conv_tail(u, w_dw, b_dw, ln_g, ln_b, w_pw):
    c = lax.conv_general_dilated(u, w_dw[:, None, :], window_strides=(1,), padding='VALID',
                                 dimension_numbers=('NWC', 'WIO', 'NWC'), feature_group_count=CONV_CH)
    c = layer_norm(c + b_dw, ln_g, ln_b)
    return jax.nn.silu(c) @ w_pw


def merge_out(y_nsa, y_conv, merge, w_out):
    gm = jax.nn.sigmoid(merge.astype(jnp.float32)).astype(y_nsa.dtype)
    return (gm[..., :D_MODEL] * y_nsa + gm[..., D_MODEL:] * y_conv) @ w_out


def cross_attend(h, mk, mv, w_xq, w_xo):
    bsz, L = h.shape[0], h.shape[1]
    q = (h @ w_xq).reshape(bsz, L, X_HEADS, X_HEAD_DIM)
    s = jnp.einsum('bqhd,bmhd->bhqm', q, mk) * (X_HEAD_DIM ** -0.5)
    p = jax.nn.softmax(s.astype(jnp.float32), axis=-1).astype(h.dtype)
    return jnp.einsum('bhqm,bmhd->bqhd', p, mv).reshape(bsz, L, X_W) @ w_xo


def sq_relu_mlp(h, w_up, w_down):
    u = jax.nn.relu(h @ w_up)
    return (u * u) @ w_down


def prompt_layer(x, mem, w):
    (norm_mix, w_in, cmp_pos_k, cmp_pos_v, w_ck1, w_ck2, w_cv1, w_cv2, w_nsa_o, w_dw, b_dw, conv_ln_g, conv_ln_b,
     w_pw, w_out, norm_x, norm_mem, w_xq, w_xk, w_xv, w_xo, norm_ff, w_up, w_down) = w
    bsz, S = x.shape[0], x.shape[1]
    pos = jnp.arange(S, dtype=jnp.int32)
    h = rms_norm(x, norm_mix)
    q, k_cmp, v_cmp, k_slc, v_slc, k_win, v_win, gates, glu, merge = mixer_inputs(h, w_in, pos)
    kcb = compress(k_cmp, cmp_pos_k, w_ck1, w_ck2)
    vcb = compress(v_cmp, cmp_pos_v, w_cv1, w_cv2)
    ksb = to_sel_blocks(k_slc)
    vsb = to_sel_blocks(v_slc)
    pad = ((0, 0), (WINDOW, 0), (0, 0), (0, 0))
    kw_pad = jnp.pad(k_win, pad)
    vw_pad = jnp.pad(v_win, pad)

    def q_block(i):
        s0 = i * QUERY_BLOCK
        qb = lax.dynamic_slice_in_dim(q, s0, QUERY_BLOCK, axis=1)
        gb = lax.dynamic_slice_in_dim(gates, s0, QUERY_BLOCK, axis=1)
        kwb = lax.dynamic_slice_in_dim(kw_pad, s0, WINDOW + QUERY_BLOCK, axis=1)
        vwb = lax.dynamic_slice_in_dim(vw_pad, s0, WINDOW + QUERY_BLOCK, axis=1)
        tb = s0 + jnp.arange(QUERY_BLOCK, dtype=jnp.int32)
        pwb = s0 - WINDOW + jnp.arange(WINDOW + QUERY_BLOCK, dtype=jnp.int32)
        return nsa_attend(qb, tb, kcb, vcb, ksb, vsb, kwb, vwb, pwb, gb)

    o = lax.map(q_block, jnp.arange(S // QUERY_BLOCK, dtype=jnp.int32))
    o = jnp.moveaxis(o, 0, 1).reshape(bsz, S, Q_W)
    u = jnp.pad(glu, ((0, 0), (CONV_WIDTH - 1, 0), (0, 0)))
    y_conv = conv_tail(u, w_dw, b_dw, conv_ln_g, conv_ln_b, w_pw)
    x = x + merge_out(o @ w_nsa_o, y_conv, merge, w_out)
    mn = rms_norm(mem, norm_mem)
    mk = (mn @ w_xk).reshape(bsz, mem.shape[1], X_HEADS, X_HEAD_DIM)
    mv = (mn @ w_xv).reshape(bsz, mem.shape[1], X_HEADS, X_HEAD_DIM)
    x = x + cross_attend(rms_norm(x, norm_x), mk, mv, w_xq, w_xo)
    x = x + sq_relu_mlp(rms_norm(x, norm_ff), w_up, w_down)
    wb = min(WINDOW, S)
    return x, (k_cmp, v_cmp, k_slc, v_slc, k_win[:, S - wb:], v_win[:, S - wb:], glu[:, S - (CONV_WIDTH - 1):], mk, mv)


def sample_layer(x, ck_cmp, cv_cmp, ck_slc, cv_slc, ck_win, cv_win, s_conv, cm_k, cm_v, page_table, w):
    (norm_mix, w_in, cmp_pos_k, cmp_pos_v, w_ck1, w_ck2, w_cv1, w_cv2, w_nsa_o, w_dw, b_dw, conv_ln_g, conv_ln_b,
     w_pw, w_out, norm_x, norm_mem, w_xq, w_xk, w_xv, w_xo, norm_ff, w_up, w_down) = w
    bsz, T = x.shape[0], x.shape[1]
    pos = PAST_LEN + jnp.arange(T, dtype=jnp.int32)
    h = rms_norm(x, norm_mix)
    q, k_cmp, v_cmp, k_slc, v_slc, k_win, v_win, gates, glu, merge = mixer_inputs(h, w_in, pos)
    full = lambda pool, new: jnp.concatenate([gather_pages(pool, page_table), new], axis=1)
    kcb = compress(full(ck_cmp, k_cmp), cmp_pos_k, w_ck1, w_ck2)
    vcb = compress(full(cv_cmp, v_cmp), cmp_pos_v, w_cv1, w_cv2)
    ksb = to_sel_blocks(full(ck_slc, k_slc))
    vsb = to_sel_blocks(full(cv_slc, v_slc))
    wb = ck_win.shape[1]
    kw = jnp.concatenate([ck_win, k_win], axis=1)
    vw = jnp.concatenate([cv_win, v_win], axis=1)
    pw = PAST_LEN - wb + jnp.arange(wb + T, dtype=jnp.int32)
    o = nsa_attend(q, pos, kcb, vcb, ksb, vsb, kw, vw, pw, gates).reshape(bsz, T, Q_W)
    u = jnp.concatenate([s_conv, glu], axis=1)
    y_conv = conv_tail(u, w_dw, b_dw, conv_ln_g, conv_ln_b, w_pw)
    x = x + merge_out(o @ w_nsa_o, y_conv, merge, w_out)
    x = x + cross_attend(rms_norm(x, norm_x), cm_k, cm_v, w_xq, w_xo)
    x = x + sq_relu_mlp(rms_norm(x, norm_ff), w_up, w_down)
    return x, (k_cmp, v_cmp, k_slc, v_slc, kw[:, T:], vw[:, T:], u[:, T:])


def setup_inputs(seed: int = 0) -> dict:
    key = jax.random.key(seed)
    keys = iter(jax.random.split(key, 48))
    nrm = lambda shape, scale: jax.random.normal(next(keys), shape, jnp.float32) * scale
    gain = lambda shape: 1.0 + nrm(shape, 0.01)
    n_pages = PAST_LEN // PAGE_SIZE
    n_used = DEC_BATCH * n_pages
    n_pool = n_used + (n_used + 3) // 4
    wb = min(WINDOW, PAST_LEN)
    pool_shape = (DEPTH, n_pool, PAGE_SIZE, N_KV_HEADS, HEAD_DIM)
    page_table = jax.random.permutation(next(keys), n_pool)[:n_used].reshape(DEC_BATCH, n_pages).astype(jnp.int32)
    L_ = DEPTH
    return {
        'x_prompt': nrm((BATCH, SEQ, D_MODEL), 1.0),
        'x_sample': nrm((DEC_BATCH, DEC_SEQ, D_MODEL), 1.0),
        'mem_prompt': nrm((BATCH, N_MEM, D_MODEL), 1.0),
        'cache_k_cmp': nrm(pool_shape, 1.0),
        'cache_v_cmp': nrm(pool_shape, 1.0),
        'cache_k_slc': nrm(pool_shape, 1.0),
        'cache_v_slc': nrm(pool_shape, 1.0),
        'cache_k_win': nrm((DEPTH, DEC_BATCH, wb, N_KV_HEADS, HEAD_DIM), 1.0),
        'cache_v_win': nrm((DEPTH, DEC_BATCH, wb, N_KV_HEADS, HEAD_DIM), 1.0),
        'state_conv': nrm((DEPTH, DEC_BATCH, CONV_WIDTH - 1, CONV_CH), 0.5),
        'cache_mem_k': nrm((DEPTH, DEC_BATCH, N_MEM, X_HEADS, X_HEAD_DIM), 1.0),
        'cache_mem_v': nrm((DEPTH, DEC_BATCH, N_MEM, X_HEADS, X_HEAD_DIM), 1.0),
        'page_table': page_table,
        'norm_mix': gain((L_, D_MODEL)),
        'w_in': nrm((L_, D_MODEL, IN_W), D_MODEL ** -0.5),
        'cmp_pos_k': nrm((L_, CMP_BLOCK, HEAD_DIM), 0.02),
        'cmp_pos_v': nrm((L_, CMP_BLOCK, HEAD_DIM), 0.02),
        'w_ck1': nrm((L_, CMP_BLOCK * HEAD_DIM, CMP_HIDDEN), (CMP_BLOCK * HEAD_DIM) ** -0.5),
        'w_ck2': nrm((L_, CMP_HIDDEN, HEAD_DIM), CMP_HIDDEN ** -0.5),
        'w_cv1': nrm((L_, CMP_BLOCK * HEAD_DIM, CMP_HIDDEN), (CMP_BLOCK * HEAD_DIM) ** -0.5),
        'w_cv2': nrm((L_, CMP_HIDDEN, HEAD_DIM), CMP_HIDDEN ** -0.5),
        'w_nsa_o': nrm((L_, Q_W, D_MODEL), Q_W ** -0.5),
        'w_dw': nrm((L_, CONV_WIDTH, CONV_CH), CONV_WIDTH ** -0.5),
        'b_dw': nrm((L_, CONV_CH), 0.01),
        'conv_ln_g': gain((L_, CONV_CH)),
        'conv_ln_b': nrm((L_, CONV_CH), 0.01),
        'w_pw': nrm((L_, CONV_CH, D_MODEL), CONV_CH ** -0.5),
        'w_out': nrm((L_, D_MODEL, D_MODEL), D_MODEL ** -0.5),
        'norm_x': gain((L_, D_MODEL)),
        'norm_mem': gain((L_, D_MODEL)),
        'w_xq': nrm((L_, D_MODEL, X_W), D_MODEL ** -0.5),
        'w_xk': nrm((L_, D_MODEL, X_W), D_MODEL ** -0.5),
        'w_xv': nrm((L_, D_MODEL, X_W), D_MODEL ** -0.5),
        'w_xo': nrm((L_, X_W, D_MODEL), X_W ** -0.5),
        'norm_ff': gain((L_, D_MODEL)),
        'w_up': nrm((L_, D_MODEL, D_FF), D_MODEL ** -0.5),
        'w_down': nrm((L_, D_FF, D_MODEL), D_FF ** -0.5),
        'norm_final': gain((D_MODEL,)),
    }


def reference(x_prompt, x_sample, mem_prompt, cache_k_cmp, cache_v_cmp, cache_k_slc, cache_v_slc, cache_k_win,
              cache_v_win, state_conv, cache_mem_k, cache_mem_v, page_table, norm_mix, w_in, cmp_pos_k, cmp_pos_v,
              w_ck1, w_ck2, w_cv1, w_cv2, w_nsa_o, w_dw, b_dw, conv_ln_g, conv_ln_b, w_pw, w_out, norm_x, norm_mem,
              w_xq, w_xk, w_xv, w_xo, norm_ff, w_up, w_down, norm_final):
    xp, xs = x_prompt, x_sample
    p_states, s_states = [], []
    for l in range(DEPTH):
        w = (norm_mix[l], w_in[l], cmp_pos_k[l], cmp_pos_v[l], w_ck1[l], w_ck2[l], w_cv1[l], w_cv2[l], w_nsa_o[l],
             w_dw[l], b_dw[l], conv_ln_g[l], conv_ln_b[l], w_pw[l], w_out[l], norm_x[l], norm_mem[l], w_xq[l],
             w_xk[l], w_xv[l], w_xo[l], norm_ff[l], w_up[l], w_down[l])
        xp, st = prompt_layer(xp, mem_prompt, w)
        p_states.append(st)
        xs, st = sample_layer(xs, cache_k_cmp[l], cache_v_cmp[l], cache_k_slc[l], cache_v_slc[l], cache_k_win[l],
                              cache_v_win[l], state_conv[l], cache_mem_k[l], cache_mem_v[l], page_table, w)
        s_states.append(st)
    y_prompt = rms_norm(xp, norm_final)
    y_sample = rms_norm(xs, norm_final)
    (pk_cmp, pv_cmp, pk_slc, pv_slc, pk_win, pv_win, p_conv, p_mem_k, p_mem_v) = [jnp.stack(a) for a in zip(*p_states)]
    (sk_cmp, sv_cmp, sk_slc, sv_slc, sk_win, sv_win, s_conv) = [jnp.stack(a) for a in zip(*s_states)]
    return (y_prompt, y_sample, pk_cmp, pv_cmp, pk_slc, pv_slc, pk_win, pv_win, p_conv, p_mem_k, p_mem_v,
            sk_cmp, sv_cmp, sk_slc, sv_slc, sk_win, sv_win, s_conv)
```

```python
from contextlib import ExitStack
import numpy as np
import concourse.bass as bass
import concourse.mybir as mybir
from concourse.bass_utils import run_bass_kernel_spmd

F32 = mybir.dt.float32
BF16 = mybir.dt.bfloat16
I32 = mybir.dt.int32
AF = mybir.ActivationFunctionType
ALU = mybir.AluOpType
AX = mybir.AxisListType

ENGS = ['pe', 'dve', 'act', 'pool', 'sp']
NDS = 48
SAME_ENG_SYNC = True

D = 1024
S_FULL = 4096
HALF = 2048
NT_ALL = 32
NT_OWN = 16
IN_W = 4376
EPS = 1e-6
NEG = -32768.0
SB = 4
ST = 4
PAST = 8192
NPAGE = 64
POOL_ROWS = 2560 * 128


import types


def _freeze(fn):
    if fn is None or fn.__closure__ is None:
        return fn
    cells = []
    for c in fn.__closure__:
        try:
            cells.append(types.CellType(c.cell_contents))
        except ValueError:
            cells.append(c)
    return types.FunctionType(fn.__code__, fn.__globals__, fn.__name__, fn.__defaults__, tuple(cells))


class Tok:
    __slots__ = ('w', 'r')

    def __init__(self):
        self.w = None
        self.r = {}


class Sched:
    def __init__(self, nc):
        self.nc = nc
        self.sem = {e: nc.alloc_semaphore('s_' + e) for e in ENGS}
        self.cnt = {e: 0 for e in ENGS}
        self.known = {e: {} for e in ENGS}
        self.prog = {e: [] for e in ENGS}
        self.snap = {}
        self.dsem = [nc.alloc_semaphore('d%d' % i) for i in range(NDS)]
        self.dcnt = [0] * NDS
        self.dnext = 0
        self.dnext_sw = 0

    def _semh(self, k):
        return self.sem[k] if isinstance(k, str) else self.dsem[k[1]]

    def _gather(self, engine, reads, writes, extra=()):
        need = {}
        kn = self.known[engine]

        def add(ev):
            if ev is None:
                return
            k, v = ev
            if k == engine and (engine == 'pe' or not SAME_ENG_SYNC):
                return
            if kn.get(k, 0) >= v:
                return
            if need.get(k, 0) < v:
                need[k] = v
        for t in reads:
            add(t.w)
        for t in writes:
            add(t.w)
            for k, v in t.r.items():
                add((k, v))
        for ev in extra:
            add(ev)
        return need

    def _apply_waits(self, engine, need):
        kn = self.known[engine]
        waits = []
        for k, v in need.items():
            if kn.get(k, 0) >= v:
                continue
            waits.append((self._semh(k), v))
            kn[k] = v
            sn = self.snap.get((k, v))
            if sn:
                for k2, v2 in sn.items():
                    if kn.get(k2, 0) < v2:
                        kn[k2] = v2
        return waits

    def _record(self, ev, reads, writes):
        k, v = ev
        for t in reads:
            if t.r.get(k, 0) < v:
                t.r[k] = v
        for t in writes:
            t.w = ev
            t.r = {}

    def op(self, engine, fn, reads=(), writes=(), inc=True):
        fn = _freeze(fn)
        reads = [getattr(t, 'tok', t) for t in reads]
        writes = [getattr(t, 'tok', t) for t in writes]
        need = self._gather(engine, reads, writes)
        waits = self._apply_waits(engine, need)
        ev = (engine, self.cnt[engine] + 1)
        sem = self.sem[engine]
        if inc:
            self.cnt[engine] += 1
            self.snap[ev] = {k: v for k, v in self.known[engine].items() if isinstance(k, str)}

        def emit(e, waits=waits, fn=fn, inc=inc, sem=sem):
            for s, v in waits:
                e.wait_ge(s, v)
            ins = fn(e)
            if inc:
                ins.then_inc(sem, 1)
        self.prog[engine].append(emit)
        self._record(ev, reads, writes)
        return ev

    def dma(self, engine, out, in_, reads=(), writes=(), custom=None):
        custom = _freeze(custom)
        reads = [getattr(t, 'tok', t) for t in reads]
        writes = [getattr(t, 'tok', t) for t in writes]
        half = NDS // 2
        if engine == 'pool':
            i = self.dnext_sw
            self.dnext_sw = (self.dnext_sw + 1) % half
        else:
            i = half + self.dnext
            self.dnext = (self.dnext + 1) % half
        extra = []
        if self.dcnt[i] > 0:
            extra.append((('d', i), self.dcnt[i]))
        need = self._gather(engine, reads, writes, extra)
        waits = self._apply_waits(engine, need)
        self.dcnt[i] += 16
        ev = (('d', i), self.dcnt[i])
        sem = self.dsem[i]

        def emit(e, waits=waits, sem=sem):
            for s, v in waits:
                e.wait_ge(s, v)
            if custom is not None:
                ins = custom(e)
            else:
                ins = e.dma_start(out=out, in_=in_)
            ins.then_inc(sem, 16)
        self.prog[engine].append(emit)
        self._record(ev, reads, writes)
        return ev

    def barrier(self, only=None):
        if only is not None:
            evs = [(e, self.cnt[e]) for e in only if self.cnt[e] > 0]
        else:
            evs = [(e, self.cnt[e]) for e in ENGS if self.cnt[e] > 0]
            evs += [(('d', i), self.dcnt[i]) for i in range(NDS) if self.dcnt[i] > 0]
        for engine in ENGS:
            need = {}
            for k, v in evs:
                if k == engine:
                    continue
                if self.known[engine].get(k, 0) < v:
                    need[k] = v
            waits = self._apply_waits(engine, need)
            if waits:
                def emit(e, waits=waits):
                    for s, v in waits:
                        e.wait_ge(s, v)
                self.prog[engine].append(emit)

    def finish(self):
        self.barrier()
        nc = self.nc
        prog = self.prog
        with nc.allow_non_contiguous_dma(reason="small transposed constant loads"), nc.Block() as block:
            @block.sync
            def _(e):
                for f in prog['sp']:
                    f(e)

            @block.tensor
            def _(e):
                for f in prog['pe']:
                    f(e)

            @block.vector
            def _(e):
                for f in prog['dve']:
                    f(e)

            @block.scalar
            def _(e):
                for f in prog['act']:
                    f(e)

            @block.gpsimd
            def _(e):
                for f in prog['pool']:
                    f(e)


class Buf:
    def __init__(self, t):
        self.t = t
        self.tok = Tok()

    def __getitem__(self, idx):
        return self.t[idx]


class Ring:
    def __init__(self, bufs):
        self.bufs = bufs
        self.i = 0

    def next(self):
        b = self.bufs[self.i]
        self.i = (self.i + 1) % len(self.bufs)
        return b


class K:
    pass


def build_program():
    nc = bass.Bass("TRN2", target_bir_lowering=False)
    S = Sched(nc)
    k = K()
    k.nc = nc
    k.S = S
    cnt = [0]

    def din(name, shape, dt=F32):
        return nc.dram_tensor(name, list(shape), dt, kind="ExternalInput")

    def dout(name, shape, dt=F32):
        return nc.dram_tensor(name, list(shape), dt, kind="ExternalOutput")

    xp = din("xp", [S_FULL, D])
    xs = din("xs", [SB * ST, D])
    memp = din("memp", [256, D])
    pools = [din(n, [2560 * 8, 2048]) for n in ("pk_cmp", "pv_cmp", "pk_slc", "pv_slc")]
    ckw = din("ckw", [SB, 512, 128])
    cvw = din("cvw", [SB, 512, 128])
    sconv_in = din("sconv_in", [SB, 30, 512])
    cmk = din("cmk", [SB, 256, 512])
    cmv = din("cmv", [SB, 256, 512])
    ptab = din("ptab", [SB, NPAGE], I32)
    w_in = din("w_in", [D, IN_W])
    cpos = [din("cpos_k", [32, 64]), din("cpos_v", [32, 64])]
    w_c1 = [din("w_ck1", [2048, 128]), din("w_cv1", [2048, 128])]
    w_c2 = [din("w_ck2", [128, 64]), din("w_cv2", [128, 64])]
    w_nsa_o = din("w_nsa_o", [512, D])
    w_dw = din("w_dw", [31, 512])
    cvec = din("cvec", [128, 12])
    w_pw = din("w_pw", [512, D])
    w_out = din("w_out", [D, D])
    w_xq = din("w_xq", [D, 512])
    w_xk = din("w_xk", [D, 512])
    w_xv = din("w_xv", [D, 512])
    w_xo = din("w_xo", [512, D])
    w_up = din("w_up", [D, 4096])
    w_down = din("w_down", [4096, D])
    gvec = din("gvec", [128, 5, D])
    c_ident = din("c_ident", [128, 128])
    c_E = din("c_E", [64, 4096])
    c_tri = din("c_tri", [128, 2, 128])
    c_cs = din("c_cs", [128, NT_ALL, 16])
    c_valid = din("c_valid", [128, NT_ALL])
    c_A = din("c_A", [128, NT_OWN, 64])
    c_B = din("c_B", [128, NT_OWN, 64])
    c_cmask = din("c_cmask", [128, 2, HALF])
    c_M = din("c_M", [128, 2, 64])
    c_pidx = din("c_pidx", [128, 1])
    c_c8 = din("c_c8", [NPAGE, 8])
    s_cs = din("s_cs", [ST, 16])
    s_A = din("s_A", [ST, 192])
    s_B = din("s_B", [ST, 192])
    s_cmask = din("s_cmask", [128, 4])
    s_M = din("s_M", [128, 4, 192])
    s_wmask = din("s_wmask", [128, 5, ST])
    s_lmask = din("s_lmask", [128, ST])
    s_valid = din("s_valid", [128, 66])
    s_wvalid = din("s_wvalid", [128, 5])

    yp = dout("yp", [HALF, D])
    okv = dout("okv", [6, HALF, 128])
    pconv = dout("pconv", [30, 512])
    pmk = dout("pmk", [256, 512])
    pmv = dout("pmv", [256, 512])
    ys = dout("ys", [SB * ST, D])
    skv = dout("skv", [6, SB * ST, 128])
    swk = dout("swk", [SB, 512, 128])
    swv = dout("swv", [SB, 512, 128])
    sconv = dout("sconv", [SB, 30, 512])

    x1s = nc.dram_tensor("x1s", [HALF + SB * ST, D], F32, kind="Internal")
    oscr = nc.dram_tensor("oscr", [HALF + SB * ST, 512], F32, kind="Internal")
    gscr = [[nc.dram_tensor("gscr_%d_%d" % (b, X), [PAST, 128], F32, kind="Internal") for X in range(4)] for b in range(SB)]
    gtok = [[Tok() for X in range(4)] for b in range(SB)]

    glob = ExitStack()

    def sb(stack, name, shape, dt=F32):
        cnt[0] += 1
        return Buf(stack.enter_context(nc.sbuf_tensor("%s_%d" % (name, cnt[0]), list(shape), dt)))

    def ps(stack, name, shape, dt=F32):
        cnt[0] += 1
        return Buf(stack.enter_context(nc.psum_tensor("%s_%d" % (name, cnt[0]), list(shape), dt)))

    def V(fn, r, w):
        S.op('dve', fn, reads=r, writes=w)

    def A(fn, r, w):
        S.op('act', fn, reads=r, writes=w)

    def G(fn, r, w):
        S.op('pool', fn, reads=r, writes=w)

    def PE(fn, r, w, inc=True):
        S.op('pe', fn, reads=r, writes=w, inc=inc)

    def LD(out, in_, w, r=(), q='sp'):
        S.dma(q, out, in_, reads=r, writes=w)

    def LDC(out, in_, w):
        S.dma('pool', out, in_, writes=w)

    def ST_(out, in_, r, w=(), q='sp'):
        S.dma(q, out, in_, reads=r, writes=w)

    pT = Ring([ps(glob, "pT", [128, 8, 128], BF16) for _ in range(2)])
    pmm = Ring([ps(glob, "pmm", [128, 512], F32) for _ in range(3)])
    pacc = Ring([ps(glob, "pacc", [128, 512], F32) for _ in range(2)])
    paccT = Ring([ps(glob, "paccT", [128, 512], F32) for _ in range(1)])

    ident = sb(glob, "ident", [128, 128], BF16)
    identf = sb(glob, "identf", [128, 128], F32)
    LDC(ident[:, :], c_ident[:, :], [ident])
    LD(identf[:, :], c_ident[:, :], [identf])

    def mk_masks(stack):
        tri = sb(stack, "tri", [128, 2, 128], BF16)
        LDC(tri[:, :, :], c_tri[:, :, :], [tri])
        tri4 = sb(stack, "tri4", [128, 2, 4, 128], BF16)
        zeros_b = sb(stack, "zeros_b", [128, 512], BF16)
        S.op('pool', lambda e: e.memset(zeros_b[:, :], 0.0), writes=[zeros_b])
        for m_ in range(2):
            S.op('dve', lambda e, m_=m_: e.tensor_copy(out=tri4[:, m_, :, :], in_=tri[:, m_, :].unsqueeze(1).to_broadcast([128, 4, 128])), reads=[tri], writes=[tri4])
        return tri4, zeros_b
    gvd = {}

    def load_g(stack, idxs):
        for gi in idxs:
            t = sb(stack, "gv%d" % gi, [128, D], F32)
            LD(t[:, :], gvec[:, gi, :], [t])
            gvd[gi] = t

    wk = ExitStack()
    xt_r = Ring([sb(wk, "xt", [128, D], F32) for _ in range(2)])
    h_r = Ring([sb(wk, "h", [128, D], BF16) for _ in range(2)])
    hT_r = Ring([sb(wk, "hT", [128, 8, 128], BF16) for _ in range(2)])
    junk = sb(wk, "junk", [128, D], F32)
    st_r = Ring([sb(wk, "st", [128, 8], F32) for _ in range(4)])

    def load_w(stack, name, w_ap, K_, N_, c0=0, c1=None, defer=False):
        c1 = N_ if c1 is None else c1
        kc = K_ // 128
        t = sb(stack, name, [128, kc, c1 - c0], BF16)
        src = w_ap[:, c0:c1].rearrange("(kc p) n -> p kc n", p=128)
        t.ktoks = [Tok() for _ in range(kc)]

        def issue():
            for i in range(kc):
                LDC(t[:, i, :], src[:, i, :], [t.ktoks[i]])
        if defer:
            t.issue = issue
        else:
            issue()
        return t

    def rmsnorm(xt, nt, gi, out):
        st = st_r.next()
        gbuf = gvd[gi]
        A(lambda e: e.activation(out=junk[0:nt, :], in_=xt[0:nt, :], func=AF.Square, accum_out=st[0:nt, 0:1]), [xt], [junk, st])
        V(lambda e: e.tensor_scalar(out=st[0:nt, 1:2], in0=st[0:nt, 0:1], scalar1=1.0 / D, scalar2=EPS, op0=ALU.mult, op1=ALU.add), [st], [st])
        A(lambda e: e.activation(out=st[0:nt, 2:3], in_=st[0:nt, 1:2], func=AF.Sqrt), [st], [st])
        V(lambda e: e.reciprocal(out=st[0:nt, 3:4], in_=st[0:nt, 2:3]), [st], [st])
        V(lambda e: e.scalar_tensor_tensor(out=out[0:nt, :], in0=xt[0:nt, :], scalar=st[0:nt, 3:4], in1=gbuf[0:nt, :], op0=ALU.mult, op1=ALU.mult), [xt, st, gbuf], [out])

    def transpose_bf(src, nt, nchunk, dst, width=128, col0=0):
        for c0 in range(0, nchunk, 8):
            n = min(8, nchunk - c0)
            p = pT.next()
            for c in range(n):
                PE(lambda e, c=c: e.transpose(out=p[0:width, c, 0:nt], in_=src[0:nt, (c0 + c) * width:(c0 + c + 1) * width], identity=ident[0:nt, 0:nt]), [src, ident], [p])
            A(lambda e, n=n, c0=c0: e.copy(out=dst[0:width, c0:c0 + n, col0:col0 + nt], in_=p[0:width, 0:n, 0:nt]), [p], [dst])

    def linear(hT, nt, kc, W, c0, c1, consume):
        g0 = c0
        while g0 < c1:
            gn = min(512, c1 - g0)
            p = pmm.next()
            for i in range(kc):
                PE(lambda e, i=i, g0=g0, gn=gn, p=p: e.matmul(p[0:nt, 0:gn], lhsT=hT[:, i, 0:nt], rhs=W[:, i, g0:g0 + gn], start=(i == 0), stop=(i == kc - 1)), [hT, W.ktoks[i] if hasattr(W, 'ktoks') else W], [p], inc=(i == kc - 1))
            consume(p, g0, gn)
            g0 += gn

    def rope(z, nt, nh_view_fn, cs, tmp):
        x1 = nh_view_fn(0, 8)
        x2 = nh_view_fn(8, 16)
        shp = list(x1.shape)
        n = 1
        for s_ in shp[1:-1]:
            n *= s_

        def bc(ap):
            a = ap
            for _ in range(len(shp) - 2):
                a = a.unsqueeze(1)
            return a.to_broadcast(shp)
        cosb = bc(cs[0](0, 8))
        sinb = bc(cs[0](8, 16))

        def tv(i):
            v = tmp[0:nt, i, 0:n * 8]
            if len(shp) == 3:
                return v.rearrange("p (a d) -> p a d", d=8)
            return v.rearrange("p (a b d) -> p a b d", b=shp[2], d=8)
        rd = [z, cs[1], tmp]
        V(lambda e: e.tensor_tensor(out=tv(0), in0=x1, in1=cosb, op=ALU.mult), rd, [tmp])
        V(lambda e: e.tensor_tensor(out=tv(1), in0=x2, in1=sinb, op=ALU.mult), rd, [tmp])
        V(lambda e: e.tensor_tensor(out=tv(2), in0=x2, in1=cosb, op=ALU.mult), rd, [tmp])
        V(lambda e: e.tensor_tensor(out=tv(3), in0=x1, in1=sinb, op=ALU.mult), rd, [tmp])
        V(lambda e: e.tensor_tensor(out=x1, in0=tv(0), in1=tv(1), op=ALU.subtract), [tmp], [z])
        V(lambda e: e.tensor_tensor(out=x2, in0=tv(2), in1=tv(3), op=ALU.add), [tmp], [z])

    def attn_branch(ctx, nq, acc, hh, Qrows, kts, lhs_fn, v_fn, vw, mask_fn, extra_r, q_fn=None, heads=(0, 1, 2, 3), gstride=128, col0=0):
        Qa_, Pb_r_, accsb_r_ = ctx
        n_k = len(kts)
        nh = len(heads)
        g0 = heads[0]
        accT = paccT.next()
        staged = {}

        def stage1(i):
            kt = kts[i]
            p = pmm.next()
            lhs, nk = lhs_fn(kt)
            rhs = q_fn(kt) if q_fn is not None else Qa_[0:Qrows, hh * 4:hh * 4 + 4, 0:nq]
            PE(lambda e, p=p, lhs=lhs, nk=nk, rhs=rhs: e.matmul(p[0:nk, 0:4 * nq].rearrange("k (g q) -> k g q", g=4), lhsT=lhs, rhs=rhs, start=True, stop=True), [Qa_] + extra_r, [p])
            Pb = Pb_r_.next()
            A(lambda e, p=p, Pb=Pb, nk=nk: e.activation(out=Pb[0:nk, :, 0:nq], in_=p[0:nk, 0:4 * nq].rearrange("k (g q) -> k g q", g=4), func=AF.Exp, scale=0.125), [p], [Pb])
            m = mask_fn(kt)
            if m is not None:
                mk_ap, mk_buf = m
                V(lambda e, Pb=Pb, nk=nk, mk_ap=mk_ap: e.tensor_tensor(out=Pb[0:nk, :, 0:nq], in0=Pb[0:nk, :, 0:nq], in1=mk_ap, op=ALU.mult), [Pb, mk_buf], [Pb])
            staged[i] = (Pb, nk)

        def stage2(i):
            kt = kts[i]
            Pb, nk = staged.pop(i)
            vap = v_fn(kt)
            PE(lambda e, Pb=Pb, nk=nk, vap=vap, i=i: e.matmul(accT[0:vw, 0:nh * nq].rearrange("v (g q) -> v g q", g=nh), lhsT=vap, rhs=Pb[0:nk, g0:g0 + nh, 0:nq], start=(i == 0), stop=(i == n_k - 1)), [Pb] + extra_r, [accT])

        LOOK = 2
        for i in range(min(LOOK, n_k)):
            stage1(i)
        for i in range(n_k):
            if i + LOOK < n_k:
                stage1(i + LOOK)
            stage2(i)
        asb = accsb_r_.next()
        A(lambda e: e.copy(out=asb[0:vw, 0:nh * nq], in_=accT[0:vw, 0:nh * nq]), [accT], [asb])
        for gi in range(nh):
            PE(lambda e, gi=gi: e.transpose(out=acc[0:nq, gi * gstride + col0:gi * gstride + col0 + vw], in_=asb[0:vw, gi * nq:(gi + 1) * nq], identity=identf[0:vw, 0:vw]), [asb, identf], [acc], inc=(gi == nh - 1))

    sper = ExitStack()
    cs_s = sb(sper, "cs_s", [ST, 16], F32)
    LD(cs_s[:, :], s_cs[:, :], [cs_s])
    KcT_s = [sb(sper, "KcTs%d" % b, [64, 2, 512], BF16) for b in range(SB)]
    Vca_s = [sb(sper, "Vcas%d" % b, [128, 4, 2, 258], BF16) for b in range(SB)]
    Ms = sb(sper, "Ms", [128, 4, 192], BF16)
    LDC(Ms[:, :, :], s_M[:, :, :], [Ms])
    pidx_t = sb(sper, "pidx_t", [NPAGE, SB], I32)
    LD(pidx_t[:, :], ptab.rearrange("b j -> j b"), [pidx_t])
    pidx_f = sb(sper, "pidx_f", [NPAGE, SB], F32)
    c8 = sb(sper, "c8", [NPAGE, 8], F32)
    pidx8 = sb(sper, "pidx8", [NPAGE, SB, 8], I32)
    LD(c8[:, :], c_c8[:, :], [c8])
    V(lambda e: e.tensor_copy(out=pidx_f[:, :], in_=pidx_t[:, :]), [pidx_t], [pidx_f])
    V(lambda e: e.tensor_scalar(out=pidx_f[:, :], in0=pidx_f[:, :], scalar1=8.0, scalar2=None, op0=ALU.mult), [pidx_f], [pidx_f])
    V(lambda e: e.tensor_tensor(out=pidx8[:, :, :], in0=pidx_f[:, :].unsqueeze(2).to_broadcast([NPAGE, SB, 8]), in1=c8[:, :].unsqueeze(1).to_broadcast([NPAGE, SB, 8]), op=ALU.add), [pidx_f, c8], [pidx8])

    def gather_page(b, j, X, stg):
        S.dma('sp', stg[:, :], gscr[b][X][j * 128:(j + 1) * 128, :], reads=[gtok[b][X]], writes=[stg])

    def pages_to_T(dst, dst_tok, stb4, n, pos0):
        p = pT.next()
        for jj in range(n):
            for hh in range(2):
                PE(lambda e, jj=jj, hh=hh, p=p: e.transpose(out=p[0:64, jj * 2 + hh, :], in_=stb4[:, jj, hh * 64:(hh + 1) * 64], identity=ident[:, :]), [stb4, ident], [p])
        for hh in range(2):
            A(lambda e, hh=hh, p=p: e.copy(out=dst(hh, pos0, pos0 + n * 128).rearrange("d (j p) -> d j p", p=128), in_=p[0:64, 0:2 * n, :].rearrange("d (j h) p -> d j h p", h=2)[:, :, hh, :]), [p], [dst_tok(hh) if callable(dst_tok) else dst_tok])

    def pages_to_T_full(dst, stb4, n, pos0):
        p = pT.next()
        for jj in range(n):
            PE(lambda e, jj=jj, p=p: e.transpose(out=p[:, jj, :], in_=stb4[:, jj, :], identity=ident[:, :]), [stb4, ident], [p])
        A(lambda e, p=p: e.copy(out=dst[:, pos0:pos0 + n * 128].rearrange("d (j p) -> d j p", p=128), in_=p[:, 0:n, :]), [p], [dst])

    def stream_pages(rings, groups, dstT=None, dstV=None, full=False):
        stg_r_, stb_r_ = rings
        for gi, (fill, n, j0) in enumerate(groups):
            stg4 = stg_r_.next()
            fill(stg4)
            if dstT is not None:
                stb4 = stb_r_.next()
                if gi % 2 == 0:
                    V(lambda e, stb4=stb4, stg4=stg4, n=n: e.tensor_copy(out=stb4[:, 0:n, :], in_=stg4[:, 0:n, :]), [stg4], [stb4])
                else:
                    A(lambda e, stb4=stb4, stg4=stg4, n=n: e.copy(out=stb4[:, 0:n, :], in_=stg4[:, 0:n, :]), [stg4], [stb4])
                if full:
                    pages_to_T_full(dstT[1], stb4, n, j0 * 128)
                else:
                    pages_to_T(dstT[0], dstT[1], stb4, n, j0 * 128)
            else:
                ap, tokb = dstV(j0, n)
                src = stg4[:, 0:n, :].rearrange("p j (h d) -> p j h d", h=2)
                if gi % 2 == 0:
                    V(lambda e, ap=ap, src=src: e.tensor_copy(out=ap, in_=src), [stg4], [tokb])
                else:
                    G(lambda e, ap=ap, src=src: e.tensor_copy(out=ap, in_=src), [stg4], [tokb])

    def pool_groups(b, X, r0):
        gs = []
        for j0 in range(0, NPAGE, 4):
            gs.append((lambda stg4, j0=j0: S.dma('sp', stg4[:, :, :], gscr[b][X][j0 * 128:(j0 + 4) * 128, :].rearrange("(j p) d -> p j d", p=128), reads=[gtok[b][X]], writes=[stg4]), 4, j0))

        def newtok(stg4):
            G(lambda e: e.memset(stg4[:, 0, :], 0.0), [], [stg4])
            LD(stg4[0:ST, 0, :], skv[X, r0:r0 + ST, :], [stg4])
        gs.append((newtok, 1, NPAGE))
        return gs

    def win_groups(b, cache, X, r0):
        def newtok(stg4):
            G(lambda e: e.memset(stg4[:, 0, :], 0.0), [], [stg4])
            LD(stg4[0:ST, 0, :], skv[X, r0:r0 + ST, :], [stg4])
        return [(lambda stg4: LD(stg4[:, :, :], cache[b, :, :].rearrange("(j p) d -> p j d", p=128), [stg4]), 4, 0), (newtok, 1, 4)]

    def emit_gathers(jobs, gst_ring, store_q):
        pend = []

        def store(b0, X0, c0, g0):
            S.dma(store_q, gscr[b0][X0].rearrange("(j r) d -> j (r d)", r=128)[:, c0 * 2048:(c0 + 1) * 2048], g0[:, :], reads=[g0], writes=[gtok[b0][X0]])
        for (b, X, c) in jobs:
            g = gst_ring.next()
            S.dma('pool', None, None, reads=[pidx8], writes=[g],
                  custom=lambda e, g=g, b=b, X=X, c=c: e.indirect_dma_start(
                      out=g[:, :], out_offset=None, in_=pools[X][:, :],
                      in_offset=bass.IndirectOffsetOnAxis(ap=pidx8[:, b, c:c + 1], axis=0)))
            pend.append((b, X, c, g))
            if len(pend) == 2:
                store(*pend.pop(0))
        for job in pend:
            store(*job)

    stores = ExitStack()
    KS = [sb(stores, "KS%d" % i, [128, S_FULL], BF16) for i in range(2)]
    KW = sb(stores, "KW", [64, 2, S_FULL], BF16)
    VS = sb(stores, "VS", [128, NT_ALL, 2, 66], BF16)
    VW = sb(stores, "VW", [128, NT_ALL, 2, 66], BF16)
    KcT = sb(stores, "KcT", [64, 2, 256], BF16)
    Vca = sb(stores, "Vca", [128, 2, 2, 130], BF16)
    cs_p = sb(stores, "cs_p", [128, NT_ALL, 16], F32)
    valid = sb(stores, "valid", [128, NT_ALL], F32)
    LD(cs_p[:, :, :], c_cs[:, :, :], [cs_p])
    LD(valid[:, :], c_valid[:, :], [valid])
    for i in range(2):
        LDC(KS[i][64:128, :], c_E[:, :], [KS[i]])
        V(lambda e, i=i: e.tensor_copy(out=VS[:, :, i, 64], in_=valid[:, :]), [valid], [VS])
        V(lambda e, i=i: e.tensor_copy(out=VW[:, :, i, 64], in_=valid[:, :]), [valid], [VW])
    Mc = sb(stores, "Mc", [128, 2, 64], BF16)
    LDC(Mc[:, :, :], c_M[:, :, :], [Mc])

    cw = ExitStack()
    w1 = []
    w2 = []
    cb = sb(cw, "cbias", [128, 2, 2], F32)
    for kv in range(2):
        t = sb(cw, "w1_%d" % kv, [128, 32, 128], BF16)
        LDC(t[0:64, :, :], w_c1[kv].rearrange("(j d) n -> d j n", d=64), [t])
        LDC(t[64:128, :, :], w_c1[kv].rearrange("(j d) n -> d j n", d=64), [t])
        w1.append(t)
        t2 = sb(cw, "w2_%d" % kv, [128, 64], BF16)
        LDC(t2[:, :], w_c2[kv][:, :], [t2])
        w2.append(t2)
    peT = sb(cw, "peT", [64, 2, 32], BF16)
    for kv in range(2):
        LDC(peT[:, kv, :], cpos[kv].rearrange("j d -> d j"), [peT])
    for kv in range(2):
        p = pmm.next()
        for j in range(32):
            PE(lambda e, j=j, kv=kv, p=p: e.matmul(p[:, 0:1], lhsT=w1[kv][0:64, j, :], rhs=peT[:, kv, j:j + 1], start=(j == 0), stop=(j == 31)), [w1[kv], peT], [p], inc=(j == 31))
        V(lambda e, kv=kv, p=p: e.tensor_copy(out=cb[:, kv, 0:1], in_=p[:, 0:1]), [p], [cb])
    gtmp = sb(cw, "gtmp", [128, 3, 512], F32)
    hid = sb(cw, "hid", [128, 512], BF16)

    cstk = ExitStack()
    KC = [sb(cstk, "KCT%d" % i, [128, S_FULL], BF16) for i in range(2)]

    with ExitStack() as ph:
        load_g(ph, [0])
        wkv = load_w(ph, "wkv", w_in, D, IN_W, 512, 1280)
        gst_r = Ring([sb(ph, "gst", [NPAGE, 2048], F32) for _ in range(3)])
        emit_gathers([(b, X, c) for X in (0, 1) for b in range(SB) for c in range(8)], gst_r, 'pool')
        zkv_r = Ring([sb(ph, "zkv", [128, 768], F32) for _ in range(2)])
        zb_r = Ring([sb(ph, "zb", [128, 768], BF16) for _ in range(2)])
        rtmp = sb(ph, "rtmp", [128, 4, 64], F32)

        def a1_pre(src_ap, nt):
            xt = xt_r.next()
            LD(xt[0:nt, :], src_ap, [xt])
            h = h_r.next()
            rmsnorm(xt, nt, 0, h)
            hT = hT_r.next()
            transpose_bf(h, nt, 8, hT)
            return hT

        def a1_post(hT, nt, cs_fn, cs_buf, zkv):
            def cons(p, g0, gn):
                A(lambda e: e.copy(out=zkv[0:nt, g0:g0 + gn], in_=p[0:nt, 0:gn]), [p], [zkv])
            linear(hT, nt, 8, wkv, 0, 768, cons)
            zv = zkv[0:nt, :].rearrange("p (s kv h d) -> p s kv h d", s=3, kv=2, h=2)
            rope(zkv, nt, lambda lo, hi: zv[:, :, 0, :, lo:hi], (cs_fn, cs_buf), rtmp)

        def a1_tile(src_ap, nt, cs_fn, cs_buf, zkv):
            a1_post(a1_pre(src_ap, nt), nt, cs_fn, cs_buf, zkv)

        hT_next = a1_pre(xp[0:128, :], 128)
        for ti in range(NT_ALL):
            zkv = zkv_r.next()
            hT_cur = hT_next
            if ti + 1 < NT_ALL:
                hT_next = a1_pre(xp[(ti + 1) * 128:(ti + 2) * 128, :], 128)
            a1_post(hT_cur, 128, lambda lo, hi, ti=ti: cs_p[:, ti, lo:hi], cs_p, zkv)
            if ti >= NT_OWN:
                t0 = (ti - NT_OWN) * 128
                for s6 in range(6):
                    ST_(okv[s6, t0:t0 + 128, :], zkv[:, s6 * 128:(s6 + 1) * 128], [zkv], q='sp')
            zb = zb_r.next()
            V(lambda e, zb=zb, zkv=zkv: e.tensor_copy(out=zb[:, :], in_=zkv[:, :]), [zkv], [zb])
            p = pT.next()
            for j, s_ in enumerate((0, 1)):
                PE(lambda e, j=j, s_=s_, p=p, zb=zb: e.transpose(out=p[:, j, :], in_=zb[:, s_ * 128:(s_ + 1) * 128], identity=ident[:, :]), [zb, ident], [p])
            for j, s_ in enumerate((2, 4)):
                for hh in range(2):
                    PE(lambda e, j=j, s_=s_, hh=hh, p=p, zb=zb: e.transpose(out=p[0:64, 2 + j * 2 + hh, :], in_=zb[:, s_ * 128 + hh * 64:s_ * 128 + hh * 64 + 64], identity=ident[:, :]), [zb, ident], [p])
            sl = slice(ti * 128, (ti + 1) * 128)
            A(lambda e, p=p, sl=sl: e.copy(out=KC[0][:, sl], in_=p[:, 0, :]), [p], [KC[0]])
            A(lambda e, p=p, sl=sl: e.copy(out=KC[1][:, sl], in_=p[:, 1, :]), [p], [KC[1]])
            for hh in range(2):
                A(lambda e, p=p, sl=sl, hh=hh: e.copy(out=KS[hh][0:64, sl], in_=p[0:64, 2 + hh, :]), [p], [KS[hh]])
            A(lambda e, p=p, sl=sl: e.copy(out=KW[:, :, sl], in_=p[0:64, 4:6, :]), [p], [KW])
            V(lambda e, zb=zb, ti=ti: e.tensor_copy(out=VS[:, ti, :, 0:64], in_=zb[:, 384:512].rearrange("p (h d) -> p h d", h=2)), [zb], [VS])
            V(lambda e, zb=zb, ti=ti: e.tensor_copy(out=VW[:, ti, :, 0:64], in_=zb[:, 640:768].rearrange("p (h d) -> p h d", h=2)), [zb], [VW])

        NS = SB * ST
        cs16 = sb(ph, "cs16", [NS, 16], F32)
        for b in range(SB):
            LD(cs16[b * ST:(b + 1) * ST, :], s_cs[:, :], [cs16])
        zkv = zkv_r.next()
        a1_tile(xs[:, :], NS, lambda lo, hi: cs16[:, lo:hi], cs16, zkv)
        for s6 in range(6):
            ST_(skv[s6, :, :], zkv[0:NS, s6 * 128:(s6 + 1) * 128], [zkv], q='sp')
        for b in range(SB):
            ST_(swk[b, 0:512 - ST, :], ckw[b, ST:512, :], [], q='sp')
            ST_(swv[b, 0:512 - ST, :], cvw[b, ST:512, :], [], q='sp')
            ST_(swk[b, 512 - ST:512, :], zkv[b * ST:(b + 1) * ST, 512:640], [zkv], q='sp')
            ST_(swv[b, 512 - ST:512, :], zkv[b * ST:(b + 1) * ST, 640:768], [zkv], q='sp')
    S.barrier()

    def gelu_to(out_bf, p, n, bias, tmp):
        x = tmp[:, 0, 0:n]
        A(lambda e: e.activation(out=x, in_=p[:, 0:n], func=AF.Identity, bias=bias), [p], [tmp])
        V(lambda e: e.tensor_tensor(out=tmp[:, 1, 0:n], in0=x, in1=x, op=ALU.mult), [tmp], [tmp])
        V(lambda e: e.tensor_scalar(out=tmp[:, 1, 0:n], in0=tmp[:, 1, 0:n], scalar1=0.044715 * 1.5957691216, scalar2=1.5957691216, op0=ALU.mult, op1=ALU.add), [tmp], [tmp])
        V(lambda e: e.tensor_tensor(out=tmp[:, 1, 0:n], in0=tmp[:, 1, 0:n], in1=x, op=ALU.mult), [tmp], [tmp])
        A(lambda e: e.activation(out=tmp[:, 2, 0:n], in_=tmp[:, 1, 0:n], func=AF.Sigmoid), [tmp], [tmp])
        V(lambda e: e.tensor_tensor(out=out_bf, in0=tmp[:, 2, 0:n], in1=x, op=ALU.mult), [tmp], [out_bf_owner[0]])

    out_bf_owner = [None]

    def compress(KCk, KCv, nblk, KcT_out, Vca_out, Msb, nsel):
        nct = (nblk + 127) // 128
        for hh in range(2):
            for kv, KCx in ((0, KCk), (1, KCv)):
                p = pmm.next()
                for j in range(32):
                    PE(lambda e, j=j, kv=kv, hh=hh, p=p, KCx=KCx: e.matmul(p[:, 0:nblk], lhsT=w1[kv][hh * 64:(hh + 1) * 64, j, :], rhs=KCx[hh * 64:(hh + 1) * 64, j:j + 16 * (nblk - 1) + 1:16], start=(j == 0), stop=(j == 31)), [w1[kv], KCx], [p], inc=(j == 31))
                out_bf_owner[0] = hid
                gelu_to(hid[:, 0:nblk], p, nblk, cb[:, kv, 0:1], gtmp)
                if kv == 0:
                    p2 = pmm.next()
                    PE(lambda e, p2=p2: e.matmul(p2[0:64, 0:nblk], lhsT=w2[0][:, :], rhs=hid[:, 0:nblk], start=True, stop=True), [w2[0], hid], [p2])
                    A(lambda e, p2=p2, hh=hh: e.copy(out=KcT_out[:, hh, 0:nblk], in_=p2[0:64, 0:nblk]), [p2], [KcT_out])
                else:
                    for ct in range(nct):
                        n = min(128, nblk - ct * 128)
                        p2 = pmm.next()
                        PE(lambda e, p2=p2, ct=ct, n=n: e.matmul(p2[0:n, 0:64], lhsT=hid[:, ct * 128:ct * 128 + n], rhs=w2[1][:, :], start=True, stop=True), [w2[1], hid], [p2])
                        A(lambda e, p2=p2, ct=ct, n=n, hh=hh: e.copy(out=Vca_out[0:n, ct, hh, 0:64], in_=p2[0:n, 0:64]), [p2], [Vca_out])
        for hh in range(2):
            V(lambda e, hh=hh: e.tensor_copy(out=Vca_out[:, :, hh, 64:64 + nsel], in_=Msb[:, :, :]), [Msb], [Vca_out])
            G(lambda e, hh=hh: e.memset(Vca_out[:, :, hh, 64 + nsel:64 + nsel + 1], 1.0), [], [Vca_out])

    G(lambda e: e.memset(KcT[:, :, :], 0.0), [], [KcT])
    G(lambda e: e.memset(Vca[:, :, :, :], 0.0), [], [Vca])
    compress(KC[0], KC[1], 255, KcT, Vca, Mc, 64)
    S.barrier()
    cstk.close()

    with ExitStack() as ph:
        KCs = [sb(ph, "KCs%d" % i, [128, 8320], BF16) for i in range(2)]
        rings = (Ring([sb(ph, "stg4", [128, 4, 128], F32) for _ in range(3)]), Ring([sb(ph, "stb4", [128, 4, 128], BF16) for _ in range(3)]))
        for b in range(SB):
            for X in range(2):
                stream_pages(rings, pool_groups(b, X, b * ST), dstT=(None, KCs[X]), full=True)
            G(lambda e, b=b: e.memset(KcT_s[b][:, :, :], 0.0), [], [KcT_s[b]])
            G(lambda e, b=b: e.memset(Vca_s[b][:, :, :, :], 0.0), [], [Vca_s[b]])
            compress(KCs[0], KCs[1], 511, KcT_s[b], Vca_s[b], Ms, 192)
    S.barrier()
    cw.close()

    with ExitStack() as ph:
        load_g(ph, [0])
        tri4, zeros_b = mk_masks(ph)
        wq = load_w(ph, "wq", w_in, D, IN_W, 0, 512)
        wg = load_w(ph, "wg", w_in, D, IN_W, 1280, 1304)
        cm_r = Ring([sb(ph, "cmask", [128, 2, 128], BF16) for _ in range(2)])
        cm4_r = Ring([sb(ph, "cmask4", [128, 2, 4, 128], BF16) for _ in range(2)])
        AB_r = Ring([sb(ph, "AB", [128, 2, 64], F32) for _ in range(2)])
        zq = sb(ph, "zq", [128, 512], F32)
        zqb = sb(ph, "zqb", [128, 512], BF16)
        Qa = sb(ph, "Qa", [128, 8, 128], BF16)
        gsig = sb(ph, "gsig", [128, 24], F32)
        rtmp = sb(ph, "rtmpb", [128, 4, 64], F32)
        Pb_r = Ring([sb(ph, "Pb", [128, 4, 128], BF16) for _ in range(4)])
        acc_c = sb(ph, "acc_c", [128, 4, 130], F32)
        impt = sb(ph, "impt", [128, 4, 64], F32)
        m8 = sb(ph, "m8", [128, 16], F32)
        nbw = sb(ph, "nbw", [128, 128], BF16)
        G(lambda e: e.memset(nbw[:, :], 0.0), [], [nbw])
        osb_r = Ring([sb(ph, "osb", [128, 512], F32) for _ in range(2)])
        otmp = sb(ph, "otmp", [128, 4, 64], F32)
        cst = sb(ph, "cst", [128, 16], F32)
        cst2 = [sb(ph, "cst2_%d" % i, [128, 16], F32) for i in range(2)]
        impt2 = [sb(ph, "impt2_%d" % i, [128, 4, 64], F32) for i in range(2)]
        m82 = [sb(ph, "m82_%d" % i, [128, 16], F32) for i in range(2)]
        nbw2 = [sb(ph, "nbw2_%d" % i, [128, 128], BF16) for i in range(2)]
        acc_c2 = [sb(ph, "acc_c2_%d" % i, [128, 4, 130], F32) for i in range(2)]
        for i in range(2):
            G(lambda e, i=i: e.memset(nbw2[i][:, :], 0.0), [], [nbw2[i]])
        accsb_r = Ring([sb(ph, "accsb", [128, 512], F32) for _ in range(2)])
        actx = (Qa, Pb_r, accsb_r)
        gst2_r = Ring([sb(ph, "gst2", [NPAGE, 2048], F32) for _ in range(3)])
        slc_jobs = [(b, X, c) for X in (2, 3) for b in range(SB) for c in range(8)]

        for qt in range(NT_OWN):
            ti = NT_OWN + qt
            emit_gathers(slc_jobs[qt * 4:(qt + 1) * 4], gst2_r, 'pool')
            xt = xt_r.next()
            LD(xt[:, :], xp[ti * 128:(ti + 1) * 128, :], [xt])
            cm = cm_r.next()
            LDC(cm[:, :, :], c_cmask[:, :, qt * 128:(qt + 1) * 128], [cm])
            cm4 = cm4_r.next()
            for ct_ in range(2):
                V(lambda e, ct_=ct_, cm=cm, cm4=cm4: e.tensor_copy(out=cm4[:, ct_, :, :], in_=cm[:, ct_, :].unsqueeze(1).to_broadcast([128, 4, 128])), [cm], [cm4])
            AB = AB_r.next()
            LD(AB[:, 0, :], c_A[:, qt, :], [AB])
            LD(AB[:, 1, :], c_B[:, qt, :], [AB])
            h = h_r.next()
            rmsnorm(xt, 128, 0, h)
            hT = hT_r.next()
            transpose_bf(h, 128, 8, hT)

            def consq(p, g0, gn):
                A(lambda e: e.copy(out=zq[:, :], in_=p[:, 0:512]), [p], [zq])
            linear(hT, 128, 8, wq, 0, 512, consq)
            zqv = zq[:, :].rearrange("p (h d) -> p h d", h=8)
            rope(zq, 128, lambda lo, hi: zqv[:, :, lo:hi], (lambda lo, hi, ti=ti: cs_p[:, ti, lo:hi], cs_p), rtmp)
            V(lambda e: e.tensor_copy(out=zqb[:, :], in_=zq[:, :]), [zq], [zqb])
            transpose_bf(zqb, 128, 8, Qa, width=64)

            def consg(p, g0, gn):
                A(lambda e: e.activation(out=gsig[:, :], in_=p[:, 0:24], func=AF.Sigmoid), [p], [gsig])
            linear(hT, 128, 8, wg, 0, 24, consg)

            osb = osb_r.next()
            def gview_of(hh):
                return gsig[:, hh * 12:(hh + 1) * 12].rearrange("p (g t) -> p g t", t=3)

            def ov_of(hh):
                return osb[:, hh * 256:(hh + 1) * 256].rearrange("p (g d) -> p g d", g=4)

            def consume(hh, bi, accX):
                av = accX[:, :].rearrange("p (g n) -> p g n", g=4)
                cs_ = cst2[hh]
                gv_ = gview_of(hh)[:, :, bi]
                ov_ = ov_of(hh)
                V(lambda e: e.tensor_scalar(out=cs_[:, 8:12], in0=av[:, :, 64], scalar1=1e-30, scalar2=None, op0=ALU.max), [accX], [cs_])
                V(lambda e: e.reciprocal(out=cs_[:, 12:16], in_=cs_[:, 8:12]), [cs_], [cs_])
                V(lambda e: e.tensor_tensor(out=cs_[:, 12:16], in0=cs_[:, 12:16], in1=gv_, op=ALU.mult), [cs_, gsig], [cs_])
                V(lambda e: e.tensor_tensor(out=otmp[:, :, :], in0=av[:, :, 0:64], in1=cs_[:, 12:16].unsqueeze(2).to_broadcast([128, 4, 64]), op=ALU.mult), [accX, cs_], [otmp])
                V(lambda e: e.tensor_tensor(out=ov_, in0=ov_, in1=otmp[:, :, :], op=ALU.add), [osb, otmp], [osb])

            for hh in range(2):
                accC = pacc.next()
                attn_branch(actx, 128, accC, hh, 64, [0, 1],
                            lambda ct, hh=hh: (KcT[:, hh, ct * 128:(ct + 1) * 128], 128),
                            lambda ct, hh=hh: Vca[:, ct, hh, 0:128], 128,
                            lambda ct, cm4=cm4: (cm4[:, ct, :, :], cm4), [KcT, Vca])
                accv = accC[:, :].rearrange("p (g n) -> p g n", g=4)
                cs_ = cst2[hh]
                im_ = impt2[hh]
                m8_ = m82[hh]
                nb_ = nbw2[hh]
                ac_ = acc_c2[hh]
                V(lambda e: e.tensor_reduce(out=cs_[:, 0:4], in_=accv[:, :, 64:128], axis=AX.X, op=ALU.add), [accC], [cs_])
                V(lambda e: e.tensor_scalar(out=cs_[:, 0:4], in0=cs_[:, 0:4], scalar1=1e-30, scalar2=None, op0=ALU.max), [cs_], [cs_])
                V(lambda e: e.reciprocal(out=cs_[:, 4:8], in_=cs_[:, 0:4]), [cs_], [cs_])
                V(lambda e: e.tensor_tensor(out=ac_[:, :, 0:128], in0=accv[:, :, 0:128], in1=cs_[:, 4:8].unsqueeze(2).to_broadcast([128, 4, 128]), op=ALU.mult), [accC, cs_], [ac_])
                ov_ = ov_of(hh)
                gv0_ = gview_of(hh)[:, :, 0:1].to_broadcast([128, 4, 64])
                V(lambda e: e.tensor_tensor(out=ov_, in0=ac_[:, :, 0:64], in1=gv0_, op=ALU.mult), [ac_, gsig], [osb])
                V(lambda e: e.tensor_reduce(out=im_[:, 0, :], in_=ac_[:, :, 64:128].rearrange("p g n -> p n g"), axis=AX.X, op=ALU.add), [ac_], [im_])
                V(lambda e, AB=AB: e.tensor_tensor(out=im_[:, 0, :], in0=im_[:, 0, :], in1=AB[:, 0, :], op=ALU.mult), [im_, AB], [im_])
                V(lambda e, AB=AB: e.tensor_tensor(out=im_[:, 0, :], in0=im_[:, 0, :], in1=AB[:, 1, :], op=ALU.add), [im_, AB], [im_])
                V(lambda e: e.max(out=m8_[:, 0:8], in_=im_[:, 0, :]), [im_], [m8_])
                V(lambda e: e.match_replace(out=im_[:, 1, :], in_to_replace=m8_[:, 0:8], in_values=im_[:, 0, :], imm_value=-1e30), [im_, m8_], [im_])
                V(lambda e: e.max(out=m8_[:, 8:16], in_=im_[:, 1, :]), [im_], [m8_])
                V(lambda e: e.tensor_scalar(out=m8_[:, 15:16], in0=m8_[:, 15:16], scalar1=-0.5, scalar2=None, op0=ALU.max), [m8_], [m8_])
                V(lambda e: e.tensor_scalar(out=im_[:, 2, :], in0=im_[:, 0, :], scalar1=m8_[:, 15:16], scalar2=None, op0=ALU.is_ge), [im_, m8_], [im_])
                V(lambda e: e.tensor_scalar(out=nb_[:, 64:128], in0=im_[:, 2, :], scalar1=-1.0, scalar2=-NEG, op0=ALU.add, op1=ALU.mult), [im_], [nb_])
            for hh in range(2):
                accW = pacc.next()
                attn_branch(actx, 128, accW, hh, 64, list(range(ti - 4, ti + 1)),
                            lambda kt, hh=hh: (KW[:, hh, kt * 128:(kt + 1) * 128], 128),
                            lambda kt, hh=hh: VW[:, kt, hh, 0:65], 65,
                            lambda kt, ti=ti: (tri4[:, 0, :, :], tri4) if kt == ti else ((tri4[:, 1, :, :], tri4) if kt == ti - 4 else None), [KW, VW])
                consume(hh, 2, accW)
            for hh in range(2):
                p = pT.next()
                PE(lambda e, p=p, hh=hh: e.transpose(out=p[:, 0, :], in_=nbw2[hh][:, :], identity=ident[:, :]), [nbw2[hh], ident], [p])
                for g in range(4):
                    A(lambda e, p=p, hh=hh, g=g: e.copy(out=Qa[64:128, hh * 4 + g, :], in_=p[64:128, 0, :]), [p], [Qa])
            for hh in range(2):
                accS = pacc.next()
                attn_branch(actx, 128, accS, hh, 128, list(range(0, ti + 1)),
                            lambda kt, hh=hh: (KS[hh][:, kt * 128:(kt + 1) * 128], 128),
                            lambda kt, hh=hh: VS[:, kt, hh, 0:65], 65,
                            lambda kt, ti=ti: (tri4[:, 0, :, :], tri4) if kt == ti else None, [KS[hh], VS])
                consume(hh, 1, accS)
            ST_(oscr[qt * 128:(qt + 1) * 128, :], osb[:, :], [osb], q='sp')
    S.barrier()
    stores.close()

    with ExitStack() as ph:
        load_g(ph, [0])
        tri4, zeros_b = mk_masks(ph)
        wq = load_w(ph, "wq", w_in, D, IN_W, 0, 512)
        wg = load_w(ph, "wg", w_in, D, IN_W, 1280, 1304)
        KSs = [sb(ph, "KSs%d" % i, [128, 8320], BF16) for i in range(2)]
        VSs = sb(ph, "VSs", [128, 66, 2, 66], BF16)
        KWs = sb(ph, "KWs", [64, 2, 640], BF16)
        VWs = sb(ph, "VWs", [128, 5, 2, 66], BF16)
        for i in range(2):
            LDC(KSs[i][64:128, 0:4096], c_E[:, :], [KSs[i]])
            LDC(KSs[i][64:128, 4096:8192], c_E[:, :], [KSs[i]])
            LDC(KSs[i][64:128, 8192:8320], c_E[:, 0:128], [KSs[i]])
        svalid = sb(ph, "svalid", [128, 66], F32)
        swvalid = sb(ph, "swvalid", [128, 5], F32)
        LD(svalid[:, :], s_valid[:, :], [svalid])
        LD(swvalid[:, :], s_wvalid[:, :], [swvalid])
        for i in range(2):
            V(lambda e, i=i: e.tensor_copy(out=VSs[:, :, i, 64], in_=svalid[:, :]), [svalid], [VSs])
            V(lambda e, i=i: e.tensor_copy(out=VWs[:, :, i, 64], in_=swvalid[:, :]), [swvalid], [VWs])
        m1 = sb(ph, "m1", [128, 10, ST], BF16)
        m4 = sb(ph, "m4", [128, 10, 4, ST], BF16)
        scm = sb(ph, "scm", [128, 4], F32)
        LD(scm[:, :], s_cmask[:, :], [scm])
        for ct in range(4):
            V(lambda e, ct=ct: e.tensor_copy(out=m1[:, ct, :], in_=scm[:, ct:ct + 1].to_broadcast([128, ST])), [scm], [m1])
        LDC(m1[:, 4:9, :], s_wmask[:, :, :], [m1])
        LDC(m1[:, 9, :], s_lmask[:, :], [m1])
        for i in range(10):
            V(lambda e, i=i: e.tensor_copy(out=m4[:, i, :, :], in_=m1[:, i, :].unsqueeze(1).to_broadcast([128, 4, ST])), [m1], [m4])
        ABs = sb(ph, "ABs", [ST, 2, 192], F32)
        LD(ABs[:, 0, :], s_A[:, :], [ABs])
        LD(ABs[:, 1, :], s_B[:, :], [ABs])
        rings = (Ring([sb(ph, "stg4", [128, 4, 128], F32) for _ in range(3)]), Ring([sb(ph, "stb4", [128, 4, 128], BF16) for _ in range(3)]))
        zq = sb(ph, "zq", [128, 512], F32)
        zqb = sb(ph, "zqb", [128, 512], BF16)
        QaS = sb(ph, "QaS", [128, 3, 8, ST], BF16)
        qT = sb(ph, "qT", [128, 8, 128], BF16)
        gsig = sb(ph, "gsig", [128, 24], F32)
        rtmp = sb(ph, "rtmpb", [128, 4, 64], F32)
        Pb_r = Ring([sb(ph, "Pb", [128, 4, 128], BF16) for _ in range(4)])
        acc_c = sb(ph, "acc_c", [ST, 4, 256], F32)
        impt = sb(ph, "impt", [ST, 4, 192], F32)
        m8 = sb(ph, "m8", [ST, 16], F32)
        nbw = sb(ph, "nbw", [ST, 3, 128], BF16)
        G(lambda e: e.memset(nbw[:, :, :], 0.0), [], [nbw])
        osb = sb(ph, "osb", [128, 512], F32)
        otmp = sb(ph, "otmp", [128, 4, 64], F32)
        cst = sb(ph, "cst", [128, 16], F32)
        accsb_r = Ring([sb(ph, "accsb", [128, 512], F32) for _ in range(2)])
        actx = (QaS, Pb_r, accsb_r)

        for b in range(SB):
            r0 = b * ST
            stream_pages(rings, pool_groups(b, 2, r0), dstT=(lambda hh, lo, hi: KSs[hh][0:64, lo:hi], lambda hh: KSs[hh]))
            stream_pages(rings, pool_groups(b, 3, r0), dstV=lambda j0, n: (VSs[:, j0:j0 + n, :, 0:64], VSs))
            stream_pages(rings, win_groups(b, ckw, 4, r0), dstT=(lambda hh, lo, hi: KWs[:, hh, lo:hi], KWs))
            stream_pages(rings, win_groups(b, cvw, 5, r0), dstV=lambda j0, n: (VWs[:, j0:j0 + n, :, 0:64], VWs))

            xt = xt_r.next()
            LD(xt[0:ST, :], xs[r0:r0 + ST, :], [xt])
            h = h_r.next()
            rmsnorm(xt, ST, 0, h)
            hT = hT_r.next()
            transpose_bf(h, ST, 8, hT)

            def consq(p, g0, gn):
                A(lambda e: e.copy(out=zq[0:ST, :], in_=p[0:ST, 0:512]), [p], [zq])
            linear(hT, ST, 8, wq, 0, 512, consq)
            zqv = zq[0:ST, :].rearrange("p (h d) -> p h d", h=8)
            rope(zq, ST, lambda lo, hi: zqv[:, :, lo:hi], (lambda lo, hi: cs_s[:, lo:hi], cs_s), rtmp)
            V(lambda e: e.tensor_copy(out=zqb[0:ST, :], in_=zq[0:ST, :]), [zq], [zqb])
            transpose_bf(zqb, ST, 8, qT, width=64)
            for v in range(3):
                V(lambda e, v=v: e.tensor_copy(out=QaS[0:64, v, :, :], in_=qT[0:64, :, 0:ST]), [qT], [QaS])

            def consg(p, g0, gn):
                A(lambda e: e.activation(out=gsig[0:ST, :], in_=p[0:ST, 0:24], func=AF.Sigmoid), [p], [gsig])
            linear(hT, ST, 8, wg, 0, 24, consg)

            for hh in range(2):
                for pair in range(2):
                    accC = pacc.next()
                    for vb in range(2):
                        attn_branch(actx, ST, accC, hh, 64, [0, 1, 2, 3],
                                    lambda ct, hh=hh: (KcT_s[b][:, hh, ct * 128:(ct + 1) * 128], 128),
                                    lambda ct, hh=hh, vb=vb: Vca_s[b][:, ct, hh, vb * 128:(vb + 1) * 128], 128,
                                    lambda ct: (m4[:, ct, :, :], m4), [KcT_s[b], Vca_s[b]],
                                    q_fn=lambda kt, hh=hh: QaS[0:64, 0, hh * 4:hh * 4 + 4, :],
                                    heads=(pair * 2, pair * 2 + 1), gstride=256, col0=vb * 128)
                    accv = accC[0:ST, :].rearrange("p (g n) -> p g n", g=2)
                    V(lambda e, accv=accv, pair=pair: e.tensor_reduce(out=cst[0:ST, pair * 2:pair * 2 + 2], in_=accv[:, :, 64:256], axis=AX.X, op=ALU.add), [accC], [cst])
                    V(lambda e, accv=accv, pair=pair: e.tensor_copy(out=acc_c[:, pair * 2:pair * 2 + 2, :], in_=accv), [accC], [acc_c])
                V(lambda e: e.tensor_scalar(out=cst[0:ST, 0:4], in0=cst[0:ST, 0:4], scalar1=1e-30, scalar2=None, op0=ALU.max), [cst], [cst])
                V(lambda e: e.reciprocal(out=cst[0:ST, 4:8], in_=cst[0:ST, 0:4]), [cst], [cst])
                V(lambda e: e.tensor_tensor(out=acc_c[:, :, :], in0=acc_c[:, :, :], in1=cst[0:ST, 4:8].unsqueeze(2).to_broadcast([ST, 4, 256]), op=ALU.mult), [acc_c, cst], [acc_c])
                V(lambda e: e.tensor_reduce(out=impt[:, 0, :], in_=acc_c[:, :, 64:256].rearrange("p g n -> p n g"), axis=AX.X, op=ALU.add), [acc_c], [impt])
                V(lambda e: e.tensor_tensor(out=impt[:, 0, :], in0=impt[:, 0, :], in1=ABs[:, 0, :], op=ALU.mult), [impt, ABs], [impt])
                V(lambda e: e.tensor_tensor(out=impt[:, 0, :], in0=impt[:, 0, :], in1=ABs[:, 1, :], op=ALU.add), [impt, ABs], [impt])
                V(lambda e: e.max(out=m8[:, 0:8], in_=impt[:, 0, :]), [impt], [m8])
                V(lambda e: e.match_replace(out=impt[:, 1, :], in_to_replace=m8[:, 0:8], in_values=impt[:, 0, :], imm_value=-1e30), [impt, m8], [impt])
                V(lambda e: e.max(out=m8[:, 8:16], in_=impt[:, 1, :]), [impt], [m8])
                V(lambda e: e.tensor_scalar(out=m8[:, 15:16], in0=m8[:, 15:16], scalar1=-0.5, scalar2=None, op0=ALU.max), [m8], [m8])
                V(lambda e: e.tensor_scalar(out=impt[:, 2, :], in0=impt[:, 0, :], scalar1=m8[:, 15:16], scalar2=None, op0=ALU.is_ge), [impt, m8], [impt])
                V(lambda e: e.tensor_scalar(out=nbw[:, :, 64:128], in0=impt[:, 2, :].rearrange("p (v n) -> p v n", v=3), scalar1=-1.0, scalar2=-NEG, op0=ALU.add, op1=ALU.mult), [impt], [nbw])
                p = pT.next()
                for v in range(3):
                    PE(lambda e, p=p, v=v: e.transpose(out=p[:, v, 0:ST], in_=nbw[0:ST, v, :], identity=ident[0:ST, 0:ST]), [nbw, ident], [p])
                for v in range(3):
                    for g in range(4):
                        A(lambda e, p=p, hh=hh, g=g, v=v: e.copy(out=QaS[64:128, v, hh * 4 + g, :], in_=p[64:128, v, 0:ST]), [p], [QaS])
                accS = pacc.next()
                attn_branch(actx, ST, accS, hh, 128, list(range(0, 65)),
                            lambda kt, hh=hh: (KSs[hh][:, kt * 128:(kt + 1) * 128], 128),
                            lambda kt, hh=hh: VSs[:, kt, hh, 0:65], 65,
                            lambda kt: (m4[:, 9, :, :], m4) if kt == 64 else None, [KSs[hh], VSs],
                            q_fn=lambda kt, hh=hh: QaS[:, min(kt // 32, 2), hh * 4:hh * 4 + 4, :])
                accW = pacc.next()
                attn_branch(actx, ST, accW, hh, 64, list(range(5)),
                            lambda kt, hh=hh: (KWs[:, hh, kt * 128:(kt + 1) * 128], 128),
                            lambda kt, hh=hh: VWs[:, kt, hh, 0:65], 65,
                            lambda kt: (m4[:, 4 + kt, :, :], m4), [KWs, VWs],
                            q_fn=lambda kt, hh=hh: QaS[0:64, 0, hh * 4:hh * 4 + 4, :])
                gview = gsig[0:ST, hh * 12:(hh + 1) * 12].rearrange("p (g t) -> p g t", t=3)
                ov = osb[0:ST, hh * 256:(hh + 1) * 256].rearrange("p (g d) -> p g d", g=4)
                V(lambda e, gview=gview, ov=ov: e.tensor_tensor(out=ov, in0=acc_c[:, :, 0:64], in1=gview[:, :, 0:1].to_broadcast([ST, 4, 64]), op=ALU.mult), [acc_c, gsig], [osb])
                for bi, accX in ((1, accS), (2, accW)):
                    av = accX[0:ST, :].rearrange("p (g n) -> p g n", g=4)
                    V(lambda e, av=av: e.tensor_scalar(out=cst[0:ST, 8:12], in0=av[:, :, 64], scalar1=1e-30, scalar2=None, op0=ALU.max), [accX], [cst])
                    V(lambda e: e.reciprocal(out=cst[0:ST, 12:16], in_=cst[0:ST, 8:12]), [cst], [cst])
                    V(lambda e, gview=gview, bi=bi: e.tensor_tensor(out=cst[0:ST, 12:16], in0=cst[0:ST, 12:16], in1=gview[:, :, bi], op=ALU.mult), [cst, gsig], [cst])
                    V(lambda e, av=av: e.tensor_tensor(out=otmp[0:ST, :, :], in0=av[:, :, 0:64], in1=cst[0:ST, 12:16].unsqueeze(2).to_broadcast([ST, 4, 64]), op=ALU.mult), [accX, cst], [otmp])
                    V(lambda e, ov=ov: e.tensor_tensor(out=ov, in0=ov, in1=otmp[0:ST, :, :], op=ALU.add), [osb, otmp], [osb])
            ST_(oscr[HALF + r0:HALF + r0 + ST, :], osb[0:ST, :], [osb], q='sp')
    S.barrier()
    sper.close()

    with ExitStack() as ph:
        load_g(ph, [0])
        wr = load_w(ph, "wr", w_in, D, IN_W, 1304, IN_W)
        wno = load_w(ph, "wno", w_nsa_o, 512, D)
        wpw = load_w(ph, "wpw", w_pw, 512, D)
        wo = load_w(ph, "wo", w_out, D, D)
        cv = sb(ph, "cv", [128, 12], F32)
        LD(cv[:, :], cvec[:, :], [cv])
        wdw_sb = sb(ph, "wdw_sb", [31, 512], F32)
        LD(wdw_sb[:, :], w_dw[:, :], [wdw_sb])
        wdT = sb(ph, "wdT", [128, 4, 32], F32)
        pw_ = pmm.next()
        for c in range(4):
            PE(lambda e, c=c: e.transpose(out=pw_[:, c * 32:c * 32 + 31], in_=wdw_sb[0:31, c * 128:(c + 1) * 128], identity=identf[0:31, 0:31]), [wdw_sb, identf], [pw_])
        V(lambda e: e.tensor_copy(out=wdT[:, :, 0:31], in_=pw_[:, 0:128].rearrange("p (c j) -> p c j", c=4)[:, :, 0:31]), [pw_], [wdT])
        dg = sb(ph, "dg", [128, 4, 31, 128], BF16)
        for c in range(4):
            for j in range(31):
                eng = V if (j % 2 == 0) else G
                eng(lambda e, c=c, j=j: e.tensor_scalar(out=dg[:, c, j, :], in0=identf[:, :], scalar1=wdT[:, c, j:j + 1], scalar2=None, op0=ALU.mult), [identf, wdT], [dg])
        ones_f = sb(ph, "ones_f", [128, 128], F32)
        G(lambda e: e.memset(ones_f[:, :], 1.0 / 512.0), [], [ones_f])
        gluT_r = Ring([sb(ph, "gluT", [128, 4, 30 + 128], BF16) for _ in range(2)])
        for g_ in gluT_r.bufs:
            G(lambda e, g_=g_: e.memset(g_[:, :, :], 0.0), [], [g_])
        glu = sb(ph, "glu", [128, 512], F32)
        glub = sb(ph, "glub", [128, 512], BF16)
        gm = sb(ph, "gm", [128, 2048], F32)
        osb = sb(ph, "osb2", [128, 512], F32)
        osbb = sb(ph, "osbb", [128, 512], BF16)
        oT = sb(ph, "oT", [128, 4, 128], BF16)
        ynsa = sb(ph, "ynsa", [128, D], F32)
        gtm = sb(ph, "gtm", [128, D], F32)
        cconv = sb(ph, "cconv", [128, 4, 128], F32)
        csq = sb(ph, "csq", [128, 4, 128], F32)
        lnst = sb(ph, "lnst", [128, 4, 128], F32)
        csT = sb(ph, "csT", [128, 4, 128], BF16)
        mg = sb(ph, "mg", [128, D], F32)
        mgb = sb(ph, "mgb", [128, D], BF16)
        mT = sb(ph, "mT", [128, 8, 128], BF16)
        x1 = mg

        def proj_glu(hT, nt):
            def cons(p, g0, gn):
                c = g0
                if c < 512:
                    A(lambda e: e.copy(out=gtm[0:nt, c:c + gn], in_=p[0:nt, 0:gn]), [p], [gtm])
                else:
                    A(lambda e: e.activation(out=gtm[0:nt, c:c + gn], in_=p[0:nt, 0:gn], func=AF.Sigmoid), [p], [gtm])
            linear(hT, nt, 8, wr, 0, 1024, cons)
            V(lambda e: e.tensor_tensor(out=glu[0:nt, :], in0=gtm[0:nt, 0:512], in1=gtm[0:nt, 512:1024], op=ALU.mult), [gtm], [glu])
            V(lambda e: e.tensor_copy(out=glub[0:nt, :], in_=glu[0:nt, :]), [glu], [glub])

        def glu_to_T(gluT, nt, col0):
            p = pT.next()
            for c in range(4):
                PE(lambda e, c=c, p=p: e.transpose(out=p[:, c, 0:nt], in_=glub[0:nt, c * 128:(c + 1) * 128], identity=ident[0:nt, 0:nt]), [glub, ident], [p])
            A(lambda e, p=p: e.copy(out=gluT[:, :, col0:col0 + nt], in_=p[:, 0:4, 0:nt]), [p], [gluT])

        def b2_rest(gluT, xt, hT, nt, o_src, dst_ap):
            def consm(p, g0, gn):
                c = g0 - 1024
                A(lambda e: e.activation(out=gm[0:nt, c:c + gn], in_=p[0:nt, 0:gn], func=AF.Sigmoid), [p], [gm])
            linear(hT, nt, 8, wr, 1024, 1024 + 2048, consm)
            LD(osb[0:nt, :], o_src, [osb])
            V(lambda e: e.tensor_copy(out=osbb[0:nt, :], in_=osb[0:nt, :]), [osb], [osbb])
            transpose_bf(osbb, nt, 4, oT)

            def consn(p, g0, gn):
                V(lambda e: e.tensor_tensor(out=mg[0:nt, g0:g0 + gn], in0=p[0:nt, 0:gn], in1=gm[0:nt, g0:g0 + gn], op=ALU.mult), [p, gm], [mg])
            linear(oT, nt, 4, wno, 0, D, consn)
            for c in range(4):
                p = pmm.next()
                for j in range(31):
                    PE(lambda e, c=c, j=j, p=p: e.matmul(p[:, 0:nt], lhsT=dg[:, c, j, :], rhs=gluT[:, c, j:j + nt], start=(j == 0), stop=(j == 30)), [dg, gluT], [p], inc=(j == 30))
                A(lambda e, c=c, p=p: e.activation(out=cconv[:, c, 0:nt], in_=p[:, 0:nt], func=AF.Identity, bias=cv[:, c:c + 1]), [p, cv], [cconv])
            V(lambda e: e.tensor_tensor(out=csq[:, :, 0:nt], in0=cconv[:, :, 0:nt], in1=cconv[:, :, 0:nt], op=ALU.mult), [cconv], [csq])
            pst = pmm.next()
            for c in range(4):
                PE(lambda e, c=c: e.matmul(pst[:, 0:nt], lhsT=ones_f[:, :], rhs=cconv[:, c, 0:nt], start=(c == 0), stop=(c == 3)), [ones_f, cconv], [pst], inc=(c == 3))
            pst2 = pmm.next()
            for c in range(4):
                PE(lambda e, c=c: e.matmul(pst2[:, 0:nt], lhsT=ones_f[:, :], rhs=csq[:, c, 0:nt], start=(c == 0), stop=(c == 3)), [ones_f, csq], [pst2], inc=(c == 3))
            V(lambda e: e.tensor_copy(out=lnst[:, 0, 0:nt], in_=pst[:, 0:nt]), [pst], [lnst])
            V(lambda e: e.tensor_tensor(out=lnst[:, 1, 0:nt], in0=lnst[:, 0, 0:nt], in1=lnst[:, 0, 0:nt], op=ALU.mult), [lnst], [lnst])
            V(lambda e: e.tensor_tensor(out=lnst[:, 1, 0:nt], in0=pst2[:, 0:nt], in1=lnst[:, 1, 0:nt], op=ALU.subtract), [pst2, lnst], [lnst])
            V(lambda e: e.tensor_scalar(out=lnst[:, 1, 0:nt], in0=lnst[:, 1, 0:nt], scalar1=EPS, scalar2=None, op0=ALU.add), [lnst], [lnst])
            A(lambda e: e.activation(out=lnst[:, 2, 0:nt], in_=lnst[:, 1, 0:nt], func=AF.Sqrt), [lnst], [lnst])
            V(lambda e: e.reciprocal(out=lnst[:, 3, 0:nt], in_=lnst[:, 2, 0:nt]), [lnst], [lnst])
            V(lambda e: e.tensor_tensor(out=cconv[:, :, 0:nt], in0=cconv[:, :, 0:nt], in1=lnst[:, 0:1, 0:nt].to_broadcast([128, 4, nt]), op=ALU.subtract), [cconv, lnst], [cconv])
            V(lambda e: e.tensor_tensor(out=cconv[:, :, 0:nt], in0=cconv[:, :, 0:nt], in1=lnst[:, 3:4, 0:nt].to_broadcast([128, 4, nt]), op=ALU.mult), [cconv, lnst], [cconv])
            for c in range(4):
                V(lambda e, c=c: e.tensor_scalar(out=cconv[:, c, 0:nt], in0=cconv[:, c, 0:nt], scalar1=cv[:, 4 + c:5 + c], scalar2=cv[:, 8 + c:9 + c], op0=ALU.mult, op1=ALU.add), [cconv, cv], [cconv])
            A(lambda e: e.activation(out=csq[:, :, 0:nt], in_=cconv[:, :, 0:nt], func=AF.Sigmoid), [cconv], [csq])
            V(lambda e: e.tensor_tensor(out=csT[:, :, 0:nt], in0=cconv[:, :, 0:nt], in1=csq[:, :, 0:nt], op=ALU.mult), [cconv, csq], [csT])

            def consc(p, g0, gn):
                V(lambda e: e.tensor_tensor(out=ynsa[0:nt, g0:g0 + gn], in0=p[0:nt, 0:gn], in1=gm[0:nt, D + g0:D + g0 + gn], op=ALU.mult), [p, gm], [ynsa])
            linear(csT, nt, 4, wpw, 0, D, consc)
            V(lambda e: e.tensor_tensor(out=mgb[0:nt, :], in0=mg[0:nt, :], in1=ynsa[0:nt, :], op=ALU.add), [mg, ynsa], [mgb])
            transpose_bf(mgb, nt, 8, mT)

            def conso(p, g0, gn):
                V(lambda e: e.tensor_tensor(out=x1[0:nt, g0:g0 + gn], in0=p[0:nt, 0:gn], in1=xt[0:nt, g0:g0 + gn], op=ALU.add), [p, xt], [x1])
            linear(mT, nt, 8, wo, 0, D, conso)
            ST_(dst_ap, x1[0:nt, :], [x1], q='sp')

        def b2_part1(qt, prev_gluT):
            ti = NT_OWN + qt
            xt = xt_r.next()
            LD(xt[:, :], xp[ti * 128:(ti + 1) * 128, :], [xt])
            h = h_r.next()
            rmsnorm(xt, 128, 0, h)
            hT = hT_r.next()
            transpose_bf(h, 128, 8, hT)
            proj_glu(hT, 128)
            gluT = gluT_r.next()
            glu_to_T(gluT, 128, 30)
            if prev_gluT is not None:
                V(lambda e: e.tensor_copy(out=gluT[:, :, 0:30], in_=prev_gluT[:, :, 128:158]), [prev_gluT], [gluT])
            if qt == NT_OWN - 1:
                ST_(pconv[:, :], glu[98:128, :], [glu], q='sp')
            return (gluT, xt, hT)

        st_prev = b2_part1(-1, None)
        st_next = b2_part1(0, st_prev[0])
        for qt in range(NT_OWN):
            st_cur = st_next
            if qt + 1 < NT_OWN:
                st_next = b2_part1(qt + 1, st_cur[0])
            b2_rest(st_cur[0], st_cur[1], st_cur[2], 128, oscr[qt * 128:(qt + 1) * 128, :], x1s[qt * 128:(qt + 1) * 128, :])

        for b in range(SB):
            r0 = b * ST
            gluT = gluT_r.next()
            LD(glu[0:30, :], sconv_in[b, :, :], [glu])
            V(lambda e: e.tensor_copy(out=glub[0:30, :], in_=glu[0:30, :]), [glu], [glub])
            glu_to_T(gluT, 30, 0)
            xt = xt_r.next()
            LD(xt[0:ST, :], xs[r0:r0 + ST, :], [xt])
            h = h_r.next()
            rmsnorm(xt, ST, 0, h)
            hT = hT_r.next()
            transpose_bf(h, ST, 8, hT)
            proj_glu(hT, ST)
            glu_to_T(gluT, ST, 30)
            ST_(sconv[b, 0:30 - ST, :], sconv_in[b, ST:30, :], [], q='sp')
            ST_(sconv[b, 30 - ST:30, :], glu[0:ST, :], [glu], q='sp')
            b2_rest(gluT, xt, hT, ST, oscr[HALF + r0:HALF + r0 + ST, :], x1s[HALF + r0:HALF + r0 + ST, :])
    S.barrier()

    with ExitStack() as ph:
        load_g(ph, [1, 2, 3, 4])
        wxq = load_w(ph, "wxq", w_xq, D, 512)
        wxo = load_w(ph, "wxo", w_xo, 512, D)
        wup = load_w(ph, "wup", w_up, D, 4096, defer=True)
        wdn = load_w(ph, "wdn", w_down, 4096, D, defer=True)
        mkT = sb(ph, "mkT", [128, 4, 256], BF16)
        mva = sb(ph, "mva", [128, 2, 4, 130], BF16)
        G(lambda e: e.memset(mva[:, :, :, 128:130], 1.0), [], [mva])
        mkf = sb(ph, "mkf", [128, 512], F32)
        qx = sb(ph, "qx", [128, 512], BF16)
        mkb = qx
        with ExitStack() as ph2:
            wxk = load_w(ph2, "wxk", w_xk, D, 512)
            wxv = load_w(ph2, "wxv", w_xv, D, 512)
            wup.issue()
            wdn.issue()
            for mt in range(2):
                xt = xt_r.next()
                LD(xt[:, :], memp[mt * 128:(mt + 1) * 128, :], [xt])
                h = h_r.next()
                rmsnorm(xt, 128, 2, h)
                hT = hT_r.next()
                transpose_bf(h, 128, 8, hT)

                def consk(p, g0, gn):
                    A(lambda e: e.copy(out=mkf[:, :], in_=p[:, 0:512]), [p], [mkf])
                linear(hT, 128, 8, wxk, 0, 512, consk)
                ST_(pmk[mt * 128:(mt + 1) * 128, :], mkf[:, :], [mkf], q='sp')
                V(lambda e: e.tensor_copy(out=mkb[:, :], in_=mkf[:, :]), [mkf], [mkb])
                p = pT.next()
                for hd in range(4):
                    PE(lambda e, hd=hd, p=p: e.transpose(out=p[:, hd, :], in_=mkb[:, hd * 128:(hd + 1) * 128], identity=ident[:, :]), [mkb, ident], [p])
                A(lambda e, p=p, mt=mt: e.copy(out=mkT[:, :, mt * 128:(mt + 1) * 128], in_=p[:, 0:4, :]), [p], [mkT])

                def consv(p, g0, gn):
                    A(lambda e: e.copy(out=mkf[:, :], in_=p[:, 0:512]), [p], [mkf])
                linear(hT, 128, 8, wxv, 0, 512, consv)
                ST_(pmv[mt * 128:(mt + 1) * 128, :], mkf[:, :], [mkf], q='sp')
                V(lambda e, mt=mt: e.tensor_copy(out=mva[:, mt, :, 0:128], in_=mkf[:, :].rearrange("p (h d) -> p h d", h=4)), [mkf], [mva])
            S.barrier(only=('pe',))

        qxT = sb(ph, "qxT", [128, 4, 128], BF16)
        PX = sb(ph, "PX", [128, 2, 4, 128], BF16)
        ox = sb(ph, "ox", [128, 512], BF16)
        oxT = sb(ph, "oxT", [128, 4, 128], BF16)
        uT = sb(ph, "uT", [128, 32, 128], BF16)
        ur = mkf
        yo = sb(ph, "yo", [128, D], F32)
        cst = sb(ph, "cst2", [128, 8], F32)

        def cd_pre(src_ap, nt):
            xt = xt_r.next()
            LD(xt[0:nt, :], src_ap, [xt])
            h = h_r.next()
            rmsnorm(xt, nt, 1, h)
            hT = hT_r.next()
            transpose_bf(h, nt, 8, hT)

            def consq(p, g0, gn):
                A(lambda e: e.copy(out=qx[0:nt, :], in_=p[0:nt, 0:512]), [p], [qx])
            linear(hT, nt, 8, wxq, 0, 512, consq)
            transpose_bf(qx, nt, 4, qxT)
            return xt

        def cd_core(nq, col0):
            for mt in range(2):
                p = pmm.next()
                for hd in range(4):
                    PE(lambda e, hd=hd, mt=mt, p=p: e.matmul(p[:, hd * nq:(hd + 1) * nq], lhsT=mkT[:, hd, mt * 128:(mt + 1) * 128], rhs=qxT[:, hd, col0:col0 + nq], start=True, stop=True), [mkT, qxT], [p], inc=(hd == 3))
                A(lambda e, mt=mt, p=p: e.activation(out=PX[:, mt, :, 0:nq], in_=p[:, 0:4 * nq].rearrange("k (h q) -> k h q", h=4), func=AF.Exp, scale=128.0 ** -0.5), [p], [PX])
            for half in range(2):
                pa = pacc.next()
                for hh in range(2):
                    hd = half * 2 + hh
                    for mt in range(2):
                        PE(lambda e, hd=hd, hh=hh, mt=mt, pa=pa: e.matmul(pa[0:nq, hh * 130:hh * 130 + 129], lhsT=PX[:, mt, hd, 0:nq], rhs=mva[:, mt, hd, 0:129], start=(mt == 0), stop=(mt == 1)), [PX, mva], [pa], inc=(mt == 1 and hh == 1))
                pav = pa[0:nq, 0:260].rearrange("p (h n) -> p h n", h=2)
                V(lambda e, pav=pav: e.reciprocal(out=cst[0:nq, 0:2], in_=pav[:, :, 128]), [pa], [cst])
                V(lambda e, pav=pav, half=half: e.tensor_tensor(out=ox[0:nq, half * 256:(half + 1) * 256].rearrange("p (h d) -> p h d", h=2), in0=pav[:, :, 0:128], in1=cst[0:nq, 0:2].unsqueeze(2).to_broadcast([nq, 2, 128]), op=ALU.mult), [pa, cst], [ox])
            transpose_bf(ox, nq, 4, oxT, col0=col0)

        def cd_post(xt, nt):
            x2 = xt

            def conso(p, g0, gn):
                V(lambda e: e.tensor_tensor(out=x2[0:nt, g0:g0 + gn], in0=p[0:nt, 0:gn], in1=xt[0:nt, g0:g0 + gn], op=ALU.add), [p, xt], [x2])
            linear(oxT, nt, 4, wxo, 0, D, conso)
            return x2

        def cd_part1(src_ap, nt):
            xt = cd_pre(src_ap, nt)
            cd_core(nt, 0)
            return cd_post(xt, nt)

        def cd_part2(x2, nt, dst_ap):
            x3 = x2
            h2 = h_r.next()
            rmsnorm(x2, nt, 3, h2)
            hT2 = hT_r.next()
            transpose_bf(h2, nt, 8, hT2)
            for c4 in range(8):
                p = pmm.next()
                for cc in range(4):
                    c = c4 * 4 + cc
                    for i in range(8):
                        PE(lambda e, c=c, cc=cc, i=i, p=p: e.matmul(p[:, cc * nt:(cc + 1) * nt], lhsT=wup[:, i, c * 128:(c + 1) * 128], rhs=hT2[:, i, 0:nt], start=(i == 0), stop=(i == 7)), [wup.ktoks[i], hT2], [p], inc=(i == 7 and cc == 3))
                A(lambda e, p=p: e.activation(out=ur[:, 0:4 * nt], in_=p[:, 0:4 * nt], func=AF.Relu), [p], [ur])
                V(lambda e, c4=c4: e.tensor_tensor(out=uT[:, c4 * 4:(c4 + 1) * 4, 0:nt], in0=ur[:, 0:4 * nt].rearrange("p (c q) -> p c q", c=4), in1=ur[:, 0:4 * nt].rearrange("p (c q) -> p c q", c=4), op=ALU.mult), [ur], [uT])

            def consd(p, g0, gn):
                V(lambda e: e.tensor_tensor(out=x3[0:nt, g0:g0 + gn], in0=p[0:nt, 0:gn], in1=x2[0:nt, g0:g0 + gn], op=ALU.add), [p, x2], [x3])
            linear(uT, nt, 32, wdn, 0, D, consd)
            rmsnorm(x3, nt, 4, yo)
            ST_(dst_ap, yo[0:nt, :], [yo], q='sp')

        x2_next = cd_part1(x1s[0:128, :], 128)
        for qt in range(NT_OWN):
            x2_cur = x2_next
            if qt + 1 < NT_OWN:
                x2_next = cd_part1(x1s[(qt + 1) * 128:(qt + 2) * 128, :], 128)
            cd_part2(x2_cur, 128, yp[qt * 128:(qt + 1) * 128, :])

        NS = SB * ST
        xts = cd_pre(x1s[HALF:HALF + NS, :], NS)
        for b in range(SB):
            for mt in range(2):
                LD(mkf[:, :], cmk[b, mt * 128:(mt + 1) * 128, :], [mkf])
                V(lambda e: e.tensor_copy(out=mkb[:, :], in_=mkf[:, :]), [mkf], [mkb])
                p = pT.next()
                for hd in range(4):
                    PE(lambda e, hd=hd, p=p: e.transpose(out=p[:, hd, :], in_=mkb[:, hd * 128:(hd + 1) * 128], identity=ident[:, :]), [mkb, ident], [p])
                A(lambda e, p=p, mt=mt: e.copy(out=mkT[:, :, mt * 128:(mt + 1) * 128], in_=p[:, 0:4, :]), [p], [mkT])
                LD(mkf[:, :], cmv[b, mt * 128:(mt + 1) * 128, :], [mkf])
                V(lambda e, mt=mt: e.tensor_copy(out=mva[:, mt, :, 0:128], in_=mkf[:, :].rearrange("p (h d) -> p h d", h=4)), [mkf], [mva])
            cd_core(ST, b * ST)
        cd_part2(cd_post(xts, NS), NS, ys[:, :])

    S.finish()
    wk.close()
    glob.close()
    return nc


def _rope_tab(pos):
    half = 8
    inv = (500000.0 ** (-np.arange(half, dtype=np.float32) / half)).astype(np.float32)
    ang = pos.astype(np.float32)[:, None] * inv[None, :]
    return np.concatenate([np.cos(ang), np.sin(ang)], axis=1).astype(np.float32)


def _core_consts(half):
    c = {}
    off = 0 if half == 1 else -HALF
    lpos = np.arange(S_FULL)
    gpos = lpos + off
    cs = _rope_tab(np.maximum(gpos, 0))
    c["c_cs"] = np.ascontiguousarray(cs.reshape(NT_ALL, 128, 16).transpose(1, 0, 2))
    c["c_valid"] = np.ascontiguousarray((gpos >= 0).astype(np.float32).reshape(NT_ALL, 128).T)
    t = gpos[HALF:]
    n = np.arange(64)
    gn = n + (0 if half == 1 else -32)
    elig = (gn[None, :] >= 0) & (gn[None, :] * 64 <= t[:, None])
    cur = t // 64
    forced = (gn[None, :] == 0) | (gn[None, :] == cur[:, None]) | (gn[None, :] == cur[:, None] - 1)
    A = (elig & ~forced).astype(np.float32)
    B = np.where(~elig, -1.0, np.where(forced, 1.0e4, 0.0)).astype(np.float32)
    c["c_A"] = np.ascontiguousarray(A.reshape(NT_OWN, 128, 64).transpose(1, 0, 2))
    c["c_B"] = np.ascontiguousarray(B.reshape(NT_OWN, 128, 64).transpose(1, 0, 2))
    cl = np.arange(256)
    gc = cl + (0 if half == 1 else -128)
    cm = (gc[:, None] >= 0) & (cl[:, None] <= 254) & (16 * gc[:, None] + 31 <= t[None, :])
    c["c_cmask"] = np.ascontiguousarray(cm.astype(np.float32).reshape(2, 128, HALF).transpose(1, 0, 2))
    start = cl[:, None] * 16
    sel0 = n[None, :] * 64
    ov = np.clip(np.minimum(start + 32, sel0 + 64) - np.maximum(start, sel0), 0, None) / 32.0
    c["c_M"] = np.ascontiguousarray(ov.astype(np.float32).reshape(2, 128, 64).transpose(1, 0, 2))
    return c


def _common_consts():
    c = {}
    c["c_ident"] = np.eye(128, dtype=np.float32)
    key = np.arange(4096)
    c["c_E"] = (key[None, :] // 64 == np.arange(64)[:, None]).astype(np.float32)
    kk = np.arange(128)[:, None]
    qq = np.arange(128)[None, :]
    c["c_tri"] = np.ascontiguousarray(np.stack([(kk <= qq), (kk > qq)], axis=1).astype(np.float32))
    c["c_pidx"] = np.arange(128, dtype=np.float32)[:, None]
    c["c_c8"] = np.ascontiguousarray(np.broadcast_to(np.arange(8, dtype=np.float32)[None, :], (NPAGE, 8)))
    tpos = PAST + np.arange(ST)
    c["s_cs"] = _rope_tab(tpos)
    n = np.arange(192)
    elig = (n[None, :] <= 128) & (n[None, :] * 64 <= tpos[:, None])
    cur = tpos // 64
    forced = (n[None, :] == 0) | (n[None, :] == cur[:, None]) | (n[None, :] == cur[:, None] - 1)
    c["s_A"] = (elig & ~forced).astype(np.float32)
    c["s_B"] = np.where(~elig, -1.0, np.where(forced, 1.0e4, 0.0)).astype(np.float32)
    cl = np.arange(512)
    c["s_cmask"] = np.ascontiguousarray((cl <= 510).astype(np.float32).reshape(4, 128).T)
    start = cl[:, None] * 16
    sel0 = n[None, :] * 64
    ov = np.clip(np.minimum(start + 32, sel0 + 64) - np.maximum(start, sel0), 0, None) / 32.0
    ov[:, 129:] = 0
    c["s_M"] = np.ascontiguousarray(ov.astype(np.float32).reshape(4, 128, 192).transpose(1, 0, 2))
    r = np.arange(640)
    kpos = np.where(r < 512, PAST - 512 + r, PAST + (r - 512))
    kvalid = r < 512 + ST
    dt = tpos[None, :] - kpos[:, None]
    wm = kvalid[:, None] & (dt >= 0) & (dt < 512)
    c["s_wmask"] = np.ascontiguousarray(wm.astype(np.float32).reshape(5, 128, ST).transpose(1, 0, 2))
    c["s_wvalid"] = np.ascontiguousarray(kvalid.astype(np.float32).reshape(5, 128).T)
    r2 = np.arange(128)
    c["s_lmask"] = ((r2[:, None] < ST) & (r2[:, None] <= np.arange(ST)[None, :])).astype(np.float32)
    r3 = np.arange(66 * 128)
    c["s_valid"] = np.ascontiguousarray((r3 < PAST + ST).astype(np.float32).reshape(66, 128).T)
    return c


_PROG = {}


def kernel(x_prompt, x_sample, mem_prompt, cache_k_cmp, cache_v_cmp, cache_k_slc, cache_v_slc, cache_k_win,
           cache_v_win, state_conv, cache_mem_k, cache_mem_v, page_table, norm_mix, w_in, cmp_pos_k, cmp_pos_v,
           w_ck1, w_ck2, w_cv1, w_cv2, w_nsa_o, w_dw, b_dw, conv_ln_g, conv_ln_b, w_pw, w_out, norm_x, norm_mem,
           w_xq, w_xk, w_xv, w_xo, norm_ff, w_up, w_down, norm_final):
    f = lambda a: np.ascontiguousarray(np.asarray(a, dtype=np.float32))
    if "nc" not in _PROG:
        _PROG["nc"] = build_program()
    nc = _PROG["nc"]
    common = _common_consts()
    shared = {
        "w_in": f(w_in[0]), "cpos_k": f(cmp_pos_k[0]), "cpos_v": f(cmp_pos_v[0]),
        "w_ck1": f(w_ck1[0]), "w_cv1": f(w_cv1[0]), "w_ck2": f(w_ck2[0]), "w_cv2": f(w_cv2[0]),
        "w_nsa_o": f(w_nsa_o[0]), "w_dw": f(w_dw[0]), "w_pw": f(w_pw[0]), "w_out": f(w_out[0]),
        "w_xq": f(w_xq[0]), "w_xk": f(w_xk[0]), "w_xv": f(w_xv[0]), "w_xo": f(w_xo[0]),
        "w_up": f(w_up[0]), "w_down": f(w_down[0]),
        "cvec": np.ascontiguousarray(np.stack([f(b_dw[0]).reshape(4, 128), f(conv_ln_g[0]).reshape(4, 128), f(conv_ln_b[0]).reshape(4, 128)], 0).reshape(12, 128).T),
        "gvec": np.ascontiguousarray(np.broadcast_to(np.stack([f(norm_mix[0]), f(norm_x[0]), f(norm_mem[0]), f(norm_ff[0]), f(norm_final)], 0)[None], (128, 5, D))),
        "pk_cmp": f(cache_k_cmp[0]).reshape(2560 * 8, 2048), "pv_cmp": f(cache_v_cmp[0]).reshape(2560 * 8, 2048),
        "pk_slc": f(cache_k_slc[0]).reshape(2560 * 8, 2048), "pv_slc": f(cache_v_slc[0]).reshape(2560 * 8, 2048),
    }
    shared.update(common)
    cc = [_core_consts(0), _core_consts(1)]
    xpr = f(x_prompt)
    in_maps = []
    for core in range(8):
        b, half = core // 2, core % 2
        m = dict(shared)
        m.update(cc[half])
        xp_ = np.zeros((S_FULL, D), np.float32)
        if half == 1:
            xp_[:] = xpr[b]
        else:
            xp_[HALF:] = xpr[b, :HALF]
        m["xp"] = xp_
        m["memp"] = f(mem_prompt[b])
        sl = slice(core * SB, (core + 1) * SB)
        m["xs"] = f(x_sample[sl]).reshape(SB * ST, D)
        m["ckw"] = f(cache_k_win[0, sl]).reshape(SB, 512, 128)
        m["cvw"] = f(cache_v_win[0, sl]).reshape(SB, 512, 128)
        m["sconv_in"] = f(state_conv[0, sl])
        m["cmk"] = f(cache_mem_k[0, sl]).reshape(SB, 256, 512)
        m["cmv"] = f(cache_mem_v[0, sl]).reshape(SB, 256, 512)
        m["ptab"] = np.ascontiguousarray(np.asarray(page_table[sl], dtype=np.int32))
        in_maps.append(m)
    res = run_bass_kernel_spmd(nc, in_maps, core_ids=list(range(8))).results

    B4 = 4
    y_prompt = np.zeros((B4, S_FULL, D), np.float32)
    pst = [np.zeros((1, B4, S_FULL, 2, 64), np.float32) for _ in range(4)]
    pwin = [np.zeros((1, B4, 512, 2, 64), np.float32) for _ in range(2)]
    p_conv = np.zeros((1, B4, 30, 512), np.float32)
    p_mk = np.zeros((1, B4, 256, 4, 128), np.float32)
    p_mv = np.zeros((1, B4, 256, 4, 128), np.float32)
    y_sample = np.zeros((32, ST, D), np.float32)
    sst = [np.zeros((1, 32, ST, 2, 64), np.float32) for _ in range(4)]
    swin = [np.zeros((1, 32, 512, 2, 64), np.float32) for _ in range(2)]
    s_conv = np.zeros((1, 32, 30, 512), np.float32)
    for core in range(8):
        b, half = core // 2, core % 2
        r = res[core]
        ts = slice(half * HALF, (half + 1) * HALF)
        y_prompt[b, ts] = r["yp"]
        for i in range(4):
            pst[i][0, b, ts] = r["okv"][i].reshape(HALF, 2, 64)
        if half == 1:
            for i in range(2):
                pwin[i][0, b] = r["okv"][4 + i][HALF - 512:].reshape(512, 2, 64)
            p_conv[0, b] = r["pconv"]
            p_mk[0, b] = r["pmk"].reshape(256, 4, 128)
            p_mv[0, b] = r["pmv"].reshape(256, 4, 128)
        sl = slice(core * SB, (core + 1) * SB)
        y_sample[sl] = r["ys"].reshape(SB, ST, D)
        for i in range(4):
            sst[i][0, sl] = r["skv"][i].reshape(SB, ST, 2, 64)
        swin[0][0, sl] = r["swk"].reshape(SB, 512, 2, 64)
        swin[1][0, sl] = r["swv"].reshape(SB, 512, 2, 64)
        s_conv[0, sl] = r["sconv"]
    return (y_prompt, y_sample, pst[0], pst[1], pst[2], pst[3], pwin[0], pwin[1], p_conv, p_mk, p_mv,
            sst[0], sst[1], sst[2], sst[3], swin[0], swin[1], s_conv)
```

```python
from contextlib import ExitStack
import numpy as np
import concourse.bass as bass
import concourse.mybir as mybir
from concourse.bass_utils import run_bass_kernel_spmd

F32 = mybir.dt.float32
BF16 = mybir.dt.bfloat16
I32 = mybir.dt.int32
AF = mybir.ActivationFunctionType
ALU = mybir.AluOpType
AX = mybir.AxisListType

ENGS = ['pe', 'dve', 'act', 'pool', 'sp']
NDS = 48
SAME_ENG_SYNC = True

D = 1024
S_FULL = 4096
HALF = 2048
NT_ALL = 32
NT_OWN = 16
IN_W = 4376
EPS = 1e-6
NEG = -32768.0
SB = 4
ST = 4
PAST = 8192
NPAGE = 64
POOL_ROWS = 2560 * 128


import types


def _freeze(fn):
    if fn is None or fn.__closure__ is None:
        return fn
    cells = []
    for c in fn.__closure__:
        try:
            cells.append(types.CellType(c.cell_contents))
        except ValueError:
            cells.append(c)
    return types.FunctionType(fn.__code__, fn.__globals__, fn.__name__, fn.__defaults__, tuple(cells))


class Tok:
    __slots__ = ('w', 'r')

    def __init__(self):
        self.w = None
        self.r = {}


class Sched:
    def __init__(self, nc):
        self.nc = nc
        self.sem = {e: nc.alloc_semaphore('s_' + e) for e in ENGS}
        self.cnt = {e: 0 for e in ENGS}
        self.known = {e: {} for e in ENGS}
        self.prog = {e: [] for e in ENGS}
        self.snap = {}
        self.dsem = [nc.alloc_semaphore('d%d' % i) for i in range(NDS)]
        self.dcnt = [0] * NDS
        self.dnext = 0
        self.dnext_sw = 0

    def _semh(self, k):
        return self.sem[k] if isinstance(k, str) else self.dsem[k[1]]

    def _gather(self, engine, reads, writes, extra=()):
        need = {}
        kn = self.known[engine]

        def add(ev):
            if ev is None:
                return
            k, v = ev
            if k == engine and (engine == 'pe' or not SAME_ENG_SYNC):
                return
            if kn.get(k, 0) >= v:
                return
            if need.get(k, 0) < v:
                need[k] = v
        for t in reads:
            add(t.w)
        for t in writes:
            add(t.w)
            for k, v in t.r.items():
                add((k, v))
        for ev in extra:
            add(ev)
        return need

    def _apply_waits(self, engine, need):
        kn = self.known[engine]
        waits = []
        for k, v in need.items():
            if kn.get(k, 0) >= v:
                continue
            waits.append((self._semh(k), v))
            kn[k] = v
            sn = self.snap.get((k, v))
            if sn:
                for k2, v2 in sn.items():
                    if kn.get(k2, 0) < v2:
                        kn[k2] = v2
        return waits

    def _record(self, ev, reads, writes):
        k, v = ev
        for t in reads:
            if t.r.get(k, 0) < v:
                t.r[k] = v
        for t in writes:
            t.w = ev
            t.r = {}

    def op(self, engine, fn, reads=(), writes=(), inc=True):
        fn = _freeze(fn)
        reads = [getattr(t, 'tok', t) for t in reads]
        writes = [getattr(t, 'tok', t) for t in writes]
        need = self._gather(engine, reads, writes)
        waits = self._apply_waits(engine, need)
        ev = (engine, self.cnt[engine] + 1)
        sem = self.sem[engine]
        if inc:
            self.cnt[engine] += 1
            self.snap[ev] = {k: v for k, v in self.known[engine].items() if isinstance(k, str)}

        def emit(e, waits=waits, fn=fn, inc=inc, sem=sem):
            for s, v in waits:
                e.wait_ge(s, v)
            ins = fn(e)
            if inc:
                ins.then_inc(sem, 1)
        self.prog[engine].append(emit)
        self._record(ev, reads, writes)
        return ev

    def dma(self, engine, out, in_, reads=(), writes=(), custom=None):
        custom = _freeze(custom)
        reads = [getattr(t, 'tok', t) for t in reads]
        writes = [getattr(t, 'tok', t) for t in writes]
        half = NDS // 2
        if engine == 'pool':
            i = self.dnext_sw
            self.dnext_sw = (self.dnext_sw + 1) % half
        else:
            i = half + self.dnext
            self.dnext = (self.dnext + 1) % half
        extra = []
        if self.dcnt[i] > 0:
            extra.append((('d', i), self.dcnt[i]))
        need = self._gather(engine, reads, writes, extra)
        waits = self._apply_waits(engine, need)
        self.dcnt[i] += 16
        ev = (('d', i), self.dcnt[i])
        sem = self.dsem[i]

        def emit(e, waits=waits, sem=sem):
            for s, v in waits:
                e.wait_ge(s, v)
            if custom is not None:
                ins = custom(e)
            else:
                ins = e.dma_start(out=out, in_=in_)
            ins.then_inc(sem, 16)
        self.prog[engine].append(emit)
        self._record(ev, reads, writes)
        return ev

    def barrier(self, only=None):
        if only is not None:
            evs = [(e, self.cnt[e]) for e in only if self.cnt[e] > 0]
        else:
            evs = [(e, self.cnt[e]) for e in ENGS if self.cnt[e] > 0]
            evs += [(('d', i), self.dcnt[i]) for i in range(NDS) if self.dcnt[i] > 0]
        for engine in ENGS:
            need = {}
            for k, v in evs:
                if k == engine:
                    continue
                if self.known[engine].get(k, 0) < v:
                    need[k] = v
            waits = self._apply_waits(engine, need)
            if waits:
                def emit(e, waits=waits):
                    for s, v in waits:
                        e.wait_ge(s, v)
                self.prog[engine].append(emit)

    def finish(self):
        self.barrier()
        nc = self.nc
        prog = self.prog
        with nc.allow_non_contiguous_dma(reason="small transposed constant loads"), nc.Block() as block:
            @block.sync
            def _(e):
                for f in prog['sp']:
                    f(e)

            @block.tensor
            def _(e):
                for f in prog['pe']:
                    f(e)

            @block.vector
            def _(e):
                for f in prog['dve']:
                    f(e)

            @block.scalar
            def _(e):
                for f in prog['act']:
                    f(e)

            @block.gpsimd
            def _(e):
                for f in prog['pool']:
                    f(e)


class Buf:
    def __init__(self, t):
        self.t = t
        self.tok = Tok()

    def __getitem__(self, idx):
        return self.t[idx]


class Ring:
    def __init__(self, bufs):
        self.bufs = bufs
        self.i = 0

    def next(self):
        b = self.bufs[self.i]
        self.i = (self.i + 1) % len(self.bufs)
        return b


class K:
    pass


def build_program():
    nc = bass.Bass("TRN2", target_bir_lowering=False)
    S = Sched(nc)
    k = K()
    k.nc = nc
    k.S = S
    cnt = [0]

    def din(name, shape, dt=F32):
        return nc.dram_tensor(name, list(shape), dt, kind="ExternalInput")

    def dout(name, shape, dt=F32):
        return nc.dram_tensor(name, list(shape), dt, kind="ExternalOutput")

    xp = din("xp", [S_FULL, D])
    xs = din("xs", [SB * ST, D])
    memp = din("memp", [256, D])
    pools = [din(n, [2560 * 8, 2048]) for n in ("pk_cmp", "pv_cmp", "pk_slc", "pv_slc")]
    ckw = din("ckw", [SB, 512, 128])
    cvw = din("cvw", [SB, 512, 128])
    sconv_in = din("sconv_in", [SB, 30, 512])
    cmk = din("cmk", [SB, 256, 512])
    cmv = din("cmv", [SB, 256, 512])
    ptab = din("ptab", [SB, NPAGE], I32)
    w_in = din("w_in", [D, IN_W])
    cpos = [din("cpos_k", [32, 64]), din("cpos_v", [32, 64])]
    w_c1 = [din("w_ck1", [2048, 128]), din("w_cv1", [2048, 128])]
    w_c2 = [din("w_ck2", [128, 64]), din("w_cv2", [128, 64])]
    w_nsa_o = din("w_nsa_o", [512, D])
    w_dw = din("w_dw", [31, 512])
    cvec = din("cvec", [128, 12])
    w_pw = din("w_pw", [512, D])
    w_out = din("w_out", [D, D])
    w_xq = din("w_xq", [D, 512])
    w_xk = din("w_xk", [D, 512])
    w_xv = din("w_xv", [D, 512])
    w_xo = din("w_xo", [512, D])
    w_up = din("w_up", [D, 4096])
    w_down = din("w_down", [4096, D])
    gvec = din("gvec", [128, 5, D])
    c_ident = din("c_ident", [128, 128])
    c_E = din("c_E", [64, 4096])
    c_tri = din("c_tri", [128, 2, 128])
    c_cs = din("c_cs", [128, NT_ALL, 16])
    c_valid = din("c_valid", [128, NT_ALL])
    c_A = din("c_A", [128, NT_OWN, 64])
    c_B = din("c_B", [128, NT_OWN, 64])
    c_cmask = din("c_cmask", [128, 2, HALF])
    c_M = din("c_M", [128, 2, 64])
    c_pidx = din("c_pidx", [128, 1])
    c_c8 = din("c_c8", [NPAGE, 8])
    s_cs = din("s_cs", [ST, 16])
    s_A = din("s_A", [ST, 192])
    s_B = din("s_B", [ST, 192])
    s_cmask = din("s_cmask", [128, 4])
    s_M = din("s_M", [128, 4, 192])
    s_wmask = din("s_wmask", [128, 5, ST])
    s_lmask = din("s_lmask", [128, ST])
    s_valid = din("s_valid", [128, 66])
    s_wvalid = din("s_wvalid", [128, 5])

    yp = dout("yp", [HALF, D])
    okv = dout("okv", [6, HALF, 128])
    pconv = dout("pconv", [30, 512])
    pmk = dout("pmk", [256, 512])
    pmv = dout("pmv", [256, 512])
    ys = dout("ys", [SB * ST, D])
    skv = dout("skv", [6, SB * ST, 128])
    swk = dout("swk", [SB, 512, 128])
    swv = dout("swv", [SB, 512, 128])
    sconv = dout("sconv", [SB, 30, 512])

    x1s = nc.dram_tensor("x1s", [HALF + SB * ST, D], F32, kind="Internal")
    oscr = nc.dram_tensor("oscr", [HALF + SB * ST, 512], F32, kind="Internal")
    gscr = [[nc.dram_tensor("gscr_%d_%d" % (b, X), [PAST, 128], F32, kind="Internal") for X in range(4)] for b in range(SB)]
    gtok = [[Tok() for X in range(4)] for b in range(SB)]

    glob = ExitStack()

    def sb(stack, name, shape, dt=F32):
        cnt[0] += 1
        return Buf(stack.enter_context(nc.sbuf_tensor("%s_%d" % (name, cnt[0]), list(shape), dt)))

    def ps(stack, name, shape, dt=F32):
        cnt[0] += 1
        return Buf(stack.enter_context(nc.psum_tensor("%s_%d" % (name, cnt[0]), list(shape), dt)))

    def V(fn, r, w):
        S.op('dve', fn, reads=r, writes=w)

    def A(fn, r, w):
        S.op('act', fn, reads=r, writes=w)

    def G(fn, r, w):
        S.op('pool', fn, reads=r, writes=w)

    def PE(fn, r, w, inc=True):
        S.op('pe', fn, reads=r, writes=w, inc=inc)

    def LD(out, in_, w, r=(), q='sp'):
        S.dma(q, out, in_, reads=r, writes=w)

    def LDC(out, in_, w):
        S.dma('pool', out, in_, writes=w)

    def ST_(out, in_, r, w=(), q='sp'):
        S.dma(q, out, in_, reads=r, writes=w)

    pT = Ring([ps(glob, "pT", [128, 8, 128], BF16) for _ in range(2)])
    pmm = Ring([ps(glob, "pmm", [128, 512], F32) for _ in range(3)])
    pacc = Ring([ps(glob, "pacc", [128, 512], F32) for _ in range(2)])
    paccT = Ring([ps(glob, "paccT", [128, 512], F32) for _ in range(1)])

    ident = sb(glob, "ident", [128, 128], BF16)
    identf = sb(glob, "identf", [128, 128], F32)
    LDC(ident[:, :], c_ident[:, :], [ident])
    LD(identf[:, :], c_ident[:, :], [identf])

    def mk_masks(stack):
        tri = sb(stack, "tri", [128, 2, 128], BF16)
        LDC(tri[:, :, :], c_tri[:, :, :], [tri])
        tri4 = sb(stack, "tri4", [128, 2, 4, 128], BF16)
        zeros_b = sb(stack, "zeros_b", [128, 512], BF16)
        S.op('pool', lambda e: e.memset(zeros_b[:, :], 0.0), writes=[zeros_b])
        for m_ in range(2):
            S.op('dve', lambda e, m_=m_: e.tensor_copy(out=tri4[:, m_, :, :], in_=tri[:, m_, :].unsqueeze(1).to_broadcast([128, 4, 128])), reads=[tri], writes=[tri4])
        return tri4, zeros_b
    gvd = {}

    def load_g(stack, idxs):
        for gi in idxs:
            t = sb(stack, "gv%d" % gi, [128, D], F32)
            LD(t[:, :], gvec[:, gi, :], [t])
            gvd[gi] = t

    wk = ExitStack()
    xt_r = Ring([sb(wk, "xt", [128, D], F32) for _ in range(2)])
    h_r = Ring([sb(wk, "h", [128, D], BF16) for _ in range(2)])
    hT_r = Ring([sb(wk, "hT", [128, 8, 128], BF16) for _ in range(2)])
    junk = sb(wk, "junk", [128, D], F32)
    st_r = Ring([sb(wk, "st", [128, 8], F32) for _ in range(4)])

    def load_w(stack, name, w_ap, K_, N_, c0=0, c1=None, defer=False):
        c1 = N_ if c1 is None else c1
        kc = K_ // 128
        t = sb(stack, name, [128, kc, c1 - c0], BF16)
        src = w_ap[:, c0:c1].rearrange("(kc p) n -> p kc n", p=128)
        t.ktoks = [Tok() for _ in range(kc)]

        def issue():
            for i in range(kc):
                LDC(t[:, i, :], src[:, i, :], [t.ktoks[i]])
        if defer:
            t.issue = issue
        else:
            issue()
        return t

    def rmsnorm(xt, nt, gi, out):
        st = st_r.next()
        gbuf = gvd[gi]
        A(lambda e: e.activation(out=junk[0:nt, :], in_=xt[0:nt, :], func=AF.Square, accum_out=st[0:nt, 0:1]), [xt], [junk, st])
        V(lambda e: e.tensor_scalar(out=st[0:nt, 1:2], in0=st[0:nt, 0:1], scalar1=1.0 / D, scalar2=EPS, op0=ALU.mult, op1=ALU.add), [st], [st])
        A(lambda e: e.activation(out=st[0:nt, 2:3], in_=st[0:nt, 1:2], func=AF.Sqrt), [st], [st])
        V(lambda e: e.reciprocal(out=st[0:nt, 3:4], in_=st[0:nt, 2:3]), [st], [st])
        V(lambda e: e.scalar_tensor_tensor(out=out[0:nt, :], in0=xt[0:nt, :], scalar=st[0:nt, 3:4], in1=gbuf[0:nt, :], op0=ALU.mult, op1=ALU.mult), [xt, st, gbuf], [out])

    def transpose_bf(src, nt, nchunk, dst, width=128, col0=0):
        for c0 in range(0, nchunk, 8):
            n = min(8, nchunk - c0)
            p = pT.next()
            for c in range(n):
                PE(lambda e, c=c: e.transpose(out=p[0:width, c, 0:nt], in_=src[0:nt, (c0 + c) * width:(c0 + c + 1) * width], identity=ident[0:nt, 0:nt]), [src, ident], [p])
            A(lambda e, n=n, c0=c0: e.copy(out=dst[0:width, c0:c0 + n, col0:col0 + nt], in_=p[0:width, 0:n, 0:nt]), [p], [dst])

    def linear(hT, nt, kc, W, c0, c1, consume):
        g0 = c0
        while g0 < c1:
            gn = min(512, c1 - g0)
            p = pmm.next()
            for i in range(kc):
                PE(lambda e, i=i, g0=g0, gn=gn, p=p: e.matmul(p[0:nt, 0:gn], lhsT=hT[:, i, 0:nt], rhs=W[:, i, g0:g0 + gn], start=(i == 0), stop=(i == kc - 1)), [hT, W.ktoks[i] if hasattr(W, 'ktoks') else W], [p], inc=(i == kc - 1))
            consume(p, g0, gn)
            g0 += gn

    def rope(z, nt, nh_view_fn, cs, tmp):
        x1 = nh_view_fn(0, 8)
        x2 = nh_view_fn(8, 16)
        shp = list(x1.shape)
        n = 1
        for s_ in shp[1:-1]:
            n *= s_

        def bc(ap):
            a = ap
            for _ in range(len(shp) - 2):
                a = a.unsqueeze(1)
            return a.to_broadcast(shp)
        cosb = bc(cs[0](0, 8))
        sinb = bc(cs[0](8, 16))

        def tv(i):
            v = tmp[0:nt, i, 0:n * 8]
            if len(shp) == 3:
                return v.rearrange("p (a d) -> p a d", d=8)
            return v.rearrange("p (a b d) -> p a b d", b=shp[2], d=8)
        rd = [z, cs[1], tmp]
        V(lambda e: e.tensor_tensor(out=tv(0), in0=x1, in1=cosb, op=ALU.mult), rd, [tmp])
        V(lambda e: e.tensor_tensor(out=tv(1), in0=x2, in1=sinb, op=ALU.mult), rd, [tmp])
        V(lambda e: e.tensor_tensor(out=tv(2), in0=x2, in1=cosb, op=ALU.mult), rd, [tmp])
        V(lambda e: e.tensor_tensor(out=tv(3), in0=x1, in1=sinb, op=ALU.mult), rd, [tmp])
        V(lambda e: e.tensor_tensor(out=x1, in0=tv(0), in1=tv(1), op=ALU.subtract), [tmp], [z])
        V(lambda e: e.tensor_tensor(out=x2, in0=tv(2), in1=tv(3), op=ALU.add), [tmp], [z])

    def attn_branch(ctx, nq, acc, hh, Qrows, kts, lhs_fn, v_fn, vw, mask_fn, extra_r, q_fn=None, heads=(0, 1, 2, 3), gstride=128, col0=0):
        Qa_, Pb_r_, accsb_r_ = ctx
        n_k = len(kts)
        nh = len(heads)
        g0 = heads[0]
        staged = {}

        def stage1(i):
            kt = kts[i]
            p = pmm.next()
            lhs, nk = lhs_fn(kt)
            rhs = q_fn(kt) if q_fn is not None else Qa_[0:Qrows, hh * 4:hh * 4 + 4, 0:nq]
            PE(lambda e, p=p, lhs=lhs, nk=nk, rhs=rhs: e.matmul(p[0:nk, 0:4 * nq].rearrange("k (g q) -> k g q", g=4), lhsT=lhs, rhs=rhs, start=True, stop=True), [Qa_] + extra_r, [p])
            Pb = Pb_r_.next()
            A(lambda e, p=p, Pb=Pb, nk=nk: e.activation(out=Pb[0:nk, :, 0:nq], in_=p[0:nk, 0:4 * nq].rearrange("k (g q) -> k g q", g=4), func=AF.Exp, scale=0.125), [p], [Pb])
            m = mask_fn(kt)
            if m is not None:
                mk_ap, mk_buf = m
                V(lambda e, Pb=Pb, nk=nk, mk_ap=mk_ap: e.tensor_tensor(out=Pb[0:nk, :, 0:nq], in0=Pb[0:nk, :, 0:nq], in1=mk_ap, op=ALU.mult), [Pb, mk_buf], [Pb])
            staged[i] = (Pb, nk)

        def stage2(i):
            kt = kts[i]
            Pb, nk = staged.pop(i)
            vap = v_fn(kt)
            PE(lambda e, Pb=Pb, nk=nk, vap=vap, i=i: e.matmul(accT[0:vw, 0:nh * nq].rearrange("v (g q) -> v g q", g=nh), lhsT=vap, rhs=Pb[0:nk, g0:g0 + nh, 0:nq], start=(i == 0), stop=(i == n_k - 1)), [Pb] + extra_r, [accT])

        if n_k <= 5:
            for i in range(n_k):
                stage1(i)
            tiles = [staged.pop(i) for i in range(n_k)]
            for gi, g in enumerate(heads):
                for i, (Pb, nk) in enumerate(tiles):
                    vap = v_fn(kts[i])
                    PE(lambda e, g=g, gi=gi, Pb=Pb, nk=nk, vap=vap, i=i: e.matmul(acc[0:nq, gi * gstride + col0:gi * gstride + col0 + vw], lhsT=Pb[0:nk, g, 0:nq], rhs=vap, start=(i == 0), stop=(i == n_k - 1)), [Pb] + extra_r, [acc], inc=(i == n_k - 1))
            return
        accT = paccT.next()
        LOOK = 2
        for i in range(min(LOOK, n_k)):
            stage1(i)
        for i in range(n_k):
            if i + LOOK < n_k:
                stage1(i + LOOK)
            stage2(i)
        asb = accsb_r_.next()
        A(lambda e: e.copy(out=asb[0:vw, 0:nh * nq], in_=accT[0:vw, 0:nh * nq]), [accT], [asb])
        for gi in range(nh):
            PE(lambda e, gi=gi: e.transpose(out=acc[0:nq, gi * gstride + col0:gi * gstride + col0 + vw], in_=asb[0:vw, gi * nq:(gi + 1) * nq], identity=identf[0:vw, 0:vw]), [asb, identf], [acc], inc=(gi == nh - 1))

    sper = ExitStack()
    cs_s = sb(sper, "cs_s", [ST, 16], F32)
    LD(cs_s[:, :], s_cs[:, :], [cs_s])
    KcT_s = [sb(sper, "KcTs%d" % b, [64, 2, 512], BF16) for b in range(SB)]
    Vca_s = [sb(sper, "Vcas%d" % b, [128, 4, 2, 258], BF16) for b in range(SB)]
    Ms = sb(sper, "Ms", [128, 4, 192], BF16)
    LDC(Ms[:, :, :], s_M[:, :, :], [Ms])
    pidx_t = sb(sper, "pidx_t", [NPAGE, SB], I32)
    LD(pidx_t[:, :], ptab.rearrange("b j -> j b"), [pidx_t])
    pidx_f = sb(sper, "pidx_f", [NPAGE, SB], F32)
    c8 = sb(sper, "c8", [NPAGE, 8], F32)
    pidx8 = sb(sper, "pidx8", [NPAGE, SB, 8], I32)
    LD(c8[:, :], c_c8[:, :], [c8])
    V(lambda e: e.tensor_copy(out=pidx_f[:, :], in_=pidx_t[:, :]), [pidx_t], [pidx_f])
    V(lambda e: e.tensor_scalar(out=pidx_f[:, :], in0=pidx_f[:, :], scalar1=8.0, scalar2=None, op0=ALU.mult), [pidx_f], [pidx_f])
    V(lambda e: e.tensor_tensor(out=pidx8[:, :, :], in0=pidx_f[:, :].unsqueeze(2).to_broadcast([NPAGE, SB, 8]), in1=c8[:, :].unsqueeze(1).to_broadcast([NPAGE, SB, 8]), op=ALU.add), [pidx_f, c8], [pidx8])

    def gather_page(b, j, X, stg):
        S.dma('sp', stg[:, :], gscr[b][X][j * 128:(j + 1) * 128, :], reads=[gtok[b][X]], writes=[stg])

    def pages_to_T(dst, dst_tok, stb4, n, pos0):
        p = pT.next()
        for jj in range(n):
            for hh in range(2):
                PE(lambda e, jj=jj, hh=hh, p=p: e.transpose(out=p[0:64, jj * 2 + hh, :], in_=stb4[:, jj, hh * 64:(hh + 1) * 64], identity=ident[:, :]), [stb4, ident], [p])
        for hh in range(2):
            A(lambda e, hh=hh, p=p: e.copy(out=dst(hh, pos0, pos0 + n * 128).rearrange("d (j p) -> d j p", p=128), in_=p[0:64, 0:2 * n, :].rearrange("d (j h) p -> d j h p", h=2)[:, :, hh, :]), [p], [dst_tok(hh) if callable(dst_tok) else dst_tok])

    def pages_to_T_full(dst, stb4, n, pos0):
        p = pT.next()
        for jj in range(n):
            PE(lambda e, jj=jj, p=p: e.transpose(out=p[:, jj, :], in_=stb4[:, jj, :], identity=ident[:, :]), [stb4, ident], [p])
        A(lambda e, p=p: e.copy(out=dst[:, pos0:pos0 + n * 128].rearrange("d (j p) -> d j p", p=128), in_=p[:, 0:n, :]), [p], [dst])

    def stream_pages(rings, groups, dstT=None, dstV=None, full=False):
        stg_r_, stb_r_ = rings
        for gi, (fill, n, j0) in enumerate(groups):
            stg4 = stg_r_.next()
            fill(stg4)
            if dstT is not None:
                stb4 = stb_r_.next()
                if gi % 2 == 0:
                    V(lambda e, stb4=stb4, stg4=stg4, n=n: e.tensor_copy(out=stb4[:, 0:n, :], in_=stg4[:, 0:n, :]), [stg4], [stb4])
                else:
                    A(lambda e, stb4=stb4, stg4=stg4, n=n: e.copy(out=stb4[:, 0:n, :], in_=stg4[:, 0:n, :]), [stg4], [stb4])
                if full:
                    pages_to_T_full(dstT[1], stb4, n, j0 * 128)
                else:
                    pages_to_T(dstT[0], dstT[1], stb4, n, j0 * 128)
            else:
                ap, tokb = dstV(j0, n)
                src = stg4[:, 0:n, :].rearrange("p j (h d) -> p j h d", h=2)
                if gi % 2 == 0:
                    V(lambda e, ap=ap, src=src: e.tensor_copy(out=ap, in_=src), [stg4], [tokb])
                else:
                    G(lambda e, ap=ap, src=src: e.tensor_copy(out=ap, in_=src), [stg4], [tokb])

    def pool_groups(b, X, r0):
        gs = []
        for j0 in range(0, NPAGE, 4):
            gs.append((lambda stg4, j0=j0: S.dma('sp', stg4[:, :, :], gscr[b][X][j0 * 128:(j0 + 4) * 128, :].rearrange("(j p) d -> p j d", p=128), reads=[gtok[b][X]], writes=[stg4]), 4, j0))

        def newtok(stg4):
            G(lambda e: e.memset(stg4[:, 0, :], 0.0), [], [stg4])
            LD(stg4[0:ST, 0, :], skv[X, r0:r0 + ST, :], [stg4])
        gs.append((newtok, 1, NPAGE))
        return gs

    def win_groups(b, cache, X, r0):
        def newtok(stg4):
            G(lambda e: e.memset(stg4[:, 0, :], 0.0), [], [stg4])
            LD(stg4[0:ST, 0, :], skv[X, r0:r0 + ST, :], [stg4])
        return [(lambda stg4: LD(stg4[:, :, :], cache[b, :, :].rearrange("(j p) d -> p j d", p=128), [stg4]), 4, 0), (newtok, 1, 4)]

    def emit_gathers(jobs, gst_ring, store_q):
        pend = []

        def store(b0, X0, c0, g0):
            S.dma(store_q, gscr[b0][X0].rearrange("(j r) d -> j (r d)", r=128)[:, c0 * 2048:(c0 + 1) * 2048], g0[:, :], reads=[g0], writes=[gtok[b0][X0]])
        for (b, X, c) in jobs:
            g = gst_ring.next()
            S.dma('pool', None, None, reads=[pidx8], writes=[g],
                  custom=lambda e, g=g, b=b, X=X, c=c: e.indirect_dma_start(
                      out=g[:, :], out_offset=None, in_=pools[X][:, :],
                      in_offset=bass.IndirectOffsetOnAxis(ap=pidx8[:, b, c:c + 1], axis=0)))
            pend.append((b, X, c, g))
            if len(pend) == 2:
                store(*pend.pop(0))
        for job in pend:
            store(*job)

    stores = ExitStack()
    KS = [sb(stores, "KS%d" % i, [128, S_FULL], BF16) for i in range(2)]
    KW = sb(stores, "KW", [64, 2, S_FULL], BF16)
    VS = sb(stores, "VS", [128, NT_ALL, 2, 66], BF16)
    VW = sb(stores, "VW", [128, NT_ALL, 2, 66], BF16)
    KcT = sb(stores, "KcT", [64, 2, 256], BF16)
    Vca = sb(stores, "Vca", [128, 2, 2, 130], BF16)
    cs_p = sb(stores, "cs_p", [128, NT_ALL, 16], F32)
    valid = sb(stores, "valid", [128, NT_ALL], F32)
    LD(cs_p[:, :, :], c_cs[:, :, :], [cs_p])
    LD(valid[:, :], c_valid[:, :], [valid])
    for i in range(2):
        LDC(KS[i][64:128, :], c_E[:, :], [KS[i]])
        V(lambda e, i=i: e.tensor_copy(out=VS[:, :, i, 64], in_=valid[:, :]), [valid], [VS])
        V(lambda e, i=i: e.tensor_copy(out=VW[:, :, i, 64], in_=valid[:, :]), [valid], [VW])
    Mc = sb(stores, "Mc", [128, 2, 64], BF16)
    LDC(Mc[:, :, :], c_M[:, :, :], [Mc])

    cw = ExitStack()
    w1 = []
    w2 = []
    cb = sb(cw, "cbias", [128, 2, 2], F32)
    for kv in range(2):
        t = sb(cw, "w1_%d" % kv, [128, 32, 128], BF16)
        LDC(t[0:64, :, :], w_c1[kv].rearrange("(j d) n -> d j n", d=64), [t])
        LDC(t[64:128, :, :], w_c1[kv].rearrange("(j d) n -> d j n", d=64), [t])
        w1.append(t)
        t2 = sb(cw, "w2_%d" % kv, [128, 64], BF16)
        LDC(t2[:, :], w_c2[kv][:, :], [t2])
        w2.append(t2)
    peT = sb(cw, "peT", [64, 2, 32], BF16)
    for kv in range(2):
        LDC(peT[:, kv, :], cpos[kv].rearrange("j d -> d j"), [peT])
    for kv in range(2):
        p = pmm.next()
        for j in range(32):
            PE(lambda e, j=j, kv=kv, p=p: e.matmul(p[:, 0:1], lhsT=w1[kv][0:64, j, :], rhs=peT[:, kv, j:j + 1], start=(j == 0), stop=(j == 31)), [w1[kv], peT], [p], inc=(j == 31))
        V(lambda e, kv=kv, p=p: e.tensor_copy(out=cb[:, kv, 0:1], in_=p[:, 0:1]), [p], [cb])
    gtmp = sb(cw, "gtmp", [128, 3, 512], F32)
    hid = sb(cw, "hid", [128, 512], BF16)

    cstk = ExitStack()
    KC = [sb(cstk, "KCT%d" % i, [128, S_FULL], BF16) for i in range(2)]

    with ExitStack() as ph:
        load_g(ph, [0])
        wkv = load_w(ph, "wkv", w_in, D, IN_W, 512, 1280)
        gst_r = Ring([sb(ph, "gst", [NPAGE, 2048], F32) for _ in range(3)])
        emit_gathers([(b, X, c) for X in (0, 1) for b in range(SB) for c in range(8)], gst_r, 'pool')
        zkv_r = Ring([sb(ph, "zkv", [128, 768], F32) for _ in range(2)])
        zb_r = Ring([sb(ph, "zb", [128, 768], BF16) for _ in range(2)])
        rtmp = sb(ph, "rtmp", [128, 4, 64], F32)

        def a1_pre(src_ap, nt):
            xt = xt_r.next()
            LD(xt[0:nt, :], src_ap, [xt])
            h = h_r.next()
            rmsnorm(xt, nt, 0, h)
            hT = hT_r.next()
            transpose_bf(h, nt, 8, hT)
            return hT

        def a1_post(hT, nt, cs_fn, cs_buf, zkv):
            def cons(p, g0, gn):
                A(lambda e: e.copy(out=zkv[0:nt, g0:g0 + gn], in_=p[0:nt, 0:gn]), [p], [zkv])
            linear(hT, nt, 8, wkv, 0, 768, cons)
            zv = zkv[0:nt, :].rearrange("p (s kv h d) -> p s kv h d", s=3, kv=2, h=2)
            rope(zkv, nt, lambda lo, hi: zv[:, :, 0, :, lo:hi], (cs_fn, cs_buf), rtmp)

        def a1_tile(src_ap, nt, cs_fn, cs_buf, zkv):
            a1_post(a1_pre(src_ap, nt), nt, cs_fn, cs_buf, zkv)

        hT_next = a1_pre(xp[0:128, :], 128)
        for ti in range(NT_ALL):
            zkv = zkv_r.next()
            hT_cur = hT_next
            if ti + 1 < NT_ALL:
                hT_next = a1_pre(xp[(ti + 1) * 128:(ti + 2) * 128, :], 128)
            a1_post(hT_cur, 128, lambda lo, hi, ti=ti: cs_p[:, ti, lo:hi], cs_p, zkv)
            if ti >= NT_OWN:
                t0 = (ti - NT_OWN) * 128
                for s6 in range(6):
                    ST_(okv[s6, t0:t0 + 128, :], zkv[:, s6 * 128:(s6 + 1) * 128], [zkv], q='sp')
            zb = zb_r.next()
            V(lambda e, zb=zb, zkv=zkv: e.tensor_copy(out=zb[:, :], in_=zkv[:, :]), [zkv], [zb])
            p = pT.next()
            for j, s_ in enumerate((0, 1)):
                PE(lambda e, j=j, s_=s_, p=p, zb=zb: e.transpose(out=p[:, j, :], in_=zb[:, s_ * 128:(s_ + 1) * 128], identity=ident[:, :]), [zb, ident], [p])
            for j, s_ in enumerate((2, 4)):
                for hh in range(2):
                    PE(lambda e, j=j, s_=s_, hh=hh, p=p, zb=zb: e.transpose(out=p[0:64, 2 + j * 2 + hh, :], in_=zb[:, s_ * 128 + hh * 64:s_ * 128 + hh * 64 + 64], identity=ident[:, :]), [zb, ident], [p])
            sl = slice(ti * 128, (ti + 1) * 128)
            A(lambda e, p=p, sl=sl: e.copy(out=KC[0][:, sl], in_=p[:, 0, :]), [p], [KC[0]])
            A(lambda e, p=p, sl=sl: e.copy(out=KC[1][:, sl], in_=p[:, 1, :]), [p], [KC[1]])
            for hh in range(2):
                A(lambda e, p=p, sl=sl, hh=hh: e.copy(out=KS[hh][0:64, sl], in_=p[0:64, 2 + hh, :]), [p], [KS[hh]])
            A(lambda e, p=p, sl=sl: e.copy(out=KW[:, :, sl], in_=p[0:64, 4:6, :]), [p], [KW])
            V(lambda e, zb=zb, ti=ti: e.tensor_copy(out=VS[:, ti, :, 0:64], in_=zb[:, 384:512].rearrange("p (h d) -> p h d", h=2)), [zb], [VS])
            V(lambda e, zb=zb, ti=ti: e.tensor_copy(out=VW[:, ti, :, 0:64], in_=zb[:, 640:768].rearrange("p (h d) -> p h d", h=2)), [zb], [VW])

        NS = SB * ST
        cs16 = sb(ph, "cs16", [NS, 16], F32)
        for b in range(SB):
            LD(cs16[b * ST:(b + 1) * ST, :], s_cs[:, :], [cs16])
        zkv = zkv_r.next()
        a1_tile(xs[:, :], NS, lambda lo, hi: cs16[:, lo:hi], cs16, zkv)
        for s6 in range(6):
            ST_(skv[s6, :, :], zkv[0:NS, s6 * 128:(s6 + 1) * 128], [zkv], q='sp')
        for b in range(SB):
            ST_(swk[b, 0:512 - ST, :], ckw[b, ST:512, :], [], q='sp')
            ST_(swv[b, 0:512 - ST, :], cvw[b, ST:512, :], [], q='sp')
            ST_(swk[b, 512 - ST:512, :], zkv[b * ST:(b + 1) * ST, 512:640], [zkv], q='sp')
            ST_(swv[b, 512 - ST:512, :], zkv[b * ST:(b + 1) * ST, 640:768], [zkv], q='sp')
    S.barrier()

    def gelu_to(out_bf, p, n, bias, tmp):
        x = tmp[:, 0, 0:n]
        A(lambda e: e.activation(out=x, in_=p[:, 0:n], func=AF.Identity, bias=bias), [p], [tmp])
        V(lambda e: e.tensor_tensor(out=tmp[:, 1, 0:n], in0=x, in1=x, op=ALU.mult), [tmp], [tmp])
        V(lambda e: e.tensor_scalar(out=tmp[:, 1, 0:n], in0=tmp[:, 1, 0:n], scalar1=0.044715 * 1.5957691216, scalar2=1.5957691216, op0=ALU.mult, op1=ALU.add), [tmp], [tmp])
        V(lambda e: e.tensor_tensor(out=tmp[:, 1, 0:n], in0=tmp[:, 1, 0:n], in1=x, op=ALU.mult), [tmp], [tmp])
        A(lambda e: e.activation(out=tmp[:, 2, 0:n], in_=tmp[:, 1, 0:n], func=AF.Sigmoid), [tmp], [tmp])
        V(lambda e: e.tensor_tensor(out=out_bf, in0=tmp[:, 2, 0:n], in1=x, op=ALU.mult), [tmp], [out_bf_owner[0]])

    out_bf_owner = [None]

    def compress(KCk, KCv, nblk, KcT_out, Vca_out, Msb, nsel):
        nct = (nblk + 127) // 128
        for hh in range(2):
            for kv, KCx in ((0, KCk), (1, KCv)):
                p = pmm.next()
                for j in range(32):
                    PE(lambda e, j=j, kv=kv, hh=hh, p=p, KCx=KCx: e.matmul(p[:, 0:nblk], lhsT=w1[kv][hh * 64:(hh + 1) * 64, j, :], rhs=KCx[hh * 64:(hh + 1) * 64, j:j + 16 * (nblk - 1) + 1:16], start=(j == 0), stop=(j == 31)), [w1[kv], KCx], [p], inc=(j == 31))
                out_bf_owner[0] = hid
                gelu_to(hid[:, 0:nblk], p, nblk, cb[:, kv, 0:1], gtmp)
                if kv == 0:
                    p2 = pmm.next()
                    PE(lambda e, p2=p2: e.matmul(p2[0:64, 0:nblk], lhsT=w2[0][:, :], rhs=hid[:, 0:nblk], start=True, stop=True), [w2[0], hid], [p2])
                    A(lambda e, p2=p2, hh=hh: e.copy(out=KcT_out[:, hh, 0:nblk], in_=p2[0:64, 0:nblk]), [p2], [KcT_out])
                else:
                    for ct in range(nct):
                        n = min(128, nblk - ct * 128)
                        p2 = pmm.next()
                        PE(lambda e, p2=p2, ct=ct, n=n: e.matmul(p2[0:n, 0:64], lhsT=hid[:, ct * 128:ct * 128 + n], rhs=w2[1][:, :], start=True, stop=True), [w2[1], hid], [p2])
                        A(lambda e, p2=p2, ct=ct, n=n, hh=hh: e.copy(out=Vca_out[0:n, ct, hh, 0:64], in_=p2[0:n, 0:64]), [p2], [Vca_out])
        for hh in range(2):
            V(lambda e, hh=hh: e.tensor_copy(out=Vca_out[:, :, hh, 64:64 + nsel], in_=Msb[:, :, :]), [Msb], [Vca_out])
            G(lambda e, hh=hh: e.memset(Vca_out[:, :, hh, 64 + nsel:64 + nsel + 1], 1.0), [], [Vca_out])

    G(lambda e: e.memset(KcT[:, :, :], 0.0), [], [KcT])
    G(lambda e: e.memset(Vca[:, :, :, :], 0.0), [], [Vca])
    compress(KC[0], KC[1], 255, KcT, Vca, Mc, 64)
    S.barrier()
    cstk.close()

    with ExitStack() as ph:
        KCs = [sb(ph, "KCs%d" % i, [128, 8320], BF16) for i in range(2)]
        rings = (Ring([sb(ph, "stg4", [128, 4, 128], F32) for _ in range(3)]), Ring([sb(ph, "stb4", [128, 4, 128], BF16) for _ in range(3)]))
        for b in range(SB):
            for X in range(2):
                stream_pages(rings, pool_groups(b, X, b * ST), dstT=(None, KCs[X]), full=True)
            G(lambda e, b=b: e.memset(KcT_s[b][:, :, :], 0.0), [], [KcT_s[b]])
            G(lambda e, b=b: e.memset(Vca_s[b][:, :, :, :], 0.0), [], [Vca_s[b]])
            compress(KCs[0], KCs[1], 511, KcT_s[b], Vca_s[b], Ms, 192)
    S.barrier()
    cw.close()

    with ExitStack() as ph:
        load_g(ph, [0])
        tri4, zeros_b = mk_masks(ph)
        wq = load_w(ph, "wq", w_in, D, IN_W, 0, 512)
        wg = load_w(ph, "wg", w_in, D, IN_W, 1280, 1304)
        cm_r = Ring([sb(ph, "cmask", [128, 2, 128], BF16) for _ in range(2)])
        cm4_r = Ring([sb(ph, "cmask4", [128, 2, 4, 128], BF16) for _ in range(2)])
        AB_r = Ring([sb(ph, "AB", [128, 2, 64], F32) for _ in range(2)])
        zq = sb(ph, "zq", [128, 512], F32)
        zqb = sb(ph, "zqb", [128, 512], BF16)
        Qa = sb(ph, "Qa", [128, 8, 128], BF16)
        gsig = sb(ph, "gsig", [128, 24], F32)
        rtmp = sb(ph, "rtmpb", [128, 4, 64], F32)
        Pb_r = Ring([sb(ph, "Pb", [128, 4, 128], BF16) for _ in range(8)])
        acc_c = sb(ph, "acc_c", [128, 4, 130], F32)
        impt = sb(ph, "impt", [128, 4, 64], F32)
        m8 = sb(ph, "m8", [128, 16], F32)
        nbw = sb(ph, "nbw", [128, 128], BF16)
        G(lambda e: e.memset(nbw[:, :], 0.0), [], [nbw])
        osb_r = Ring([sb(ph, "osb", [128, 512], F32) for _ in range(2)])
        otmp = sb(ph, "otmp", [128, 4, 64], F32)
        cst = sb(ph, "cst", [128, 16], F32)
        cst2 = [sb(ph, "cst2_%d" % i, [128, 16], F32) for i in range(2)]
        impt2 = [sb(ph, "impt2_%d" % i, [128, 4, 64], F32) for i in range(2)]
        m82 = [sb(ph, "m82_%d" % i, [128, 16], F32) for i in range(2)]
        nbw2 = [sb(ph, "nbw2_%d" % i, [128, 128], BF16) for i in range(2)]
        acc_c2 = [sb(ph, "acc_c2_%d" % i, [128, 4, 130], F32) for i in range(2)]
        for i in range(2):
            G(lambda e, i=i: e.memset(nbw2[i][:, :], 0.0), [], [nbw2[i]])
        accsb_r = Ring([sb(ph, "accsb", [128, 512], F32) for _ in range(2)])
        actx = (Qa, Pb_r, accsb_r)
        gst2_r = Ring([sb(ph, "gst2", [NPAGE, 2048], F32) for _ in range(3)])
        slc_jobs = [(b, X, c) for X in (2, 3) for b in range(SB) for c in range(8)]

        for qt in range(NT_OWN):
            ti = NT_OWN + qt
            emit_gathers(slc_jobs[qt * 4:(qt + 1) * 4], gst2_r, 'pool')
            xt = xt_r.next()
            LD(xt[:, :], xp[ti * 128:(ti + 1) * 128, :], [xt])
            cm = cm_r.next()
            LDC(cm[:, :, :], c_cmask[:, :, qt * 128:(qt + 1) * 128], [cm])
            cm4 = cm4_r.next()
            for ct_ in range(2):
                V(lambda e, ct_=ct_, cm=cm, cm4=cm4: e.tensor_copy(out=cm4[:, ct_, :, :], in_=cm[:, ct_, :].unsqueeze(1).to_broadcast([128, 4, 128])), [cm], [cm4])
            AB = AB_r.next()
            LD(AB[:, 0, :], c_A[:, qt, :], [AB])
            LD(AB[:, 1, :], c_B[:, qt, :], [AB])
            h = h_r.next()
            rmsnorm(xt, 128, 0, h)
            hT = hT_r.next()
            transpose_bf(h, 128, 8, hT)

            def consq(p, g0, gn):
                A(lambda e: e.copy(out=zq[:, :], in_=p[:, 0:512]), [p], [zq])
            linear(hT, 128, 8, wq, 0, 512, consq)
            zqv = zq[:, :].rearrange("p (h d) -> p h d", h=8)
            rope(zq, 128, lambda lo, hi: zqv[:, :, lo:hi], (lambda lo, hi, ti=ti: cs_p[:, ti, lo:hi], cs_p), rtmp)
            V(lambda e: e.tensor_copy(out=zqb[:, :], in_=zq[:, :]), [zq], [zqb])
            transpose_bf(zqb, 128, 8, Qa, width=64)

            def consg(p, g0, gn):
                A(lambda e: e.activation(out=gsig[:, :], in_=p[:, 0:24], func=AF.Sigmoid), [p], [gsig])
            linear(hT, 128, 8, wg, 0, 24, consg)

            osb = osb_r.next()
            def gview_of(hh):
                return gsig[:, hh * 12:(hh + 1) * 12].rearrange("p (g t) -> p g t", t=3)

            def ov_of(hh):
                return osb[:, hh * 256:(hh + 1) * 256].rearrange("p (g d) -> p g d", g=4)

            def consume(hh, bi, accX):
                av = accX[:, :].rearrange("p (g n) -> p g n", g=4)
                cs_ = cst2[hh]
                gv_ = gview_of(hh)[:, :, bi]
                ov_ = ov_of(hh)
                V(lambda e: e.tensor_scalar(out=cs_[:, 8:12], in0=av[:, :, 64], scalar1=1e-30, scalar2=None, op0=ALU.max), [accX], [cs_])
                V(lambda e: e.reciprocal(out=cs_[:, 12:16], in_=cs_[:, 8:12]), [cs_], [cs_])
                V(lambda e: e.tensor_tensor(out=cs_[:, 12:16], in0=cs_[:, 12:16], in1=gv_, op=ALU.mult), [cs_, gsig], [cs_])
                V(lambda e: e.tensor_tensor(out=otmp[:, :, :], in0=av[:, :, 0:64], in1=cs_[:, 12:16].unsqueeze(2).to_broadcast([128, 4, 64]), op=ALU.mult), [accX, cs_], [otmp])
                V(lambda e: e.tensor_tensor(out=ov_, in0=ov_, in1=otmp[:, :, :], op=ALU.add), [osb, otmp], [osb])

            for hh in range(2):
                accC = pacc.next()
                attn_branch(actx, 128, accC, hh, 64, [0, 1],
                            lambda ct, hh=hh: (KcT[:, hh, ct * 128:(ct + 1) * 128], 128),
                            lambda ct, hh=hh: Vca[:, ct, hh, 0:128], 128,
                            lambda ct, cm4=cm4: (cm4[:, ct, :, :], cm4), [KcT, Vca])
                accv = accC[:, :].rearrange("p (g n) -> p g n", g=4)
                cs_ = cst2[hh]
                im_ = impt2[hh]
                m8_ = m82[hh]
                nb_ = nbw2[hh]
                ac_ = acc_c2[hh]
                V(lambda e: e.tensor_reduce(out=cs_[:, 0:4], in_=accv[:, :, 64:128], axis=AX.X, op=ALU.add), [accC], [cs_])
                V(lambda e: e.tensor_scalar(out=cs_[:, 0:4], in0=cs_[:, 0:4], scalar1=1e-30, scalar2=None, op0=ALU.max), [cs_], [cs_])
                V(lambda e: e.reciprocal(out=cs_[:, 4:8], in_=cs_[:, 0:4]), [cs_], [cs_])
                V(lambda e: e.tensor_tensor(out=ac_[:, :, 0:128], in0=accv[:, :, 0:128], in1=cs_[:, 4:8].unsqueeze(2).to_broadcast([128, 4, 128]), op=ALU.mult), [accC, cs_], [ac_])
                ov_ = ov_of(hh)
                gv0_ = gview_of(hh)[:, :, 0:1].to_broadcast([128, 4, 64])
                V(lambda e: e.tensor_tensor(out=ov_, in0=ac_[:, :, 0:64], in1=gv0_, op=ALU.mult), [ac_, gsig], [osb])
                V(lambda e: e.tensor_reduce(out=im_[:, 0, :], in_=ac_[:, :, 64:128].rearrange("p g n -> p n g"), axis=AX.X, op=ALU.add), [ac_], [im_])
                V(lambda e, AB=AB: e.tensor_tensor(out=im_[:, 0, :], in0=im_[:, 0, :], in1=AB[:, 0, :], op=ALU.mult), [im_, AB], [im_])
                V(lambda e, AB=AB: e.tensor_tensor(out=im_[:, 0, :], in0=im_[:, 0, :], in1=AB[:, 1, :], op=ALU.add), [im_, AB], [im_])
                V(lambda e: e.max(out=m8_[:, 0:8], in_=im_[:, 0, :]), [im_], [m8_])
                V(lambda e: e.match_replace(out=im_[:, 1, :], in_to_replace=m8_[:, 0:8], in_values=im_[:, 0, :], imm_value=-1e30), [im_, m8_], [im_])
                V(lambda e: e.max(out=m8_[:, 8:16], in_=im_[:, 1, :]), [im_], [m8_])
                V(lambda e: e.tensor_scalar(out=m8_[:, 15:16], in0=m8_[:, 15:16], scalar1=-0.5, scalar2=None, op0=ALU.max), [m8_], [m8_])
                V(lambda e: e.tensor_scalar(out=im_[:, 2, :], in0=im_[:, 0, :], scalar1=m8_[:, 15:16], scalar2=None, op0=ALU.is_ge), [im_, m8_], [im_])
                V(lambda e: e.tensor_scalar(out=nb_[:, 64:128], in0=im_[:, 2, :], scalar1=-1.0, scalar2=-NEG, op0=ALU.add, op1=ALU.mult), [im_], [nb_])
            for hh in range(2):
                accW = pacc.next()
                attn_branch(actx, 128, accW, hh, 64, list(range(ti - 4, ti + 1)),
                            lambda kt, hh=hh: (KW[:, hh, kt * 128:(kt + 1) * 128], 128),
                            lambda kt, hh=hh: VW[:, kt, hh, 0:65], 65,
                            lambda kt, ti=ti: (tri4[:, 0, :, :], tri4) if kt == ti else ((tri4[:, 1, :, :], tri4) if kt == ti - 4 else None), [KW, VW])
                consume(hh, 2, accW)
            for hh in range(2):
                p = pT.next()
                PE(lambda e, p=p, hh=hh: e.transpose(out=p[:, 0, :], in_=nbw2[hh][:, :], identity=ident[:, :]), [nbw2[hh], ident], [p])
                for g in range(4):
                    A(lambda e, p=p, hh=hh, g=g: e.copy(out=Qa[64:128, hh * 4 + g, :], in_=p[64:128, 0, :]), [p], [Qa])
            for hh in range(2):
                accS = pacc.next()
                attn_branch(actx, 128, accS, hh, 128, list(range(0, ti + 1)),
                            lambda kt, hh=hh: (KS[hh][:, kt * 128:(kt + 1) * 128], 128),
                            lambda kt, hh=hh: VS[:, kt, hh, 0:65], 65,
                            lambda kt, ti=ti: (tri4[:, 0, :, :], tri4) if kt == ti else None, [KS[hh], VS])
                consume(hh, 1, accS)
            ST_(oscr[qt * 128:(qt + 1) * 128, :], osb[:, :], [osb], q='sp')
    S.barrier()
    stores.close()

    with ExitStack() as ph:
        load_g(ph, [0])
        tri4, zeros_b = mk_masks(ph)
        wq = load_w(ph, "wq", w_in, D, IN_W, 0, 512)
        wg = load_w(ph, "wg", w_in, D, IN_W, 1280, 1304)
        KSs = [sb(ph, "KSs%d" % i, [128, 8320], BF16) for i in range(2)]
        VSs = sb(ph, "VSs", [128, 66, 2, 66], BF16)
        KWs = sb(ph, "KWs", [64, 2, 640], BF16)
        VWs = sb(ph, "VWs", [128, 5, 2, 66], BF16)
        for i in range(2):
            LDC(KSs[i][64:128, 0:4096], c_E[:, :], [KSs[i]])
            LDC(KSs[i][64:128, 4096:8192], c_E[:, :], [KSs[i]])
            LDC(KSs[i][64:128, 8192:8320], c_E[:, 0:128], [KSs[i]])
        svalid = sb(ph, "svalid", [128, 66], F32)
        swvalid = sb(ph, "swvalid", [128, 5], F32)
        LD(svalid[:, :], s_valid[:, :], [svalid])
        LD(swvalid[:, :], s_wvalid[:, :], [swvalid])
        for i in range(2):
            V(lambda e, i=i: e.tensor_copy(out=VSs[:, :, i, 64], in_=svalid[:, :]), [svalid], [VSs])
            V(lambda e, i=i: e.tensor_copy(out=VWs[:, :, i, 64], in_=swvalid[:, :]), [swvalid], [VWs])
        m1 = sb(ph, "m1", [128, 10, ST], BF16)
        m4 = sb(ph, "m4", [128, 10, 4, ST], BF16)
        scm = sb(ph, "scm", [128, 4], F32)
        LD(scm[:, :], s_cmask[:, :], [scm])
        for ct in range(4):
            V(lambda e, ct=ct: e.tensor_copy(out=m1[:, ct, :], in_=scm[:, ct:ct + 1].to_broadcast([128, ST])), [scm], [m1])
        LDC(m1[:, 4:9, :], s_wmask[:, :, :], [m1])
        LDC(m1[:, 9, :], s_lmask[:, :], [m1])
        for i in range(10):
            V(lambda e, i=i: e.tensor_copy(out=m4[:, i, :, :], in_=m1[:, i, :].unsqueeze(1).to_broadcast([128, 4, ST])), [m1], [m4])
        ABs = sb(ph, "ABs", [ST, 2, 192], F32)
        LD(ABs[:, 0, :], s_A[:, :], [ABs])
        LD(ABs[:, 1, :], s_B[:, :], [ABs])
        rings = (Ring([sb(ph, "stg4", [128, 4, 128], F32) for _ in range(3)]), Ring([sb(ph, "stb4", [128, 4, 128], BF16) for _ in range(3)]))
        zq = sb(ph, "zq", [128, 512], F32)
        zqb = sb(ph, "zqb", [128, 512], BF16)
        QaS = sb(ph, "QaS", [128, 3, 8, ST], BF16)
        qT = sb(ph, "qT", [128, 8, 128], BF16)
        gsig = sb(ph, "gsig", [128, 24], F32)
        rtmp = sb(ph, "rtmpb", [128, 4, 64], F32)
        Pb_r = Ring([sb(ph, "Pb", [128, 4, 128], BF16) for _ in range(8)])
        acc_c = sb(ph, "acc_c", [ST, 4, 256], F32)
        impt = sb(ph, "impt", [ST, 4, 192], F32)
        m8 = sb(ph, "m8", [ST, 16], F32)
        nbw = sb(ph, "nbw", [ST, 3, 128], BF16)
        G(lambda e: e.memset(nbw[:, :, :], 0.0), [], [nbw])
        osb = sb(ph, "osb", [128, 512], F32)
        otmp = sb(ph, "otmp", [128, 4, 64], F32)
        cst = sb(ph, "cst", [128, 16], F32)
        accsb_r = Ring([sb(ph, "accsb", [128, 512], F32) for _ in range(2)])
        actx = (QaS, Pb_r, accsb_r)

        for b in range(SB):
            r0 = b * ST
            stream_pages(rings, pool_groups(b, 2, r0), dstT=(lambda hh, lo, hi: KSs[hh][0:64, lo:hi], lambda hh: KSs[hh]))
            stream_pages(rings, pool_groups(b, 3, r0), dstV=lambda j0, n: (VSs[:, j0:j0 + n, :, 0:64], VSs))
            stream_pages(rings, win_groups(b, ckw, 4, r0), dstT=(lambda hh, lo, hi: KWs[:, hh, lo:hi], KWs))
            stream_pages(rings, win_groups(b, cvw, 5, r0), dstV=lambda j0, n: (VWs[:, j0:j0 + n, :, 0:64], VWs))

            xt = xt_r.next()
            LD(xt[0:ST, :], xs[r0:r0 + ST, :], [xt])
            h = h_r.next()
            rmsnorm(xt, ST, 0, h)
            hT = hT_r.next()
            transpose_bf(h, ST, 8, hT)

            def consq(p, g0, gn):
                A(lambda e: e.copy(out=zq[0:ST, :], in_=p[0:ST, 0:512]), [p], [zq])
            linear(hT, ST, 8, wq, 0, 512, consq)
            zqv = zq[0:ST, :].rearrange("p (h d) -> p h d", h=8)
            rope(zq, ST, lambda lo, hi: zqv[:, :, lo:hi], (lambda lo, hi: cs_s[:, lo:hi], cs_s), rtmp)
            V(lambda e: e.tensor_copy(out=zqb[0:ST, :], in_=zq[0:ST, :]), [zq], [zqb])
            transpose_bf(zqb, ST, 8, qT, width=64)
            for v in range(3):
                V(lambda e, v=v: e.tensor_copy(out=QaS[0:64, v, :, :], in_=qT[0:64, :, 0:ST]), [qT], [QaS])

            def consg(p, g0, gn):
                A(lambda e: e.activation(out=gsig[0:ST, :], in_=p[0:ST, 0:24], func=AF.Sigmoid), [p], [gsig])
            linear(hT, ST, 8, wg, 0, 24, consg)

            for hh in range(2):
                for pair in range(2):
                    accC = pacc.next()
                    for vb in range(2):
                        attn_branch(actx, ST, accC, hh, 64, [0, 1, 2, 3],
                                    lambda ct, hh=hh: (KcT_s[b][:, hh, ct * 128:(ct + 1) * 128], 128),
                                    lambda ct, hh=hh, vb=vb: Vca_s[b][:, ct, hh, vb * 128:(vb + 1) * 128], 128,
                                    lambda ct: (m4[:, ct, :, :], m4), [KcT_s[b], Vca_s[b]],
                                    q_fn=lambda kt, hh=hh: QaS[0:64, 0, hh * 4:hh * 4 + 4, :],
                                    heads=(pair * 2, pair * 2 + 1), gstride=256, col0=vb * 128)
                    accv = accC[0:ST, :].rearrange("p (g n) -> p g n", g=2)
                    V(lambda e, accv=accv, pair=pair: e.tensor_reduce(out=cst[0:ST, pair * 2:pair * 2 + 2], in_=accv[:, :, 64:256], axis=AX.X, op=ALU.add), [accC], [cst])
                    V(lambda e, accv=accv, pair=pair: e.tensor_copy(out=acc_c[:, pair * 2:pair * 2 + 2, :], in_=accv), [accC], [acc_c])
                V(lambda e: e.tensor_scalar(out=cst[0:ST, 0:4], in0=cst[0:ST, 0:4], scalar1=1e-30, scalar2=None, op0=ALU.max), [cst], [cst])
                V(lambda e: e.reciprocal(out=cst[0:ST, 4:8], in_=cst[0:ST, 0:4]), [cst], [cst])
                V(lambda e: e.tensor_tensor(out=acc_c[:, :, :], in0=acc_c[:, :, :], in1=cst[0:ST, 4:8].unsqueeze(2).to_broadcast([ST, 4, 256]), op=ALU.mult), [acc_c, cst], [acc_c])
                V(lambda e: e.tensor_reduce(out=impt[:, 0, :], in_=acc_c[:, :, 64:256].rearrange("p g n -> p n g"), axis=AX.X, op=ALU.add), [acc_c], [impt])
                V(lambda e: e.tensor_tensor(out=impt[:, 0, :], in0=impt[:, 0, :], in1=ABs[:, 0, :], op=ALU.mult), [impt, ABs], [impt])
                V(lambda e: e.tensor_tensor(out=impt[:, 0, :], in0=impt[:, 0, :], in1=ABs[:, 1, :], op=ALU.add), [impt, ABs], [impt])
                V(lambda e: e.max(out=m8[:, 0:8], in_=impt[:, 0, :]), [impt], [m8])
                V(lambda e: e.match_replace(out=impt[:, 1, :], in_to_replace=m8[:, 0:8], in_values=impt[:, 0, :], imm_value=-1e30), [impt, m8], [impt])
                V(lambda e: e.max(out=m8[:, 8:16], in_=impt[:, 1, :]), [impt], [m8])
                V(lambda e: e.tensor_scalar(out=m8[:, 15:16], in0=m8[:, 15:16], scalar1=-0.5, scalar2=None, op0=ALU.max), [m8], [m8])
                V(lambda e: e.tensor_scalar(out=impt[:, 2, :], in0=impt[:, 0, :], scalar1=m8[:, 15:16], scalar2=None, op0=ALU.is_ge), [impt, m8], [impt])
                V(lambda e: e.tensor_scalar(out=nbw[:, :, 64:128], in0=impt[:, 2, :].rearrange("p (v n) -> p v n", v=3), scalar1=-1.0, scalar2=-NEG, op0=ALU.add, op1=ALU.mult), [impt], [nbw])
                p = pT.next()
                for v in range(3):
                    PE(lambda e, p=p, v=v: e.transpose(out=p[:, v, 0:ST], in_=nbw[0:ST, v, :], identity=ident[0:ST, 0:ST]), [nbw, ident], [p])
                for v in range(3):
                    for g in range(4):
                        A(lambda e, p=p, hh=hh, g=g, v=v: e.copy(out=QaS[64:128, v, hh * 4 + g, :], in_=p[64:128, v, 0:ST]), [p], [QaS])
                accS = pacc.next()
                attn_branch(actx, ST, accS, hh, 128, list(range(0, 65)),
                            lambda kt, hh=hh: (KSs[hh][:, kt * 128:(kt + 1) * 128], 128),
                            lambda kt, hh=hh: VSs[:, kt, hh, 0:65], 65,
                            lambda kt: (m4[:, 9, :, :], m4) if kt == 64 else None, [KSs[hh], VSs],
                            q_fn=lambda kt, hh=hh: QaS[:, min(kt // 32, 2), hh * 4:hh * 4 + 4, :])
                accW = pacc.next()
                attn_branch(actx, ST, accW, hh, 64, list(range(5)),
                            lambda kt, hh=hh: (KWs[:, hh, kt * 128:(kt + 1) * 128], 128),
                            lambda kt, hh=hh: VWs[:, kt, hh, 0:65], 65,
                            lambda kt: (m4[:, 4 + kt, :, :], m4), [KWs, VWs],
                            q_fn=lambda kt, hh=hh: QaS[0:64, 0, hh * 4:hh * 4 + 4, :])
                gview = gsig[0:ST, hh * 12:(hh + 1) * 12].rearrange("p (g t) -> p g t", t=3)
                ov = osb[0:ST, hh * 256:(hh + 1) * 256].rearrange("p (g d) -> p g d", g=4)
                V(lambda e, gview=gview, ov=ov: e.tensor_tensor(out=ov, in0=acc_c[:, :, 0:64], in1=gview[:, :, 0:1].to_broadcast([ST, 4, 64]), op=ALU.mult), [acc_c, gsig], [osb])
                for bi, accX in ((1, accS), (2, accW)):
                    av = accX[0:ST, :].rearrange("p (g n) -> p g n", g=4)
                    V(lambda e, av=av: e.tensor_scalar(out=cst[0:ST, 8:12], in0=av[:, :, 64], scalar1=1e-30, scalar2=None, op0=ALU.max), [accX], [cst])
                    V(lambda e: e.reciprocal(out=cst[0:ST, 12:16], in_=cst[0:ST, 8:12]), [cst], [cst])
                    V(lambda e, gview=gview, bi=bi: e.tensor_tensor(out=cst[0:ST, 12:16], in0=cst[0:ST, 12:16], in1=gview[:, :, bi], op=ALU.mult), [cst, gsig], [cst])
                    V(lambda e, av=av: e.tensor_tensor(out=otmp[0:ST, :, :], in0=av[:, :, 0:64], in1=cst[0:ST, 12:16].unsqueeze(2).to_broadcast([ST, 4, 64]), op=ALU.mult), [accX, cst], [otmp])
                    V(lambda e, ov=ov: e.tensor_tensor(out=ov, in0=ov, in1=otmp[0:ST, :, :], op=ALU.add), [osb, otmp], [osb])
            ST_(oscr[HALF + r0:HALF + r0 + ST, :], osb[0:ST, :], [osb], q='sp')
    S.barrier()
    sper.close()

    with ExitStack() as ph:
        load_g(ph, [0])
        wr = load_w(ph, "wr", w_in, D, IN_W, 1304, IN_W)
        wno = load_w(ph, "wno", w_nsa_o, 512, D)
        wpw = load_w(ph, "wpw", w_pw, 512, D)
        wo = load_w(ph, "wo", w_out, D, D)
        cv = sb(ph, "cv", [128, 12], F32)
        LD(cv[:, :], cvec[:, :], [cv])
        wdw_sb = sb(ph, "wdw_sb", [31, 512], F32)
        LD(wdw_sb[:, :], w_dw[:, :], [wdw_sb])
        wdT = sb(ph, "wdT", [128, 4, 32], F32)
        pw_ = pmm.next()
        for c in range(4):
            PE(lambda e, c=c: e.transpose(out=pw_[:, c * 32:c * 32 + 31], in_=wdw_sb[0:31, c * 128:(c + 1) * 128], identity=identf[0:31, 0:31]), [wdw_sb, identf], [pw_])
        V(lambda e: e.tensor_copy(out=wdT[:, :, 0:31], in_=pw_[:, 0:128].rearrange("p (c j) -> p c j", c=4)[:, :, 0:31]), [pw_], [wdT])
        dg = sb(ph, "dg", [128, 4, 31, 128], BF16)
        for c in range(4):
            for j in range(31):
                eng = V if (j % 2 == 0) else G
                eng(lambda e, c=c, j=j: e.tensor_scalar(out=dg[:, c, j, :], in0=identf[:, :], scalar1=wdT[:, c, j:j + 1], scalar2=None, op0=ALU.mult), [identf, wdT], [dg])
        ones_f = sb(ph, "ones_f", [128, 128], F32)
        G(lambda e: e.memset(ones_f[:, :], 1.0 / 512.0), [], [ones_f])
        gluT_r = Ring([sb(ph, "gluT", [128, 4, 30 + 128], BF16) for _ in range(2)])
        for g_ in gluT_r.bufs:
            G(lambda e, g_=g_: e.memset(g_[:, :, :], 0.0), [], [g_])
        glu = sb(ph, "glu", [128, 512], F32)
        glub = sb(ph, "glub", [128, 512], BF16)
        gm = sb(ph, "gm", [128, 2048], F32)
        osb = sb(ph, "osb2", [128, 512], F32)
        osbb = sb(ph, "osbb", [128, 512], BF16)
        oT = sb(ph, "oT", [128, 4, 128], BF16)
        ynsa = sb(ph, "ynsa", [128, D], F32)
        gtm = sb(ph, "gtm", [128, D], F32)
        cconv = sb(ph, "cconv", [128, 4, 128], F32)
        csq = sb(ph, "csq", [128, 4, 128], F32)
        lnst = sb(ph, "lnst", [128, 4, 128], F32)
        csT = sb(ph, "csT", [128, 4, 128], BF16)
        mg = sb(ph, "mg", [128, D], F32)
        mgb = sb(ph, "mgb", [128, D], BF16)
        mT = sb(ph, "mT", [128, 8, 128], BF16)
        x1 = mg

        def proj_glu(hT, nt):
            def cons(p, g0, gn):
                c = g0
                if c < 512:
                    A(lambda e: e.copy(out=gtm[0:nt, c:c + gn], in_=p[0:nt, 0:gn]), [p], [gtm])
                else:
                    A(lambda e: e.activation(out=gtm[0:nt, c:c + gn], in_=p[0:nt, 0:gn], func=AF.Sigmoid), [p], [gtm])
            linear(hT, nt, 8, wr, 0, 1024, cons)
            V(lambda e: e.tensor_tensor(out=glu[0:nt, :], in0=gtm[0:nt, 0:512], in1=gtm[0:nt, 512:1024], op=ALU.mult), [gtm], [glu])
            V(lambda e: e.tensor_copy(out=glub[0:nt, :], in_=glu[0:nt, :]), [glu], [glub])

        def glu_to_T(gluT, nt, col0):
            p = pT.next()
            for c in range(4):
                PE(lambda e, c=c, p=p: e.transpose(out=p[:, c, 0:nt], in_=glub[0:nt, c * 128:(c + 1) * 128], identity=ident[0:nt, 0:nt]), [glub, ident], [p])
            A(lambda e, p=p: e.copy(out=gluT[:, :, col0:col0 + nt], in_=p[:, 0:4, 0:nt]), [p], [gluT])

        def b2_rest(gluT, xt, hT, nt, o_src, dst_ap):
            def consm(p, g0, gn):
                c = g0 - 1024
                A(lambda e: e.activation(out=gm[0:nt, c:c + gn], in_=p[0:nt, 0:gn], func=AF.Sigmoid), [p], [gm])
            linear(hT, nt, 8, wr, 1024, 1024 + 2048, consm)
            LD(osb[0:nt, :], o_src, [osb])
            V(lambda e: e.tensor_copy(out=osbb[0:nt, :], in_=osb[0:nt, :]), [osb], [osbb])
            transpose_bf(osbb, nt, 4, oT)

            def consn(p, g0, gn):
                V(lambda e: e.tensor_tensor(out=mg[0:nt, g0:g0 + gn], in0=p[0:nt, 0:gn], in1=gm[0:nt, g0:g0 + gn], op=ALU.mult), [p, gm], [mg])
            linear(oT, nt, 4, wno, 0, D, consn)
            for c in range(4):
                p = pmm.next()
                for j in range(31):
                    PE(lambda e, c=c, j=j, p=p: e.matmul(p[:, 0:nt], lhsT=dg[:, c, j, :], rhs=gluT[:, c, j:j + nt], start=(j == 0), stop=(j == 30)), [dg, gluT], [p], inc=(j == 30))
                A(lambda e, c=c, p=p: e.activation(out=cconv[:, c, 0:nt], in_=p[:, 0:nt], func=AF.Identity, bias=cv[:, c:c + 1]), [p, cv], [cconv])
            V(lambda e: e.tensor_tensor(out=csq[:, :, 0:nt], in0=cconv[:, :, 0:nt], in1=cconv[:, :, 0:nt], op=ALU.mult), [cconv], [csq])
            pst = pmm.next()
            for c in range(4):
                PE(lambda e, c=c: e.matmul(pst[:, 0:nt], lhsT=ones_f[:, :], rhs=cconv[:, c, 0:nt], start=(c == 0), stop=(c == 3)), [ones_f, cconv], [pst], inc=(c == 3))
            pst2 = pmm.next()
            for c in range(4):
                PE(lambda e, c=c: e.matmul(pst2[:, 0:nt], lhsT=ones_f[:, :], rhs=csq[:, c, 0:nt], start=(c == 0), stop=(c == 3)), [ones_f, csq], [pst2], inc=(c == 3))
            V(lambda e: e.tensor_copy(out=lnst[:, 0, 0:nt], in_=pst[:, 0:nt]), [pst], [lnst])
            V(lambda e: e.tensor_tensor(out=lnst[:, 1, 0:nt], in0=lnst[:, 0, 0:nt], in1=lnst[:, 0, 0:nt], op=ALU.mult), [lnst], [lnst])
            V(lambda e: e.tensor_tensor(out=lnst[:, 1, 0:nt], in0=pst2[:, 0:nt], in1=lnst[:, 1, 0:nt], op=ALU.subtract), [pst2, lnst], [lnst])
            V(lambda e: e.tensor_scalar(out=lnst[:, 1, 0:nt], in0=lnst[:, 1, 0:nt], scalar1=EPS, scalar2=None, op0=ALU.add), [lnst], [lnst])
            A(lambda e: e.activation(out=lnst[:, 2, 0:nt], in_=lnst[:, 1, 0:nt], func=AF.Sqrt), [lnst], [lnst])
            V(lambda e: e.reciprocal(out=lnst[:, 3, 0:nt], in_=lnst[:, 2, 0:nt]), [lnst], [lnst])
            V(lambda e: e.tensor_tensor(out=cconv[:, :, 0:nt], in0=cconv[:, :, 0:nt], in1=lnst[:, 0:1, 0:nt].to_broadcast([128, 4, nt]), op=ALU.subtract), [cconv, lnst], [cconv])
            V(lambda e: e.tensor_tensor(out=cconv[:, :, 0:nt], in0=cconv[:, :, 0:nt], in1=lnst[:, 3:4, 0:nt].to_broadcast([128, 4, nt]), op=ALU.mult), [cconv, lnst], [cconv])
            for c in range(4):
                V(lambda e, c=c: e.tensor_scalar(out=cconv[:, c, 0:nt], in0=cconv[:, c, 0:nt], scalar1=cv[:, 4 + c:5 + c], scalar2=cv[:, 8 + c:9 + c], op0=ALU.mult, op1=ALU.add), [cconv, cv], [cconv])
            A(lambda e: e.activation(out=csq[:, :, 0:nt], in_=cconv[:, :, 0:nt], func=AF.Sigmoid), [cconv], [csq])
            V(lambda e: e.tensor_tensor(out=csT[:, :, 0:nt], in0=cconv[:, :, 0:nt], in1=csq[:, :, 0:nt], op=ALU.mult), [cconv, csq], [csT])

            def consc(p, g0, gn):
                V(lambda e: e.tensor_tensor(out=ynsa[0:nt, g0:g0 + gn], in0=p[0:nt, 0:gn], in1=gm[0:nt, D + g0:D + g0 + gn], op=ALU.mult), [p, gm], [ynsa])
            linear(csT, nt, 4, wpw, 0, D, consc)
            V(lambda e: e.tensor_tensor(out=mgb[0:nt, :], in0=mg[0:nt, :], in1=ynsa[0:nt, :], op=ALU.add), [mg, ynsa], [mgb])
            transpose_bf(mgb, nt, 8, mT)

            def conso(p, g0, gn):
                V(lambda e: e.tensor_tensor(out=x1[0:nt, g0:g0 + gn], in0=p[0:nt, 0:gn], in1=xt[0:nt, g0:g0 + gn], op=ALU.add), [p, xt], [x1])
            linear(mT, nt, 8, wo, 0, D, conso)
            ST_(dst_ap, x1[0:nt, :], [x1], q='sp')

        def b2_part1(qt, prev_gluT):
            ti = NT_OWN + qt
            xt = xt_r.next()
            LD(xt[:, :], xp[ti * 128:(ti + 1) * 128, :], [xt])
            h = h_r.next()
            rmsnorm(xt, 128, 0, h)
            hT = hT_r.next()
            transpose_bf(h, 128, 8, hT)
            proj_glu(hT, 128)
            gluT = gluT_r.next()
            glu_to_T(gluT, 128, 30)
            if prev_gluT is not None:
                V(lambda e: e.tensor_copy(out=gluT[:, :, 0:30], in_=prev_gluT[:, :, 128:158]), [prev_gluT], [gluT])
            if qt == NT_OWN - 1:
                ST_(pconv[:, :], glu[98:128, :], [glu], q='sp')
            return (gluT, xt, hT)

        st_prev = b2_part1(-1, None)
        st_next = b2_part1(0, st_prev[0])
        for qt in range(NT_OWN):
            st_cur = st_next
            if qt + 1 < NT_OWN:
                st_next = b2_part1(qt + 1, st_cur[0])
            b2_rest(st_cur[0], st_cur[1], st_cur[2], 128, oscr[qt * 128:(qt + 1) * 128, :], x1s[qt * 128:(qt + 1) * 128, :])

        for b in range(SB):
            r0 = b * ST
            gluT = gluT_r.next()
            LD(glu[0:30, :], sconv_in[b, :, :], [glu])
            V(lambda e: e.tensor_copy(out=glub[0:30, :], in_=glu[0:30, :]), [glu], [glub])
            glu_to_T(gluT, 30, 0)
            xt = xt_r.next()
            LD(xt[0:ST, :], xs[r0:r0 + ST, :], [xt])
            h = h_r.next()
            rmsnorm(xt, ST, 0, h)
            hT = hT_r.next()
            transpose_bf(h, ST, 8, hT)
            proj_glu(hT, ST)
            glu_to_T(gluT, ST, 30)
            ST_(sconv[b, 0:30 - ST, :], sconv_in[b, ST:30, :], [], q='sp')
            ST_(sconv[b, 30 - ST:30, :], glu[0:ST, :], [glu], q='sp')
            b2_rest(gluT, xt, hT, ST, oscr[HALF + r0:HALF + r0 + ST, :], x1s[HALF + r0:HALF + r0 + ST, :])
    S.barrier()

    with ExitStack() as ph:
        load_g(ph, [1, 2, 3, 4])
        wxq = load_w(ph, "wxq", w_xq, D, 512)
        wxo = load_w(ph, "wxo", w_xo, 512, D)
        wup = load_w(ph, "wup", w_up, D, 4096, defer=True)
        wdn = load_w(ph, "wdn", w_down, 4096, D, defer=True)
        mkT = sb(ph, "mkT", [128, 4, 256], BF16)
        mva = sb(ph, "mva", [128, 2, 4, 130], BF16)
        G(lambda e: e.memset(mva[:, :, :, 128:130], 1.0), [], [mva])
        mkf = sb(ph, "mkf", [128, 512], F32)
        qx = sb(ph, "qx", [128, 512], BF16)
        mkb = qx
        with ExitStack() as ph2:
            wxk = load_w(ph2, "wxk", w_xk, D, 512)
            wxv = load_w(ph2, "wxv", w_xv, D, 512)
            wup.issue()
            wdn.issue()
            for mt in range(2):
                xt = xt_r.next()
                LD(xt[:, :], memp[mt * 128:(mt + 1) * 128, :], [xt])
                h = h_r.next()
                rmsnorm(xt, 128, 2, h)
                hT = hT_r.next()
                transpose_bf(h, 128, 8, hT)

                def consk(p, g0, gn):
                    A(lambda e: e.copy(out=mkf[:, :], in_=p[:, 0:512]), [p], [mkf])
                linear(hT, 128, 8, wxk, 0, 512, consk)
                ST_(pmk[mt * 128:(mt + 1) * 128, :], mkf[:, :], [mkf], q='sp')
                V(lambda e: e.tensor_copy(out=mkb[:, :], in_=mkf[:, :]), [mkf], [mkb])
                p = pT.next()
                for hd in range(4):
                    PE(lambda e, hd=hd, p=p: e.transpose(out=p[:, hd, :], in_=mkb[:, hd * 128:(hd + 1) * 128], identity=ident[:, :]), [mkb, ident], [p])
                A(lambda e, p=p, mt=mt: e.copy(out=mkT[:, :, mt * 128:(mt + 1) * 128], in_=p[:, 0:4, :]), [p], [mkT])

                def consv(p, g0, gn):
                    A(lambda e: e.copy(out=mkf[:, :], in_=p[:, 0:512]), [p], [mkf])
                linear(hT, 128, 8, wxv, 0, 512, consv)
                ST_(pmv[mt * 128:(mt + 1) * 128, :], mkf[:, :], [mkf], q='sp')
                V(lambda e, mt=mt: e.tensor_copy(out=mva[:, mt, :, 0:128], in_=mkf[:, :].rearrange("p (h d) -> p h d", h=4)), [mkf], [mva])
            S.barrier(only=('pe',))

        qxT = sb(ph, "qxT", [128, 4, 128], BF16)
        PX = sb(ph, "PX", [128, 2, 4, 128], BF16)
        ox = sb(ph, "ox", [128, 512], BF16)
        oxT = sb(ph, "oxT", [128, 4, 128], BF16)
        uT = sb(ph, "uT", [128, 32, 128], BF16)
        ur = mkf
        yo = sb(ph, "yo", [128, D], F32)
        cst = sb(ph, "cst2", [128, 8], F32)

        def cd_pre(src_ap, nt):
            xt = xt_r.next()
            LD(xt[0:nt, :], src_ap, [xt])
            h = h_r.next()
            rmsnorm(xt, nt, 1, h)
            hT = hT_r.next()
            transpose_bf(h, nt, 8, hT)

            def consq(p, g0, gn):
                A(lambda e: e.copy(out=qx[0:nt, :], in_=p[0:nt, 0:512]), [p], [qx])
            linear(hT, nt, 8, wxq, 0, 512, consq)
            transpose_bf(qx, nt, 4, qxT)
            return xt

        def cd_core(nq, col0):
            for mt in range(2):
                p = pmm.next()
                for hd in range(4):
                    PE(lambda e, hd=hd, mt=mt, p=p: e.matmul(p[:, hd * nq:(hd + 1) * nq], lhsT=mkT[:, hd, mt * 128:(mt + 1) * 128], rhs=qxT[:, hd, col0:col0 + nq], start=True, stop=True), [mkT, qxT], [p], inc=(hd == 3))
                A(lambda e, mt=mt, p=p: e.activation(out=PX[:, mt, :, 0:nq], in_=p[:, 0:4 * nq].rearrange("k (h q) -> k h q", h=4), func=AF.Exp, scale=128.0 ** -0.5), [p], [PX])
            for half in range(2):
                pa = pacc.next()
                for hh in range(2):
                    hd = half * 2 + hh
                    for mt in range(2):
                        PE(lambda e, hd=hd, hh=hh, mt=mt, pa=pa: e.matmul(pa[0:nq, hh * 130:hh * 130 + 129], lhsT=PX[:, mt, hd, 0:nq], rhs=mva[:, mt, hd, 0:129], start=(mt == 0), stop=(mt == 1)), [PX, mva], [pa], inc=(mt == 1 and hh == 1))
                pav = pa[0:nq, 0:260].rearrange("p (h n) -> p h n", h=2)
                V(lambda e, pav=pav: e.reciprocal(out=cst[0:nq, 0:2], in_=pav[:, :, 128]), [pa], [cst])
                V(lambda e, pav=pav, half=half: e.tensor_tensor(out=ox[0:nq, half * 256:(half + 1) * 256].rearrange("p (h d) -> p h d", h=2), in0=pav[:, :, 0:128], in1=cst[0:nq, 0:2].unsqueeze(2).to_broadcast([nq, 2, 128]), op=ALU.mult), [pa, cst], [ox])
            transpose_bf(ox, nq, 4, oxT, col0=col0)

        def cd_post(xt, nt):
            x2 = xt

            def conso(p, g0, gn):
                V(lambda e: e.tensor_tensor(out=x2[0:nt, g0:g0 + gn], in0=p[0:nt, 0:gn], in1=xt[0:nt, g0:g0 + gn], op=ALU.add), [p, xt], [x2])
            linear(oxT, nt, 4, wxo, 0, D, conso)
            return x2

        def cd_part1(src_ap, nt):
            xt = cd_pre(src_ap, nt)
            cd_core(nt, 0)
            return cd_post(xt, nt)

        def cd_part2(x2, nt, dst_ap):
            x3 = x2
            h2 = h_r.next()
            rmsnorm(x2, nt, 3, h2)
            hT2 = hT_r.next()
            transpose_bf(h2, nt, 8, hT2)
            for c4 in range(8):
                p = pmm.next()
                for cc in range(4):
                    c = c4 * 4 + cc
                    for i in range(8):
                        PE(lambda e, c=c, cc=cc, i=i, p=p: e.matmul(p[:, cc * nt:(cc + 1) * nt], lhsT=wup[:, i, c * 128:(c + 1) * 128], rhs=hT2[:, i, 0:nt], start=(i == 0), stop=(i == 7)), [wup.ktoks[i], hT2], [p], inc=(i == 7 and cc == 3))
                A(lambda e, p=p: e.activation(out=ur[:, 0:4 * nt], in_=p[:, 0:4 * nt], func=AF.Relu), [p], [ur])
                V(lambda e, c4=c4: e.tensor_tensor(out=uT[:, c4 * 4:(c4 + 1) * 4, 0:nt], in0=ur[:, 0:4 * nt].rearrange("p (c q) -> p c q", c=4), in1=ur[:, 0:4 * nt].rearrange("p (c q) -> p c q", c=4), op=ALU.mult), [ur], [uT])

            def consd(p, g0, gn):
                V(lambda e: e.tensor_tensor(out=x3[0:nt, g0:g0 + gn], in0=p[0:nt, 0:gn], in1=x2[0:nt, g0:g0 + gn], op=ALU.add), [p, x2], [x3])
            linear(uT, nt, 32, wdn, 0, D, consd)
            rmsnorm(x3, nt, 4, yo)
            ST_(dst_ap, yo[0:nt, :], [yo], q='sp')

        x2_next = cd_part1(x1s[0:128, :], 128)
        for qt in range(NT_OWN):
            x2_cur = x2_next
            if qt + 1 < NT_OWN:
                x2_next = cd_part1(x1s[(qt + 1) * 128:(qt + 2) * 128, :], 128)
            cd_part2(x2_cur, 128, yp[qt * 128:(qt + 1) * 128, :])

        NS = SB * ST
        xts = cd_pre(x1s[HALF:HALF + NS, :], NS)
        for b in range(SB):
            for mt in range(2):
                LD(mkf[:, :], cmk[b, mt * 128:(mt + 1) * 128, :], [mkf])
                V(lambda e: e.tensor_copy(out=mkb[:, :], in_=mkf[:, :]), [mkf], [mkb])
                p = pT.next()
                for hd in range(4):
                    PE(lambda e, hd=hd, p=p: e.transpose(out=p[:, hd, :], in_=mkb[:, hd * 128:(hd + 1) * 128], identity=ident[:, :]), [mkb, ident], [p])
                A(lambda e, p=p, mt=mt: e.copy(out=mkT[:, :, mt * 128:(mt + 1) * 128], in_=p[:, 0:4, :]), [p], [mkT])
                LD(mkf[:, :], cmv[b, mt * 128:(mt + 1) * 128, :], [mkf])
                V(lambda e, mt=mt: e.tensor_copy(out=mva[:, mt, :, 0:128], in_=mkf[:, :].rearrange("p (h d) -> p h d", h=4)), [mkf], [mva])
            cd_core(ST, b * ST)
        cd_part2(cd_post(xts, NS), NS, ys[:, :])

    S.finish()
    wk.close()
    glob.close()
    return nc


def _rope_tab(pos):
    half = 8
    inv = (500000.0 ** (-np.arange(half, dtype=np.float32) / half)).astype(np.float32)
    ang = pos.astype(np.float32)[:, None] * inv[None, :]
    return np.concatenate([np.cos(ang), np.sin(ang)], axis=1).astype(np.float32)


def _core_consts(half):
    c = {}
    off = 0 if half == 1 else -HALF
    lpos = np.arange(S_FULL)
    gpos = lpos + off
    cs = _rope_tab(np.maximum(gpos, 0))
    c["c_cs"] = np.ascontiguousarray(cs.reshape(NT_ALL, 128, 16).transpose(1, 0, 2))
    c["c_valid"] = np.ascontiguousarray((gpos >= 0).astype(np.float32).reshape(NT_ALL, 128).T)
    t = gpos[HALF:]
    n = np.arange(64)
    gn = n + (0 if half == 1 else -32)
    elig = (gn[None, :] >= 0) & (gn[None, :] * 64 <= t[:, None])
    cur = t // 64
    forced = (gn[None, :] == 0) | (gn[None, :] == cur[:, None]) | (gn[None, :] == cur[:, None] - 1)
    A = (elig & ~forced).astype(np.float32)
    B = np.where(~elig, -1.0, np.where(forced, 1.0e4, 0.0)).astype(np.float32)
    c["c_A"] = np.ascontiguousarray(A.reshape(NT_OWN, 128, 64).transpose(1, 0, 2))
    c["c_B"] = np.ascontiguousarray(B.reshape(NT_OWN, 128, 64).transpose(1, 0, 2))
    cl = np.arange(256)
    gc = cl + (0 if half == 1 else -128)
    cm = (gc[:, None] >= 0) & (cl[:, None] <= 254) & (16 * gc[:, None] + 31 <= t[None, :])
    c["c_cmask"] = np.ascontiguousarray(cm.astype(np.float32).reshape(2, 128, HALF).transpose(1, 0, 2))
    start = cl[:, None] * 16
    sel0 = n[None, :] * 64
    ov = np.clip(np.minimum(start + 32, sel0 + 64) - np.maximum(start, sel0), 0, None) / 32.0
    c["c_M"] = np.ascontiguousarray(ov.astype(np.float32).reshape(2, 128, 64).transpose(1, 0, 2))
    return c


def _common_consts():
    c = {}
    c["c_ident"] = np.eye(128, dtype=np.float32)
    key = np.arange(4096)
    c["c_E"] = (key[None, :] // 64 == np.arange(64)[:, None]).astype(np.float32)
    kk = np.arange(128)[:, None]
    qq = np.arange(128)[None, :]
    c["c_tri"] = np.ascontiguousarray(np.stack([(kk <= qq), (kk > qq)], axis=1).astype(np.float32))
    c["c_pidx"] = np.arange(128, dtype=np.float32)[:, None]
    c["c_c8"] = np.ascontiguousarray(np.broadcast_to(np.arange(8, dtype=np.float32)[None, :], (NPAGE, 8)))
    tpos = PAST + np.arange(ST)
    c["s_cs"] = _rope_tab(tpos)
    n = np.arange(192)
    elig = (n[None, :] <= 128) & (n[None, :] * 64 <= tpos[:, None])
    cur = tpos // 64
    forced = (n[None, :] == 0) | (n[None, :] == cur[:, None]) | (n[None, :] == cur[:, None] - 1)
    c["s_A"] = (elig & ~forced).astype(np.float32)
    c["s_B"] = np.where(~elig, -1.0, np.where(forced, 1.0e4, 0.0)).astype(np.float32)
    cl = np.arange(512)
    c["s_cmask"] = np.ascontiguousarray((cl <= 510).astype(np.float32).reshape(4, 128).T)
    start = cl[:, None] * 16
    sel0 = n[None, :] * 64
    ov = np.clip(np.minimum(start + 32, sel0 + 64) - np.maximum(start, sel0), 0, None) / 32.0
    ov[:, 129:] = 0
    c["s_M"] = np.ascontiguousarray(ov.astype(np.float32).reshape(4, 128, 192).transpose(1, 0, 2))
    r = np.arange(640)
    kpos = np.where(r < 512, PAST - 512 + r, PAST + (r - 512))
    kvalid = r < 512 + ST
    dt = tpos[None, :] - kpos[:, None]
    wm = kvalid[:, None] & (dt >= 0) & (dt < 512)
    c["s_wmask"] = np.ascontiguousarray(wm.astype(np.float32).reshape(5, 128, ST).transpose(1, 0, 2))
    c["s_wvalid"] = np.ascontiguousarray(kvalid.astype(np.float32).reshape(5, 128).T)
    r2 = np.arange(128)
    c["s_lmask"] = ((r2[:, None] < ST) & (r2[:, None] <= np.arange(ST)[None, :])).astype(np.float32)
    r3 = np.arange(66 * 128)
    c["s_valid"] = np.ascontiguousarray((r3 < PAST + ST).astype(np.float32).reshape(66, 128).T)
    return c


_PROG = {}


def kernel(x_prompt, x_sample, mem_prompt, cache_k_cmp, cache_v_cmp, cache_k_slc, cache_v_slc, cache_k_win,
           cache_v_win, state_conv, cache_mem_k, cache_mem_v, page_table, norm_mix, w_in, cmp_pos_k, cmp_pos_v,
           w_ck1, w_ck2, w_cv1, w_cv2, w_nsa_o, w_dw, b_dw, conv_ln_g, conv_ln_b, w_pw, w_out, norm_x, norm_mem,
           w_xq, w_xk, w_xv, w_xo, norm_ff, w_up, w_down, norm_final):
    f = lambda a: np.ascontiguousarray(np.asarray(a, dtype=np.float32))
    if "nc" not in _PROG:
        _PROG["nc"] = build_program()
    nc = _PROG["nc"]
    common = _common_consts()
    shared = {
        "w_in": f(w_in[0]), "cpos_k": f(cmp_pos_k[0]), "cpos_v": f(cmp_pos_v[0]),
        "w_ck1": f(w_ck1[0]), "w_cv1": f(w_cv1[0]), "w_ck2": f(w_ck2[0]), "w_cv2": f(w_cv2[0]),
        "w_nsa_o": f(w_nsa_o[0]), "w_dw": f(w_dw[0]), "w_pw": f(w_pw[0]), "w_out": f(w_out[0]),
        "w_xq": f(w_xq[0]), "w_xk": f(w_xk[0]), "w_xv": f(w_xv[0]), "w_xo": f(w_xo[0]),
        "w_up": f(w_up[0]), "w_down": f(w_down[0]),
        "cvec": np.ascontiguousarray(np.stack([f(b_dw[0]).reshape(4, 128), f(conv_ln_g[0]).reshape(4, 128), f(conv_ln_b[0]).reshape(4, 128)], 0).reshape(12, 128).T),
        "gvec": np.ascontiguousarray(np.broadcast_to(np.stack([f(norm_mix[0]), f(norm_x[0]), f(norm_mem[0]), f(norm_ff[0]), f(norm_final)], 0)[None], (128, 5, D))),
        "pk_cmp": f(cache_k_cmp[0]).reshape(2560 * 8, 2048), "pv_cmp": f(cache_v_cmp[0]).reshape(2560 * 8, 2048),
        "pk_slc": f(cache_k_slc[0]).reshape(2560 * 8, 2048), "pv_slc": f(cache_v_slc[0]).reshape(2560 * 8, 2048),
    }
    shared.update(common)
    cc = [_core_consts(0), _core_consts(1)]
    xpr = f(x_prompt)
    in_maps = []
    for core in range(8):
        b, half = core // 2, core % 2
        m = dict(shared)
        m.update(cc[half])
        xp_ = np.zeros((S_FULL, D), np.float32)
        if half == 1:
            xp_[:] = xpr[b]
        else:
            xp_[HALF:] = xpr[b, :HALF]
        m["xp"] = xp_
        m["memp"] = f(mem_prompt[b])
        sl = slice(core * SB, (core + 1) * SB)
        m["xs"] = f(x_sample[sl]).reshape(SB * ST, D)
        m["ckw"] = f(cache_k_win[0, sl]).reshape(SB, 512, 128)
        m["cvw"] = f(cache_v_win[0, sl]).reshape(SB, 512, 128)
        m["sconv_in"] = f(state_conv[0, sl])
        m["cmk"] = f(cache_mem_k[0, sl]).reshape(SB, 256, 512)
        m["cmv"] = f(cache_mem_v[0, sl]).reshape(SB, 256, 512)
        m["ptab"] = np.ascontiguousarray(np.asarray(page_table[sl], dtype=np.int32))
        in_maps.append(m)
    res = run_bass_kernel_spmd(nc, in_maps, core_ids=list(range(8))).results

    B4 = 4
    y_prompt = np.zeros((B4, S_FULL, D), np.float32)
    pst = [np.zeros((1, B4, S_FULL, 2, 64), np.float32) for _ in range(4)]
    pwin = [np.zeros((1, B4, 512, 2, 64), np.float32) for _ in range(2)]
    p_conv = np.zeros((1, B4, 30, 512), np.float32)
    p_mk = np.zeros((1, B4, 256, 4, 128), np.float32)
    p_mv = np.zeros((1, B4, 256, 4, 128), np.float32)
    y_sample = np.zeros((32, ST, D), np.float32)
    sst = [np.zeros((1, 32, ST, 2, 64), np.float32) for _ in range(4)]
    swin = [np.zeros((1, 32, 512, 2, 64), np.float32) for _ in range(2)]
    s_conv = np.zeros((1, 32, 30, 512), np.float32)
    for core in range(8):
        b, half = core // 2, core % 2
        r = res[core]
        ts = slice(half * HALF, (half + 1) * HALF)
        y_prompt[b, ts] = r["yp"]
        for i in range(4):
            pst[i][0, b, ts] = r["okv"][i].reshape(HALF, 2, 64)
        if half == 1:
            for i in range(2):
                pwin[i][0, b] = r["okv"][4 + i][HALF - 512:].reshape(512, 2, 64)
            p_conv[0, b] = r["pconv"]
            p_mk[0, b] = r["pmk"].reshape(256, 4, 128)
            p_mv[0, b] = r["pmv"].reshape(256, 4, 128)
        sl = slice(core * SB, (core + 1) * SB)
        y_sample[sl] = r["ys"].reshape(SB, ST, D)
        for i in range(4):
            sst[i][0, sl] = r["skv"][i].reshape(SB, ST, 2, 64)
        swin[0][0, sl] = r["swk"].reshape(SB, 512, 2, 64)
        swin[1][0, sl] = r["swv"].reshape(SB, 512, 2, 64)
        s_conv[0, sl] = r["sconv"]
    return (y_prompt, y_sample, pst[0], pst[1], pst[2], pst[3], pwin[0], pwin[1], p_conv, p_mk, p_mv,
            sst[0], sst[1], sst[2], sst[3], swin[0], swin[1], s_conv)
```

```python
from contextlib import ExitStack
import numpy as np
import concourse.bass as bass
import concourse.mybir as mybir
from concourse.bass_utils import run_bass_kernel_spmd

F32 = mybir.dt.float32
BF16 = mybir.dt.bfloat16
I32 = mybir.dt.int32
AF = mybir.ActivationFunctionType
ALU = mybir.AluOpType
AX = mybir.AxisListType

ENGS = ['pe', 'dve', 'act', 'pool', 'sp']
NDS = 48
SAME_ENG_SYNC = True

D = 1024
S_FULL = 4096
HALF = 2048
NT_ALL = 32
NT_OWN = 16
IN_W = 4376
EPS = 1e-6
NEG = -32768.0
SB = 4
ST = 4
PAST = 8192
NPAGE = 64
POOL_ROWS = 2560 * 128


import types


def _freeze(fn):
    if fn is None or fn.__closure__ is None:
        return fn
    cells = []
    for c in fn.__closure__:
        try:
            cells.append(types.CellType(c.cell_contents))
        except ValueError:
            cells.append(c)
    return types.FunctionType(fn.__code__, fn.__globals__, fn.__name__, fn.__defaults__, tuple(cells))


class Tok:
    __slots__ = ('w', 'r')

    def __init__(self):
        self.w = None
        self.r = {}


class Sched:
    def __init__(self, nc):
        self.nc = nc
        self.sem = {e: nc.alloc_semaphore('s_' + e) for e in ENGS}
        self.cnt = {e: 0 for e in ENGS}
        self.known = {e: {} for e in ENGS}
        self.prog = {e: [] for e in ENGS}
        self.snap = {}
        self.dsem = [nc.alloc_semaphore('d%d' % i) for i in range(NDS)]
        self.dcnt = [0] * NDS
        self.dnext = 0
        self.dnext_sw = 0

    def _semh(self, k):
        return self.sem[k] if isinstance(k, str) else self.dsem[k[1]]

    def _gather(self, engine, reads, writes, extra=()):
        need = {}
        kn = self.known[engine]

        def add(ev):
            if ev is None:
                return
            k, v = ev
            if k == engine and (engine == 'pe' or not SAME_ENG_SYNC):
                return
            if kn.get(k, 0) >= v:
                return
            if need.get(k, 0) < v:
                need[k] = v
        for t in reads:
            add(t.w)
        for t in writes:
            add(t.w)
            for k, v in t.r.items():
                add((k, v))
        for ev in extra:
            add(ev)
        return need

    def _apply_waits(self, engine, need):
        kn = self.known[engine]
        waits = []
        for k, v in need.items():
            if kn.get(k, 0) >= v:
                continue
            waits.append((self._semh(k), v))
            kn[k] = v
            sn = self.snap.get((k, v))
            if sn:
                for k2, v2 in sn.items():
                    if kn.get(k2, 0) < v2:
                        kn[k2] = v2
        return waits

    def _record(self, ev, reads, writes):
        k, v = ev
        for t in reads:
            if t.r.get(k, 0) < v:
                t.r[k] = v
        for t in writes:
            t.w = ev
            t.r = {}

    def op(self, engine, fn, reads=(), writes=(), inc=True):
        fn = _freeze(fn)
        reads = [getattr(t, 'tok', t) for t in reads]
        writes = [getattr(t, 'tok', t) for t in writes]
        need = self._gather(engine, reads, writes)
        waits = self._apply_waits(engine, need)
        ev = (engine, self.cnt[engine] + 1)
        sem = self.sem[engine]
        if inc:
            self.cnt[engine] += 1
            self.snap[ev] = {k: v for k, v in self.known[engine].items() if isinstance(k, str)}

        def emit(e, waits=waits, fn=fn, inc=inc, sem=sem):
            for s, v in waits:
                e.wait_ge(s, v)
            ins = fn(e)
            if inc:
                ins.then_inc(sem, 1)
        self.prog[engine].append(emit)
        self._record(ev, reads, writes)
        return ev

    def dma(self, engine, out, in_, reads=(), writes=(), custom=None):
        custom = _freeze(custom)
        reads = [getattr(t, 'tok', t) for t in reads]
        writes = [getattr(t, 'tok', t) for t in writes]
        half = NDS // 2
        if engine == 'pool':
            i = self.dnext_sw
            self.dnext_sw = (self.dnext_sw + 1) % half
        else:
            i = half + self.dnext
            self.dnext = (self.dnext + 1) % half
        extra = []
        if self.dcnt[i] > 0:
            extra.append((('d', i), self.dcnt[i]))
        need = self._gather(engine, reads, writes, extra)
        waits = self._apply_waits(engine, need)
        self.dcnt[i] += 16
        ev = (('d', i), self.dcnt[i])
        sem = self.dsem[i]

        def emit(e, waits=waits, sem=sem):
            for s, v in waits:
                e.wait_ge(s, v)
            if custom is not None:
                ins = custom(e)
            else:
                ins = e.dma_start(out=out, in_=in_)
            ins.then_inc(sem, 16)
        self.prog[engine].append(emit)
        self._record(ev, reads, writes)
        return ev

    def barrier(self, only=None):
        if only is not None:
            evs = [(e, self.cnt[e]) for e in only if self.cnt[e] > 0]
        else:
            evs = [(e, self.cnt[e]) for e in ENGS if self.cnt[e] > 0]
            evs += [(('d', i), self.dcnt[i]) for i in range(NDS) if self.dcnt[i] > 0]
        for engine in ENGS:
            need = {}
            for k, v in evs:
                if k == engine:
                    continue
                if self.known[engine].get(k, 0) < v:
                    need[k] = v
            waits = self._apply_waits(engine, need)
            if waits:
                def emit(e, waits=waits):
                    for s, v in waits:
                        e.wait_ge(s, v)
                self.prog[engine].append(emit)

    def finish(self):
        self.barrier()
        nc = self.nc
        prog = self.prog
        with nc.allow_non_contiguous_dma(reason="small transposed constant loads"), nc.Block() as block:
            @block.sync
            def _(e):
                for f in prog['sp']:
                    f(e)

            @block.tensor
            def _(e):
                for f in prog['pe']:
                    f(e)

            @block.vector
            def _(e):
                for f in prog['dve']:
                    f(e)

            @block.scalar
            def _(e):
                for f in prog['act']:
                    f(e)

            @block.gpsimd
            def _(e):
                for f in prog['pool']:
                    f(e)


class Buf:
    def __init__(self, t):
        self.t = t
        self.tok = Tok()

    def __getitem__(self, idx):
        return self.t[idx]


class Ring:
    def __init__(self, bufs):
        self.bufs = bufs
        self.i = 0

    def next(self):
        b = self.bufs[self.i]
        self.i = (self.i + 1) % len(self.bufs)
        return b


class K:
    pass


def build_program():
    nc = bass.Bass("TRN2", target_bir_lowering=False)
    S = Sched(nc)
    k = K()
    k.nc = nc
    k.S = S
    cnt = [0]

    def din(name, shape, dt=F32):
        return nc.dram_tensor(name, list(shape), dt, kind="ExternalInput")

    def dout(name, shape, dt=F32):
        return nc.dram_tensor(name, list(shape), dt, kind="ExternalOutput")

    xp = din("xp", [S_FULL, D])
    xs = din("xs", [SB * ST, D])
    memp = din("memp", [256, D])
    pools = [din(n, [2560 * 8, 2048]) for n in ("pk_cmp", "pv_cmp", "pk_slc", "pv_slc")]
    ckw = din("ckw", [SB, 512, 128])
    cvw = din("cvw", [SB, 512, 128])
    sconv_in = din("sconv_in", [SB, 30, 512])
    cmk = din("cmk", [SB, 256, 512])
    cmv = din("cmv", [SB, 256, 512])
    ptab = din("ptab", [SB, NPAGE], I32)
    w_in = din("w_in", [D, IN_W])
    cpos = [din("cpos_k", [32, 64]), din("cpos_v", [32, 64])]
    w_c1 = [din("w_ck1", [2048, 128]), din("w_cv1", [2048, 128])]
    w_c2 = [din("w_ck2", [128, 64]), din("w_cv2", [128, 64])]
    w_nsa_o = din("w_nsa_o", [512, D])
    w_dw = din("w_dw", [31, 512])
    cvec = din("cvec", [128, 12])
    w_pw = din("w_pw", [512, D])
    w_out = din("w_out", [D, D])
    w_xq = din("w_xq", [D, 512])
    w_xk = din("w_xk", [D, 512])
    w_xv = din("w_xv", [D, 512])
    w_xo = din("w_xo", [512, D])
    w_up = din("w_up", [D, 4096])
    w_down = din("w_down", [4096, D])
    gvec = din("gvec", [128, 5, D])
    c_ident = din("c_ident", [128, 128])
    c_E = din("c_E", [64, 4096])
    c_tri = din("c_tri", [128, 2, 128])
    c_cs = din("c_cs", [128, NT_ALL, 16])
    c_valid = din("c_valid", [128, NT_ALL])
    c_A = din("c_A", [128, NT_OWN, 64])
    c_B = din("c_B", [128, NT_OWN, 64])
    c_cmask = din("c_cmask", [128, 2, HALF])
    c_M = din("c_M", [128, 2, 64])
    c_pidx = din("c_pidx", [128, 1])
    c_c8 = din("c_c8", [NPAGE, 8])
    s_cs = din("s_cs", [ST, 16])
    s_A = din("s_A", [ST, 192])
    s_B = din("s_B", [ST, 192])
    s_cmask = din("s_cmask", [128, 4])
    s_M = din("s_M", [128, 4, 192])
    s_wmask = din("s_wmask", [128, 5, ST])
    s_lmask = din("s_lmask", [128, ST])
    s_valid = din("s_valid", [128, 66])
    s_wvalid = din("s_wvalid", [128, 5])

    yp = dout("yp", [HALF, D])
    okv = dout("okv", [6, HALF, 128])
    pconv = dout("pconv", [30, 512])
    pmk = dout("pmk", [256, 512])
    pmv = dout("pmv", [256, 512])
    ys = dout("ys", [SB * ST, D])
    skv = dout("skv", [6, SB * ST, 128])
    swk = dout("swk", [SB, 512, 128])
    swv = dout("swv", [SB, 512, 128])
    sconv = dout("sconv", [SB, 30, 512])

    x1s = nc.dram_tensor("x1s", [HALF + SB * ST, D], F32, kind="Internal")
    oscr = nc.dram_tensor("oscr", [HALF + SB * ST, 512], F32, kind="Internal")
    gscr = [[nc.dram_tensor("gscr_%d_%d" % (b, X), [PAST, 128], F32, kind="Internal") for X in range(4)] for b in range(SB)]
    gtok = [[Tok() for X in range(4)] for b in range(SB)]

    glob = ExitStack()

    def sb(stack, name, shape, dt=F32):
        cnt[0] += 1
        return Buf(stack.enter_context(nc.sbuf_tensor("%s_%d" % (name, cnt[0]), list(shape), dt)))

    def ps(stack, name, shape, dt=F32):
        cnt[0] += 1
        return Buf(stack.enter_context(nc.psum_tensor("%s_%d" % (name, cnt[0]), list(shape), dt)))

    def V(fn, r, w):
        S.op('dve', fn, reads=r, writes=w)

    def A(fn, r, w):
        S.op('act', fn, reads=r, writes=w)

    def G(fn, r, w):
        S.op('pool', fn, reads=r, writes=w)

    def PE(fn, r, w, inc=True):
        S.op('pe', fn, reads=r, writes=w, inc=inc)

    def LD(out, in_, w, r=(), q='sp'):
        S.dma(q, out, in_, reads=r, writes=w)

    def LDC(out, in_, w):
        S.dma('pool', out, in_, writes=w)

    def ST_(out, in_, r, w=(), q='sp'):
        S.dma(q, out, in_, reads=r, writes=w)

    pT = Ring([ps(glob, "pT", [128, 8, 128], BF16) for _ in range(2)])
    pmm = Ring([ps(glob, "pmm", [128, 512], F32) for _ in range(3)])
    pacc = Ring([ps(glob, "pacc", [128, 512], F32) for _ in range(2)])
    paccT = Ring([ps(glob, "paccT", [128, 512], F32) for _ in range(1)])

    ident = sb(glob, "ident", [128, 128], BF16)
    identf = sb(glob, "identf", [128, 128], F32)
    LDC(ident[:, :], c_ident[:, :], [ident])
    LD(identf[:, :], c_ident[:, :], [identf])

    def mk_masks(stack):
        tri = sb(stack, "tri", [128, 2, 128], BF16)
        LDC(tri[:, :, :], c_tri[:, :, :], [tri])
        tri4 = sb(stack, "tri4", [128, 2, 4, 128], BF16)
        zeros_b = sb(stack, "zeros_b", [128, 512], BF16)
        S.op('pool', lambda e: e.memset(zeros_b[:, :], 0.0), writes=[zeros_b])
        for m_ in range(2):
            S.op('dve', lambda e, m_=m_: e.tensor_copy(out=tri4[:, m_, :, :], in_=tri[:, m_, :].unsqueeze(1).to_broadcast([128, 4, 128])), reads=[tri], writes=[tri4])
        return tri4, zeros_b
    gvd = {}

    def load_g(stack, idxs):
        for gi in idxs:
            t = sb(stack, "gv%d" % gi, [128, D], F32)
            LD(t[:, :], gvec[:, gi, :], [t])
            gvd[gi] = t

    wk = ExitStack()
    xt_r = Ring([sb(wk, "xt", [128, D], F32) for _ in range(2)])
    h_r = Ring([sb(wk, "h", [128, D], BF16) for _ in range(2)])
    hT_r = Ring([sb(wk, "hT", [128, 8, 128], BF16) for _ in range(2)])
    junk = sb(wk, "junk", [128, D], F32)
    st_r = Ring([sb(wk, "st", [128, 8], F32) for _ in range(4)])

    def load_w(stack, name, w_ap, K_, N_, c0=0, c1=None, defer=False):
        c1 = N_ if c1 is None else c1
        kc = K_ // 128
        t = sb(stack, name, [128, kc, c1 - c0], BF16)
        src = w_ap[:, c0:c1].rearrange("(kc p) n -> p kc n", p=128)
        t.ktoks = [Tok() for _ in range(kc)]

        def issue():
            for i in range(kc):
                LDC(t[:, i, :], src[:, i, :], [t.ktoks[i]])
        if defer:
            t.issue = issue
        else:
            issue()
        return t

    def rmsnorm(xt, nt, gi, out):
        st = st_r.next()
        gbuf = gvd[gi]
        A(lambda e: e.activation(out=junk[0:nt, :], in_=xt[0:nt, :], func=AF.Square, accum_out=st[0:nt, 0:1]), [xt], [junk, st])
        V(lambda e: e.tensor_scalar(out=st[0:nt, 1:2], in0=st[0:nt, 0:1], scalar1=1.0 / D, scalar2=EPS, op0=ALU.mult, op1=ALU.add), [st], [st])
        A(lambda e: e.activation(out=st[0:nt, 2:3], in_=st[0:nt, 1:2], func=AF.Sqrt), [st], [st])
        V(lambda e: e.reciprocal(out=st[0:nt, 3:4], in_=st[0:nt, 2:3]), [st], [st])
        V(lambda e: e.scalar_tensor_tensor(out=out[0:nt, :], in0=xt[0:nt, :], scalar=st[0:nt, 3:4], in1=gbuf[0:nt, :], op0=ALU.mult, op1=ALU.mult), [xt, st, gbuf], [out])

    def transpose_bf(src, nt, nchunk, dst, width=128, col0=0):
        for c0 in range(0, nchunk, 8):
            n = min(8, nchunk - c0)
            p = pT.next()
            for c in range(n):
                PE(lambda e, c=c: e.transpose(out=p[0:width, c, 0:nt], in_=src[0:nt, (c0 + c) * width:(c0 + c + 1) * width], identity=ident[0:nt, 0:nt]), [src, ident], [p])
            A(lambda e, n=n, c0=c0: e.copy(out=dst[0:width, c0:c0 + n, col0:col0 + nt], in_=p[0:width, 0:n, 0:nt]), [p], [dst])

    def linear(hT, nt, kc, W, c0, c1, consume):
        g0 = c0
        while g0 < c1:
            gn = min(512, c1 - g0)
            p = pmm.next()
            for i in range(kc):
                PE(lambda e, i=i, g0=g0, gn=gn, p=p: e.matmul(p[0:nt, 0:gn], lhsT=hT[:, i, 0:nt], rhs=W[:, i, g0:g0 + gn], start=(i == 0), stop=(i == kc - 1)), [hT, W.ktoks[i] if hasattr(W, 'ktoks') else W], [p], inc=(i == kc - 1))
            consume(p, g0, gn)
            g0 += gn

    def rope(z, nt, nh_view_fn, cs, tmp):
        x1 = nh_view_fn(0, 8)
        x2 = nh_view_fn(8, 16)
        shp = list(x1.shape)
        n = 1
        for s_ in shp[1:-1]:
            n *= s_

        def bc(ap):
            a = ap
            for _ in range(len(shp) - 2):
                a = a.unsqueeze(1)
            return a.to_broadcast(shp)
        cosb = bc(cs[0](0, 8))
        sinb = bc(cs[0](8, 16))

        def tv(i):
            v = tmp[0:nt, i, 0:n * 8]
            if len(shp) == 3:
                return v.rearrange("p (a d) -> p a d", d=8)
            return v.rearrange("p (a b d) -> p a b d", b=shp[2], d=8)
        rd = [z, cs[1], tmp]
        V(lambda e: e.tensor_tensor(out=tv(0), in0=x1, in1=cosb, op=ALU.mult), rd, [tmp])
        V(lambda e: e.tensor_tensor(out=tv(1), in0=x2, in1=sinb, op=ALU.mult), rd, [tmp])
        V(lambda e: e.tensor_tensor(out=tv(2), in0=x2, in1=cosb, op=ALU.mult), rd, [tmp])
        V(lambda e: e.tensor_tensor(out=tv(3), in0=x1, in1=sinb, op=ALU.mult), rd, [tmp])
        V(lambda e: e.tensor_tensor(out=x1, in0=tv(0), in1=tv(1), op=ALU.subtract), [tmp], [z])
        V(lambda e: e.tensor_tensor(out=x2, in0=tv(2), in1=tv(3), op=ALU.add), [tmp], [z])

    def attn_branch(ctx, nq, acc, hh, Qrows, kts, lhs_fn, v_fn, vw, mask_fn, extra_r, q_fn=None, heads=(0, 1, 2, 3), gstride=128, col0=0):
        Qa_, Pb_r_, accsb_r_ = ctx
        n_k = len(kts)
        nh = len(heads)
        g0 = heads[0]
        staged = {}

        def stage1(i):
            kt = kts[i]
            p = pmm.next()
            lhs, nk = lhs_fn(kt)
            rhs = q_fn(kt) if q_fn is not None else Qa_[0:Qrows, hh * 4:hh * 4 + 4, 0:nq]
            PE(lambda e, p=p, lhs=lhs, nk=nk, rhs=rhs: e.matmul(p[0:nk, 0:4 * nq].rearrange("k (g q) -> k g q", g=4), lhsT=lhs, rhs=rhs, start=True, stop=True), [Qa_] + extra_r, [p])
            Pb = Pb_r_.next()
            A(lambda e, p=p, Pb=Pb, nk=nk: e.activation(out=Pb[0:nk, :, 0:nq], in_=p[0:nk, 0:4 * nq].rearrange("k (g q) -> k g q", g=4), func=AF.Exp, scale=0.125), [p], [Pb])
            m = mask_fn(kt)
            if m is not None:
                mk_ap, mk_buf = m
                V(lambda e, Pb=Pb, nk=nk, mk_ap=mk_ap: e.tensor_tensor(out=Pb[0:nk, :, 0:nq], in0=Pb[0:nk, :, 0:nq], in1=mk_ap, op=ALU.mult), [Pb, mk_buf], [Pb])
            staged[i] = (Pb, nk)

        def stage2(i):
            kt = kts[i]
            Pb, nk = staged.pop(i)
            vap = v_fn(kt)
            PE(lambda e, Pb=Pb, nk=nk, vap=vap, i=i: e.matmul(accT[0:vw, 0:nh * nq].rearrange("v (g q) -> v g q", g=nh), lhsT=vap, rhs=Pb[0:nk, g0:g0 + nh, 0:nq], start=(i == 0), stop=(i == n_k - 1)), [Pb] + extra_r, [accT])

        if n_k <= 5:
            for i in range(n_k):
                stage1(i)
            tiles = [staged.pop(i) for i in range(n_k)]
            for gi, g in enumerate(heads):
                for i, (Pb, nk) in enumerate(tiles):
                    vap = v_fn(kts[i])
                    PE(lambda e, g=g, gi=gi, Pb=Pb, nk=nk, vap=vap, i=i: e.matmul(acc[0:nq, gi * gstride + col0:gi * gstride + col0 + vw], lhsT=Pb[0:nk, g, 0:nq], rhs=vap, start=(i == 0), stop=(i == n_k - 1)), [Pb] + extra_r, [acc], inc=(i == n_k - 1))
            return
        accT = paccT.next()
        LOOK = 2
        for i in range(min(LOOK, n_k)):
            stage1(i)
        for i in range(n_k):
            if i + LOOK < n_k:
                stage1(i + LOOK)
            stage2(i)
        asb = accsb_r_.next()
        A(lambda e: e.copy(out=asb[0:vw, 0:nh * nq], in_=accT[0:vw, 0:nh * nq]), [accT], [asb])
        for gi in range(nh):
            PE(lambda e, gi=gi: e.transpose(out=acc[0:nq, gi * gstride + col0:gi * gstride + col0 + vw], in_=asb[0:vw, gi * nq:(gi + 1) * nq], identity=identf[0:vw, 0:vw]), [asb, identf], [acc], inc=(gi == nh - 1))

    sper = ExitStack()
    cs_s = sb(sper, "cs_s", [ST, 16], F32)
    LD(cs_s[:, :], s_cs[:, :], [cs_s])
    KcT_s = [sb(sper, "KcTs%d" % b, [64, 2, 512], BF16) for b in range(SB)]
    Vca_s = [sb(sper, "Vcas%d" % b, [128, 4, 2, 258], BF16) for b in range(SB)]
    Ms = sb(sper, "Ms", [128, 4, 192], BF16)
    LDC(Ms[:, :, :], s_M[:, :, :], [Ms])
    pidx_t = sb(sper, "pidx_t", [NPAGE, SB], I32)
    LD(pidx_t[:, :], ptab.rearrange("b j -> j b"), [pidx_t])
    pidx_f = sb(sper, "pidx_f", [NPAGE, SB], F32)
    c8 = sb(sper, "c8", [NPAGE, 8], F32)
    pidx8 = sb(sper, "pidx8", [NPAGE, SB, 8], I32)
    LD(c8[:, :], c_c8[:, :], [c8])
    V(lambda e: e.tensor_copy(out=pidx_f[:, :], in_=pidx_t[:, :]), [pidx_t], [pidx_f])
    V(lambda e: e.tensor_scalar(out=pidx_f[:, :], in0=pidx_f[:, :], scalar1=8.0, scalar2=None, op0=ALU.mult), [pidx_f], [pidx_f])
    V(lambda e: e.tensor_tensor(out=pidx8[:, :, :], in0=pidx_f[:, :].unsqueeze(2).to_broadcast([NPAGE, SB, 8]), in1=c8[:, :].unsqueeze(1).to_broadcast([NPAGE, SB, 8]), op=ALU.add), [pidx_f, c8], [pidx8])

    def gather_page(b, j, X, stg):
        S.dma('sp', stg[:, :], gscr[b][X][j * 128:(j + 1) * 128, :], reads=[gtok[b][X]], writes=[stg])

    def pages_to_T(dst, dst_tok, stb4, n, pos0):
        p = pT.next()
        for jj in range(n):
            for hh in range(2):
                PE(lambda e, jj=jj, hh=hh, p=p: e.transpose(out=p[0:64, jj * 2 + hh, :], in_=stb4[:, jj, hh * 64:(hh + 1) * 64], identity=ident[:, :]), [stb4, ident], [p])
        for hh in range(2):
            A(lambda e, hh=hh, p=p: e.copy(out=dst(hh, pos0, pos0 + n * 128).rearrange("d (j p) -> d j p", p=128), in_=p[0:64, 0:2 * n, :].rearrange("d (j h) p -> d j h p", h=2)[:, :, hh, :]), [p], [dst_tok(hh) if callable(dst_tok) else dst_tok])

    def pages_to_T_full(dst, stb4, n, pos0):
        p = pT.next()
        for jj in range(n):
            PE(lambda e, jj=jj, p=p: e.transpose(out=p[:, jj, :], in_=stb4[:, jj, :], identity=ident[:, :]), [stb4, ident], [p])
        m0 = pos0 // 16
        A(lambda e, p=p: e.copy(out=dst[:, :, m0:m0 + 8 * n], in_=p[:, 0:n, :].rearrange("d j (mm r) -> d r (j mm)", r=16)), [p], [dst])

    def stream_pages(rings, groups, dstT=None, dstV=None, full=False):
        stg_r_, stb_r_ = rings
        for gi, (fill, n, j0) in enumerate(groups):
            stg4 = stg_r_.next()
            fill(stg4)
            if dstT is not None:
                stb4 = stb_r_.next()
                if gi % 2 == 0:
                    V(lambda e, stb4=stb4, stg4=stg4, n=n: e.tensor_copy(out=stb4[:, 0:n, :], in_=stg4[:, 0:n, :]), [stg4], [stb4])
                else:
                    A(lambda e, stb4=stb4, stg4=stg4, n=n: e.copy(out=stb4[:, 0:n, :], in_=stg4[:, 0:n, :]), [stg4], [stb4])
                if full:
                    pages_to_T_full(dstT[1], stb4, n, j0 * 128)
                else:
                    pages_to_T(dstT[0], dstT[1], stb4, n, j0 * 128)
            else:
                ap, tokb = dstV(j0, n)
                src = stg4[:, 0:n, :].rearrange("p j (h d) -> p j h d", h=2)
                if gi % 2 == 0:
                    V(lambda e, ap=ap, src=src: e.tensor_copy(out=ap, in_=src), [stg4], [tokb])
                else:
                    G(lambda e, ap=ap, src=src: e.tensor_copy(out=ap, in_=src), [stg4], [tokb])

    def pool_groups(b, X, r0):
        gs = []
        for j0 in range(0, NPAGE, 4):
            gs.append((lambda stg4, j0=j0: S.dma('sp', stg4[:, :, :], gscr[b][X][j0 * 128:(j0 + 4) * 128, :].rearrange("(j p) d -> p j d", p=128), reads=[gtok[b][X]], writes=[stg4]), 4, j0))

        def newtok(stg4):
            G(lambda e: e.memset(stg4[:, 0, :], 0.0), [], [stg4])
            LD(stg4[0:ST, 0, :], skv[X, r0:r0 + ST, :], [stg4])
        gs.append((newtok, 1, NPAGE))
        return gs

    def win_groups(b, cache, X, r0):
        def newtok(stg4):
            G(lambda e: e.memset(stg4[:, 0, :], 0.0), [], [stg4])
            LD(stg4[0:ST, 0, :], skv[X, r0:r0 + ST, :], [stg4])
        return [(lambda stg4: LD(stg4[:, :, :], cache[b, :, :].rearrange("(j p) d -> p j d", p=128), [stg4]), 4, 0), (newtok, 1, 4)]

    def emit_gathers(jobs, gst_ring, store_q):
        pend = []

        def store(b0, X0, c0, g0):
            S.dma(store_q, gscr[b0][X0].rearrange("(j r) d -> j (r d)", r=128)[:, c0 * 2048:(c0 + 1) * 2048], g0[:, :], reads=[g0], writes=[gtok[b0][X0]])
        for (b, X, c) in jobs:
            g = gst_ring.next()
            S.dma('pool', None, None, reads=[pidx8], writes=[g],
                  custom=lambda e, g=g, b=b, X=X, c=c: e.indirect_dma_start(
                      out=g[:, :], out_offset=None, in_=pools[X][:, :],
                      in_offset=bass.IndirectOffsetOnAxis(ap=pidx8[:, b, c:c + 1], axis=0)))
            pend.append((b, X, c, g))
            if len(pend) == 2:
                store(*pend.pop(0))
        for job in pend:
            store(*job)

    stores = ExitStack()
    KS = [sb(stores, "KS%d" % i, [128, S_FULL], BF16) for i in range(2)]
    KW = sb(stores, "KW", [64, 2, S_FULL], BF16)
    VS = sb(stores, "VS", [128, NT_ALL, 2, 66], BF16)
    VW = sb(stores, "VW", [128, NT_ALL, 2, 66], BF16)
    KcT = sb(stores, "KcT", [64, 2, 256], BF16)
    Vca = sb(stores, "Vca", [128, 2, 2, 130], BF16)
    cs_p = sb(stores, "cs_p", [128, NT_ALL, 16], F32)
    valid = sb(stores, "valid", [128, NT_ALL], F32)
    LD(cs_p[:, :, :], c_cs[:, :, :], [cs_p])
    LD(valid[:, :], c_valid[:, :], [valid])
    for i in range(2):
        LDC(KS[i][64:128, :], c_E[:, :], [KS[i]])
        V(lambda e, i=i: e.tensor_copy(out=VS[:, :, i, 64], in_=valid[:, :]), [valid], [VS])
        V(lambda e, i=i: e.tensor_copy(out=VW[:, :, i, 64], in_=valid[:, :]), [valid], [VW])
    Mc = sb(stores, "Mc", [128, 2, 64], BF16)
    LDC(Mc[:, :, :], c_M[:, :, :], [Mc])

    cw = ExitStack()
    w1 = []
    w2 = []
    cb = sb(cw, "cbias", [128, 2, 2], F32)
    for kv in range(2):
        t = sb(cw, "w1_%d" % kv, [128, 32, 128], BF16)
        LDC(t[0:64, :, :], w_c1[kv].rearrange("(j d) n -> d j n", d=64), [t])
        LDC(t[64:128, :, :], w_c1[kv].rearrange("(j d) n -> d j n", d=64), [t])
        w1.append(t)
        t2 = sb(cw, "w2_%d" % kv, [128, 64], BF16)
        LDC(t2[:, :], w_c2[kv][:, :], [t2])
        w2.append(t2)
    peT = sb(cw, "peT", [64, 2, 32], BF16)
    for kv in range(2):
        LDC(peT[:, kv, :], cpos[kv].rearrange("j d -> d j"), [peT])
    for kv in range(2):
        p = pmm.next()
        for j in range(32):
            PE(lambda e, j=j, kv=kv, p=p: e.matmul(p[:, 0:1], lhsT=w1[kv][0:64, j, :], rhs=peT[:, kv, j:j + 1], start=(j == 0), stop=(j == 31)), [w1[kv], peT], [p], inc=(j == 31))
        V(lambda e, kv=kv, p=p: e.tensor_copy(out=cb[:, kv, 0:1], in_=p[:, 0:1]), [p], [cb])
    gtmp = sb(cw, "gtmp", [128, 3, 512], F32)
    hid = sb(cw, "hid", [128, 512], BF16)

    cstk = ExitStack()
    KC = [sb(cstk, "KCT%d" % i, [128, 16, S_FULL // 16], BF16) for i in range(2)]

    with ExitStack() as ph:
        load_g(ph, [0])
        wkv = load_w(ph, "wkv", w_in, D, IN_W, 512, 1280)
        gst_r = Ring([sb(ph, "gst", [NPAGE, 2048], F32) for _ in range(3)])
        emit_gathers([(b, X, c) for X in (0, 1) for b in range(SB) for c in range(8)], gst_r, 'pool')
        zkv_r = Ring([sb(ph, "zkv", [128, 768], F32) for _ in range(2)])
        zb_r = Ring([sb(ph, "zb", [128, 768], BF16) for _ in range(2)])
        rtmp = sb(ph, "rtmp", [128, 4, 64], F32)

        def a1_pre(src_ap, nt):
            xt = xt_r.next()
            LD(xt[0:nt, :], src_ap, [xt])
            h = h_r.next()
            rmsnorm(xt, nt, 0, h)
            hT = hT_r.next()
            transpose_bf(h, nt, 8, hT)
            return hT

        def a1_post(hT, nt, cs_fn, cs_buf, zkv):
            def cons(p, g0, gn):
                A(lambda e: e.copy(out=zkv[0:nt, g0:g0 + gn], in_=p[0:nt, 0:gn]), [p], [zkv])
            linear(hT, nt, 8, wkv, 0, 768, cons)
            zv = zkv[0:nt, :].rearrange("p (s kv h d) -> p s kv h d", s=3, kv=2, h=2)
            rope(zkv, nt, lambda lo, hi: zv[:, :, 0, :, lo:hi], (cs_fn, cs_buf), rtmp)

        def a1_tile(src_ap, nt, cs_fn, cs_buf, zkv):
            a1_post(a1_pre(src_ap, nt), nt, cs_fn, cs_buf, zkv)

        hT_next = a1_pre(xp[0:128, :], 128)
        for ti in range(NT_ALL):
            zkv = zkv_r.next()
            hT_cur = hT_next
            if ti + 1 < NT_ALL:
                hT_next = a1_pre(xp[(ti + 1) * 128:(ti + 2) * 128, :], 128)
            a1_post(hT_cur, 128, lambda lo, hi, ti=ti: cs_p[:, ti, lo:hi], cs_p, zkv)
            if ti >= NT_OWN:
                t0 = (ti - NT_OWN) * 128
                for s6 in range(6):
                    ST_(okv[s6, t0:t0 + 128, :], zkv[:, s6 * 128:(s6 + 1) * 128], [zkv], q='sp')
            zb = zb_r.next()
            V(lambda e, zb=zb, zkv=zkv: e.tensor_copy(out=zb[:, :], in_=zkv[:, :]), [zkv], [zb])
            p = pT.next()
            for j, s_ in enumerate((0, 1)):
                PE(lambda e, j=j, s_=s_, p=p, zb=zb: e.transpose(out=p[:, j, :], in_=zb[:, s_ * 128:(s_ + 1) * 128], identity=ident[:, :]), [zb, ident], [p])
            for j, s_ in enumerate((2, 4)):
                for hh in range(2):
                    PE(lambda e, j=j, s_=s_, hh=hh, p=p, zb=zb: e.transpose(out=p[0:64, 2 + j * 2 + hh, :], in_=zb[:, s_ * 128 + hh * 64:s_ * 128 + hh * 64 + 64], identity=ident[:, :]), [zb, ident], [p])
            sl = slice(ti * 128, (ti + 1) * 128)
            for x_ in range(2):
                A(lambda e, p=p, ti=ti, x_=x_: e.copy(out=KC[x_][:, :, ti * 8:(ti + 1) * 8], in_=p[:, x_, :].rearrange("d (m r) -> d r m", r=16)), [p], [KC[x_]])
            for hh in range(2):
                A(lambda e, p=p, sl=sl, hh=hh: e.copy(out=KS[hh][0:64, sl], in_=p[0:64, 2 + hh, :]), [p], [KS[hh]])
            A(lambda e, p=p, sl=sl: e.copy(out=KW[:, :, sl], in_=p[0:64, 4:6, :]), [p], [KW])
            V(lambda e, zb=zb, ti=ti: e.tensor_copy(out=VS[:, ti, :, 0:64], in_=zb[:, 384:512].rearrange("p (h d) -> p h d", h=2)), [zb], [VS])
            V(lambda e, zb=zb, ti=ti: e.tensor_copy(out=VW[:, ti, :, 0:64], in_=zb[:, 640:768].rearrange("p (h d) -> p h d", h=2)), [zb], [VW])

        NS = SB * ST
        cs16 = sb(ph, "cs16", [NS, 16], F32)
        for b in range(SB):
            LD(cs16[b * ST:(b + 1) * ST, :], s_cs[:, :], [cs16])
        zkv = zkv_r.next()
        a1_tile(xs[:, :], NS, lambda lo, hi: cs16[:, lo:hi], cs16, zkv)
        for s6 in range(6):
            ST_(skv[s6, :, :], zkv[0:NS, s6 * 128:(s6 + 1) * 128], [zkv], q='sp')
        for b in range(SB):
            ST_(swk[b, 0:512 - ST, :], ckw[b, ST:512, :], [], q='sp')
            ST_(swv[b, 0:512 - ST, :], cvw[b, ST:512, :], [], q='sp')
            ST_(swk[b, 512 - ST:512, :], zkv[b * ST:(b + 1) * ST, 512:640], [zkv], q='sp')
            ST_(swv[b, 512 - ST:512, :], zkv[b * ST:(b + 1) * ST, 640:768], [zkv], q='sp')
    S.barrier()

    def gelu_to(out_bf, p, n, bias, tmp):
        x = tmp[:, 0, 0:n]
        A(lambda e: e.activation(out=x, in_=p[:, 0:n], func=AF.Identity, bias=bias), [p], [tmp])
        V(lambda e: e.tensor_tensor(out=tmp[:, 1, 0:n], in0=x, in1=x, op=ALU.mult), [tmp], [tmp])
        V(lambda e: e.tensor_scalar(out=tmp[:, 1, 0:n], in0=tmp[:, 1, 0:n], scalar1=0.044715 * 1.5957691216, scalar2=1.5957691216, op0=ALU.mult, op1=ALU.add), [tmp], [tmp])
        V(lambda e: e.tensor_tensor(out=tmp[:, 1, 0:n], in0=tmp[:, 1, 0:n], in1=x, op=ALU.mult), [tmp], [tmp])
        A(lambda e: e.activation(out=tmp[:, 2, 0:n], in_=tmp[:, 1, 0:n], func=AF.Sigmoid), [tmp], [tmp])
        V(lambda e: e.tensor_tensor(out=out_bf, in0=tmp[:, 2, 0:n], in1=x, op=ALU.mult), [tmp], [out_bf_owner[0]])

    out_bf_owner = [None]

    def compress(KCk, KCv, nblk, KcT_out, Vca_out, Msb, nsel):
        nct = (nblk + 127) // 128
        for hh in range(2):
            for kv, KCx in ((0, KCk), (1, KCv)):
                p = pmm.next()
                for j in range(32):
                    PE(lambda e, j=j, kv=kv, hh=hh, p=p, KCx=KCx: e.matmul(p[:, 0:nblk], lhsT=w1[kv][hh * 64:(hh + 1) * 64, j, :], rhs=KCx[hh * 64:(hh + 1) * 64, j % 16, j // 16:j // 16 + nblk], start=(j == 0), stop=(j == 31)), [w1[kv], KCx], [p], inc=(j == 31))
                out_bf_owner[0] = hid
                gelu_to(hid[:, 0:nblk], p, nblk, cb[:, kv, 0:1], gtmp)
                if kv == 0:
                    p2 = pmm.next()
                    PE(lambda e, p2=p2: e.matmul(p2[0:64, 0:nblk], lhsT=w2[0][:, :], rhs=hid[:, 0:nblk], start=True, stop=True), [w2[0], hid], [p2])
                    A(lambda e, p2=p2, hh=hh: e.copy(out=KcT_out[:, hh, 0:nblk], in_=p2[0:64, 0:nblk]), [p2], [KcT_out])
                else:
                    for ct in range(nct):
                        n = min(128, nblk - ct * 128)
                        p2 = pmm.next()
                        PE(lambda e, p2=p2, ct=ct, n=n: e.matmul(p2[0:n, 0:64], lhsT=hid[:, ct * 128:ct * 128 + n], rhs=w2[1][:, :], start=True, stop=True), [w2[1], hid], [p2])
                        A(lambda e, p2=p2, ct=ct, n=n, hh=hh: e.copy(out=Vca_out[0:n, ct, hh, 0:64], in_=p2[0:n, 0:64]), [p2], [Vca_out])
        for hh in range(2):
            V(lambda e, hh=hh: e.tensor_copy(out=Vca_out[:, :, hh, 64:64 + nsel], in_=Msb[:, :, :]), [Msb], [Vca_out])
            G(lambda e, hh=hh: e.memset(Vca_out[:, :, hh, 64 + nsel:64 + nsel + 1], 1.0), [], [Vca_out])

    G(lambda e: e.memset(KcT[:, :, :], 0.0), [], [KcT])
    G(lambda e: e.memset(Vca[:, :, :, :], 0.0), [], [Vca])
    compress(KC[0], KC[1], 255, KcT, Vca, Mc, 64)
    S.barrier()
    cstk.close()

    with ExitStack() as ph:
        KCs = [sb(ph, "KCs%d" % i, [128, 16, 520], BF16) for i in range(2)]
        rings = (Ring([sb(ph, "stg4", [128, 4, 128], F32) for _ in range(3)]), Ring([sb(ph, "stb4", [128, 4, 128], BF16) for _ in range(3)]))
        for b in range(SB):
            for X in range(2):
                stream_pages(rings, pool_groups(b, X, b * ST), dstT=(None, KCs[X]), full=True)
            G(lambda e, b=b: e.memset(KcT_s[b][:, :, :], 0.0), [], [KcT_s[b]])
            G(lambda e, b=b: e.memset(Vca_s[b][:, :, :, :], 0.0), [], [Vca_s[b]])
            compress(KCs[0], KCs[1], 511, KcT_s[b], Vca_s[b], Ms, 192)
    S.barrier()
    cw.close()

    with ExitStack() as ph:
        load_g(ph, [0])
        tri4, zeros_b = mk_masks(ph)
        wq = load_w(ph, "wq", w_in, D, IN_W, 0, 512)
        wg = load_w(ph, "wg", w_in, D, IN_W, 1280, 1304)
        cm_r = Ring([sb(ph, "cmask", [128, 2, 128], BF16) for _ in range(2)])
        cm4_r = Ring([sb(ph, "cmask4", [128, 2, 4, 128], BF16) for _ in range(2)])
        AB_r = Ring([sb(ph, "AB", [128, 2, 64], F32) for _ in range(2)])
        zq = sb(ph, "zq", [128, 512], F32)
        zqb = sb(ph, "zqb", [128, 512], BF16)
        Qa = sb(ph, "Qa", [128, 8, 128], BF16)
        gsig = sb(ph, "gsig", [128, 24], F32)
        rtmp = sb(ph, "rtmpb", [128, 4, 64], F32)
        Pb_r = Ring([sb(ph, "Pb", [128, 4, 128], BF16) for _ in range(8)])
        acc_c = sb(ph, "acc_c", [128, 4, 130], F32)
        impt = sb(ph, "impt", [128, 4, 64], F32)
        m8 = sb(ph, "m8", [128, 16], F32)
        nbw = sb(ph, "nbw", [128, 128], BF16)
        G(lambda e: e.memset(nbw[:, :], 0.0), [], [nbw])
        osb_r = Ring([sb(ph, "osb", [128, 512], F32) for _ in range(2)])
        otmp = sb(ph, "otmp", [128, 4, 64], F32)
        cst = sb(ph, "cst", [128, 16], F32)
        cst2 = [sb(ph, "cst2_%d" % i, [128, 16], F32) for i in range(2)]
        impt2 = [sb(ph, "impt2_%d" % i, [128, 4, 64], F32) for i in range(2)]
        m82 = [sb(ph, "m82_%d" % i, [128, 16], F32) for i in range(2)]
        nbw2 = [sb(ph, "nbw2_%d" % i, [128, 128], BF16) for i in range(2)]
        acc_c2 = [sb(ph, "acc_c2_%d" % i, [128, 4, 130], F32) for i in range(2)]
        for i in range(2):
            G(lambda e, i=i: e.memset(nbw2[i][:, :], 0.0), [], [nbw2[i]])
        accsb_r = Ring([sb(ph, "accsb", [128, 512], F32) for _ in range(2)])
        actx = (Qa, Pb_r, accsb_r)
        gst2_r = Ring([sb(ph, "gst2", [NPAGE, 2048], F32) for _ in range(3)])
        slc_jobs = [(b, X, c) for X in (2, 3) for b in range(SB) for c in range(8)]

        for qt in range(NT_OWN):
            ti = NT_OWN + qt
            emit_gathers(slc_jobs[qt * 4:(qt + 1) * 4], gst2_r, 'pool')
            xt = xt_r.next()
            LD(xt[:, :], xp[ti * 128:(ti + 1) * 128, :], [xt])
            cm = cm_r.next()
            LDC(cm[:, :, :], c_cmask[:, :, qt * 128:(qt + 1) * 128], [cm])
            cm4 = cm4_r.next()
            for ct_ in range(2):
                V(lambda e, ct_=ct_, cm=cm, cm4=cm4: e.tensor_copy(out=cm4[:, ct_, :, :], in_=cm[:, ct_, :].unsqueeze(1).to_broadcast([128, 4, 128])), [cm], [cm4])
            AB = AB_r.next()
            LD(AB[:, 0, :], c_A[:, qt, :], [AB])
            LD(AB[:, 1, :], c_B[:, qt, :], [AB])
            h = h_r.next()
            rmsnorm(xt, 128, 0, h)
            hT = hT_r.next()
            transpose_bf(h, 128, 8, hT)

            def consq(p, g0, gn):
                A(lambda e: e.copy(out=zq[:, :], in_=p[:, 0:512]), [p], [zq])
            linear(hT, 128, 8, wq, 0, 512, consq)
            zqv = zq[:, :].rearrange("p (h d) -> p h d", h=8)
            rope(zq, 128, lambda lo, hi: zqv[:, :, lo:hi], (lambda lo, hi, ti=ti: cs_p[:, ti, lo:hi], cs_p), rtmp)
            V(lambda e: e.tensor_copy(out=zqb[:, :], in_=zq[:, :]), [zq], [zqb])
            transpose_bf(zqb, 128, 8, Qa, width=64)

            def consg(p, g0, gn):
                A(lambda e: e.activation(out=gsig[:, :], in_=p[:, 0:24], func=AF.Sigmoid), [p], [gsig])
            linear(hT, 128, 8, wg, 0, 24, consg)

            osb = osb_r.next()
            def gview_of(hh):
                return gsig[:, hh * 12:(hh + 1) * 12].rearrange("p (g t) -> p g t", t=3)

            def ov_of(hh):
                return osb[:, hh * 256:(hh + 1) * 256].rearrange("p (g d) -> p g d", g=4)

            def consume(hh, bi, accX):
                av = accX[:, :].rearrange("p (g n) -> p g n", g=4)
                cs_ = cst2[hh]
                gv_ = gview_of(hh)[:, :, bi]
                ov_ = ov_of(hh)
                V(lambda e: e.tensor_scalar(out=cs_[:, 8:12], in0=av[:, :, 64], scalar1=1e-30, scalar2=None, op0=ALU.max), [accX], [cs_])
                V(lambda e: e.reciprocal(out=cs_[:, 12:16], in_=cs_[:, 8:12]), [cs_], [cs_])
                V(lambda e: e.tensor_tensor(out=cs_[:, 12:16], in0=cs_[:, 12:16], in1=gv_, op=ALU.mult), [cs_, gsig], [cs_])
                V(lambda e: e.tensor_tensor(out=otmp[:, :, :], in0=av[:, :, 0:64], in1=cs_[:, 12:16].unsqueeze(2).to_broadcast([128, 4, 64]), op=ALU.mult), [accX, cs_], [otmp])
                V(lambda e: e.tensor_tensor(out=ov_, in0=ov_, in1=otmp[:, :, :], op=ALU.add), [osb, otmp], [osb])

            for hh in range(2):
                accC = pacc.next()
                attn_branch(actx, 128, accC, hh, 64, [0, 1],
                            lambda ct, hh=hh: (KcT[:, hh, ct * 128:(ct + 1) * 128], 128),
                            lambda ct, hh=hh: Vca[:, ct, hh, 0:128], 128,
                            lambda ct, cm4=cm4: (cm4[:, ct, :, :], cm4), [KcT, Vca])
                accv = accC[:, :].rearrange("p (g n) -> p g n", g=4)
                cs_ = cst2[hh]
                im_ = impt2[hh]
                m8_ = m82[hh]
                nb_ = nbw2[hh]
                ac_ = acc_c2[hh]
                V(lambda e: e.tensor_reduce(out=cs_[:, 0:4], in_=accv[:, :, 64:128], axis=AX.X, op=ALU.add), [accC], [cs_])
                V(lambda e: e.tensor_scalar(out=cs_[:, 0:4], in0=cs_[:, 0:4], scalar1=1e-30, scalar2=None, op0=ALU.max), [cs_], [cs_])
                V(lambda e: e.reciprocal(out=cs_[:, 4:8], in_=cs_[:, 0:4]), [cs_], [cs_])
                V(lambda e: e.tensor_tensor(out=ac_[:, :, 0:128], in0=accv[:, :, 0:128], in1=cs_[:, 4:8].unsqueeze(2).to_broadcast([128, 4, 128]), op=ALU.mult), [accC, cs_], [ac_])
                ov_ = ov_of(hh)
                gv0_ = gview_of(hh)[:, :, 0:1].to_broadcast([128, 4, 64])
                V(lambda e: e.tensor_tensor(out=ov_, in0=ac_[:, :, 0:64], in1=gv0_, op=ALU.mult), [ac_, gsig], [osb])
                V(lambda e: e.tensor_reduce(out=im_[:, 0, :], in_=ac_[:, :, 64:128].rearrange("p g n -> p n g"), axis=AX.X, op=ALU.add), [ac_], [im_])
                V(lambda e, AB=AB: e.tensor_tensor(out=im_[:, 0, :], in0=im_[:, 0, :], in1=AB[:, 0, :], op=ALU.mult), [im_, AB], [im_])
                V(lambda e, AB=AB: e.tensor_tensor(out=im_[:, 0, :], in0=im_[:, 0, :], in1=AB[:, 1, :], op=ALU.add), [im_, AB], [im_])
                V(lambda e: e.max(out=m8_[:, 0:8], in_=im_[:, 0, :]), [im_], [m8_])
                V(lambda e: e.match_replace(out=im_[:, 1, :], in_to_replace=m8_[:, 0:8], in_values=im_[:, 0, :], imm_value=-1e30), [im_, m8_], [im_])
                V(lambda e: e.max(out=m8_[:, 8:16], in_=im_[:, 1, :]), [im_], [m8_])
                V(lambda e: e.tensor_scalar(out=m8_[:, 15:16], in0=m8_[:, 15:16], scalar1=-0.5, scalar2=None, op0=ALU.max), [m8_], [m8_])
                V(lambda e: e.tensor_scalar(out=im_[:, 2, :], in0=im_[:, 0, :], scalar1=m8_[:, 15:16], scalar2=None, op0=ALU.is_ge), [im_, m8_], [im_])
                V(lambda e: e.tensor_scalar(out=nb_[:, 64:128], in0=im_[:, 2, :], scalar1=-1.0, scalar2=-NEG, op0=ALU.add, op1=ALU.mult), [im_], [nb_])
            for hh in range(2):
                accW = pacc.next()
                attn_branch(actx, 128, accW, hh, 64, list(range(ti - 4, ti + 1)),
                            lambda kt, hh=hh: (KW[:, hh, kt * 128:(kt + 1) * 128], 128),
                            lambda kt, hh=hh: VW[:, kt, hh, 0:65], 65,
                            lambda kt, ti=ti: (tri4[:, 0, :, :], tri4) if kt == ti else ((tri4[:, 1, :, :], tri4) if kt == ti - 4 else None), [KW, VW])
                consume(hh, 2, accW)
            for hh in range(2):
                p = pT.next()
                PE(lambda e, p=p, hh=hh: e.transpose(out=p[:, 0, :], in_=nbw2[hh][:, :], identity=ident[:, :]), [nbw2[hh], ident], [p])
                for g in range(4):
                    A(lambda e, p=p, hh=hh, g=g: e.copy(out=Qa[64:128, hh * 4 + g, :], in_=p[64:128, 0, :]), [p], [Qa])
            for hh in range(2):
                accS = pacc.next()
                attn_branch(actx, 128, accS, hh, 128, list(range(0, ti + 1)),
                            lambda kt, hh=hh: (KS[hh][:, kt * 128:(kt + 1) * 128], 128),
                            lambda kt, hh=hh: VS[:, kt, hh, 0:65], 65,
                            lambda kt, ti=ti: (tri4[:, 0, :, :], tri4) if kt == ti else None, [KS[hh], VS])
                consume(hh, 1, accS)
            ST_(oscr[qt * 128:(qt + 1) * 128, :], osb[:, :], [osb], q='sp')
    S.barrier()
    stores.close()

    with ExitStack() as ph:
        load_g(ph, [0])
        tri4, zeros_b = mk_masks(ph)
        wq = load_w(ph, "wq", w_in, D, IN_W, 0, 512)
        wg = load_w(ph, "wg", w_in, D, IN_W, 1280, 1304)
        KSs = [sb(ph, "KSs%d" % i, [128, 8320], BF16) for i in range(2)]
        VSs = sb(ph, "VSs", [128, 66, 2, 66], BF16)
        KWs = sb(ph, "KWs", [64, 2, 640], BF16)
        VWs = sb(ph, "VWs", [128, 5, 2, 66], BF16)
        for i in range(2):
            LDC(KSs[i][64:128, 0:4096], c_E[:, :], [KSs[i]])
            LDC(KSs[i][64:128, 4096:8192], c_E[:, :], [KSs[i]])
            LDC(KSs[i][64:128, 8192:8320], c_E[:, 0:128], [KSs[i]])
        svalid = sb(ph, "svalid", [128, 66], F32)
        swvalid = sb(ph, "swvalid", [128, 5], F32)
        LD(svalid[:, :], s_valid[:, :], [svalid])
        LD(swvalid[:, :], s_wvalid[:, :], [swvalid])
        for i in range(2):
            V(lambda e, i=i: e.tensor_copy(out=VSs[:, :, i, 64], in_=svalid[:, :]), [svalid], [VSs])
            V(lambda e, i=i: e.tensor_copy(out=VWs[:, :, i, 64], in_=swvalid[:, :]), [swvalid], [VWs])
        m1 = sb(ph, "m1", [128, 10, ST], BF16)
        m4 = sb(ph, "m4", [128, 10, 4, ST], BF16)
        scm = sb(ph, "scm", [128, 4], F32)
        LD(scm[:, :], s_cmask[:, :], [scm])
        for ct in range(4):
            V(lambda e, ct=ct: e.tensor_copy(out=m1[:, ct, :], in_=scm[:, ct:ct + 1].to_broadcast([128, ST])), [scm], [m1])
        LDC(m1[:, 4:9, :], s_wmask[:, :, :], [m1])
        LDC(m1[:, 9, :], s_lmask[:, :], [m1])
        for i in range(10):
            V(lambda e, i=i: e.tensor_copy(out=m4[:, i, :, :], in_=m1[:, i, :].unsqueeze(1).to_broadcast([128, 4, ST])), [m1], [m4])
        ABs = sb(ph, "ABs", [ST, 2, 192], F32)
        LD(ABs[:, 0, :], s_A[:, :], [ABs])
        LD(ABs[:, 1, :], s_B[:, :], [ABs])
        rings = (Ring([sb(ph, "stg4", [128, 4, 128], F32) for _ in range(3)]), Ring([sb(ph, "stb4", [128, 4, 128], BF16) for _ in range(3)]))
        zq = sb(ph, "zq", [128, 512], F32)
        zqb = sb(ph, "zqb", [128, 512], BF16)
        QaS = sb(ph, "QaS", [128, 3, 8, ST], BF16)
        qT = sb(ph, "qT", [128, 8, 128], BF16)
        gsig = sb(ph, "gsig", [128, 24], F32)
        rtmp = sb(ph, "rtmpb", [128, 4, 64], F32)
        Pb_r = Ring([sb(ph, "Pb", [128, 4, 128], BF16) for _ in range(8)])
        acc_c = sb(ph, "acc_c", [ST, 4, 256], F32)
        impt = sb(ph, "impt", [ST, 4, 192], F32)
        m8 = sb(ph, "m8", [ST, 16], F32)
        nbw = sb(ph, "nbw", [ST, 3, 128], BF16)
        G(lambda e: e.memset(nbw[:, :, :], 0.0), [], [nbw])
        osb = sb(ph, "osb", [128, 512], F32)
        otmp = sb(ph, "otmp", [128, 4, 64], F32)
        cst = sb(ph, "cst", [128, 16], F32)
        accsb_r = Ring([sb(ph, "accsb", [128, 512], F32) for _ in range(2)])
        actx = (QaS, Pb_r, accsb_r)

        for b in range(SB):
            r0 = b * ST
            stream_pages(rings, pool_groups(b, 2, r0), dstT=(lambda hh, lo, hi: KSs[hh][0:64, lo:hi], lambda hh: KSs[hh]))
            stream_pages(rings, pool_groups(b, 3, r0), dstV=lambda j0, n: (VSs[:, j0:j0 + n, :, 0:64], VSs))
            stream_pages(rings, win_groups(b, ckw, 4, r0), dstT=(lambda hh, lo, hi: KWs[:, hh, lo:hi], KWs))
            stream_pages(rings, win_groups(b, cvw, 5, r0), dstV=lambda j0, n: (VWs[:, j0:j0 + n, :, 0:64], VWs))

            xt = xt_r.next()
            LD(xt[0:ST, :], xs[r0:r0 + ST, :], [xt])
            h = h_r.next()
            rmsnorm(xt, ST, 0, h)
            hT = hT_r.next()
            transpose_bf(h, ST, 8, hT)

            def consq(p, g0, gn):
                A(lambda e: e.copy(out=zq[0:ST, :], in_=p[0:ST, 0:512]), [p], [zq])
            linear(hT, ST, 8, wq, 0, 512, consq)
            zqv = zq[0:ST, :].rearrange("p (h d) -> p h d", h=8)
            rope(zq, ST, lambda lo, hi: zqv[:, :, lo:hi], (lambda lo, hi: cs_s[:, lo:hi], cs_s), rtmp)
            V(lambda e: e.tensor_copy(out=zqb[0:ST, :], in_=zq[0:ST, :]), [zq], [zqb])
            transpose_bf(zqb, ST, 8, qT, width=64)
            for v in range(3):
                V(lambda e, v=v: e.tensor_copy(out=QaS[0:64, v, :, :], in_=qT[0:64, :, 0:ST]), [qT], [QaS])

            def consg(p, g0, gn):
                A(lambda e: e.activation(out=gsig[0:ST, :], in_=p[0:ST, 0:24], func=AF.Sigmoid), [p], [gsig])
            linear(hT, ST, 8, wg, 0, 24, consg)

            for hh in range(2):
                for pair in range(2):
                    accC = pacc.next()
                    for vb in range(2):
                        attn_branch(actx, ST, accC, hh, 64, [0, 1, 2, 3],
                                    lambda ct, hh=hh: (KcT_s[b][:, hh, ct * 128:(ct + 1) * 128], 128),
                                    lambda ct, hh=hh, vb=vb: Vca_s[b][:, ct, hh, vb * 128:(vb + 1) * 128], 128,
                                    lambda ct: (m4[:, ct, :, :], m4), [KcT_s[b], Vca_s[b]],
                                    q_fn=lambda kt, hh=hh: QaS[0:64, 0, hh * 4:hh * 4 + 4, :],
                                    heads=(pair * 2, pair * 2 + 1), gstride=256, col0=vb * 128)
                    accv = accC[0:ST, :].rearrange("p (g n) -> p g n", g=2)
                    V(lambda e, accv=accv, pair=pair: e.tensor_reduce(out=cst[0:ST, pair * 2:pair * 2 + 2], in_=accv[:, :, 64:256], axis=AX.X, op=ALU.add), [accC], [cst])
                    V(lambda e, accv=accv, pair=pair: e.tensor_copy(out=acc_c[:, pair * 2:pair * 2 + 2, :], in_=accv), [accC], [acc_c])
                V(lambda e: e.tensor_scalar(out=cst[0:ST, 0:4], in0=cst[0:ST, 0:4], scalar1=1e-30, scalar2=None, op0=ALU.max), [cst], [cst])
                V(lambda e: e.reciprocal(out=cst[0:ST, 4:8], in_=cst[0:ST, 0:4]), [cst], [cst])
                V(lambda e: e.tensor_tensor(out=acc_c[:, :, :], in0=acc_c[:, :, :], in1=cst[0:ST, 4:8].unsqueeze(2).to_broadcast([ST, 4, 256]), op=ALU.mult), [acc_c, cst], [acc_c])
                V(lambda e: e.tensor_reduce(out=impt[:, 0, :], in_=acc_c[:, :, 64:256].rearrange("p g n -> p n g"), axis=AX.X, op=ALU.add), [acc_c], [impt])
                V(lambda e: e.tensor_tensor(out=impt[:, 0, :], in0=impt[:, 0, :], in1=ABs[:, 0, :], op=ALU.mult), [impt, ABs], [impt])
                V(lambda e: e.tensor_tensor(out=impt[:, 0, :], in0=impt[:, 0, :], in1=ABs[:, 1, :], op=ALU.add), [impt, ABs], [impt])
                V(lambda e: e.max(out=m8[:, 0:8], in_=impt[:, 0, :]), [impt], [m8])
                V(lambda e: e.match_replace(out=impt[:, 1, :], in_to_replace=m8[:, 0:8], in_values=impt[:, 0, :], imm_value=-1e30), [impt, m8], [impt])
                V(lambda e: e.max(out=m8[:, 8:16], in_=impt[:, 1, :]), [impt], [m8])
                V(lambda e: e.tensor_scalar(out=m8[:, 15:16], in0=m8[:, 15:16], scalar1=-0.5, scalar2=None, op0=ALU.max), [m8], [m8])
                V(lambda e: e.tensor_scalar(out=impt[:, 2, :], in0=impt[:, 0, :], scalar1=m8[:, 15:16], scalar2=None, op0=ALU.is_ge), [impt, m8], [impt])
                V(lambda e: e.tensor_scalar(out=nbw[:, :, 64:128], in0=impt[:, 2, :].rearrange("p (v n) -> p v n", v=3), scalar1=-1.0, scalar2=-NEG, op0=ALU.add, op1=ALU.mult), [impt], [nbw])
                p = pT.next()
                for v in range(3):
                    PE(lambda e, p=p, v=v: e.transpose(out=p[:, v, 0:ST], in_=nbw[0:ST, v, :], identity=ident[0:ST, 0:ST]), [nbw, ident], [p])
                for v in range(3):
                    for g in range(4):
                        A(lambda e, p=p, hh=hh, g=g, v=v: e.copy(out=QaS[64:128, v, hh * 4 + g, :], in_=p[64:128, v, 0:ST]), [p], [QaS])
                accS = pacc.next()
                attn_branch(actx, ST, accS, hh, 128, list(range(0, 65)),
                            lambda kt, hh=hh: (KSs[hh][:, kt * 128:(kt + 1) * 128], 128),
                            lambda kt, hh=hh: VSs[:, kt, hh, 0:65], 65,
                            lambda kt: (m4[:, 9, :, :], m4) if kt == 64 else None, [KSs[hh], VSs],
                            q_fn=lambda kt, hh=hh: QaS[:, min(kt // 32, 2), hh * 4:hh * 4 + 4, :])
                accW = pacc.next()
                attn_branch(actx, ST, accW, hh, 64, list(range(5)),
                            lambda kt, hh=hh: (KWs[:, hh, kt * 128:(kt + 1) * 128], 128),
                            lambda kt, hh=hh: VWs[:, kt, hh, 0:65], 65,
                            lambda kt: (m4[:, 4 + kt, :, :], m4), [KWs, VWs],
                            q_fn=lambda kt, hh=hh: QaS[0:64, 0, hh * 4:hh * 4 + 4, :])
                gview = gsig[0:ST, hh * 12:(hh + 1) * 12].rearrange("p (g t) -> p g t", t=3)
                ov = osb[0:ST, hh * 256:(hh + 1) * 256].rearrange("p (g d) -> p g d", g=4)
                V(lambda e, gview=gview, ov=ov: e.tensor_tensor(out=ov, in0=acc_c[:, :, 0:64], in1=gview[:, :, 0:1].to_broadcast([ST, 4, 64]), op=ALU.mult), [acc_c, gsig], [osb])
                for bi, accX in ((1, accS), (2, accW)):
                    av = accX[0:ST, :].rearrange("p (g n) -> p g n", g=4)
                    V(lambda e, av=av: e.tensor_scalar(out=cst[0:ST, 8:12], in0=av[:, :, 64], scalar1=1e-30, scalar2=None, op0=ALU.max), [accX], [cst])
                    V(lambda e: e.reciprocal(out=cst[0:ST, 12:16], in_=cst[0:ST, 8:12]), [cst], [cst])
                    V(lambda e, gview=gview, bi=bi: e.tensor_tensor(out=cst[0:ST, 12:16], in0=cst[0:ST, 12:16], in1=gview[:, :, bi], op=ALU.mult), [cst, gsig], [cst])
                    V(lambda e, av=av: e.tensor_tensor(out=otmp[0:ST, :, :], in0=av[:, :, 0:64], in1=cst[0:ST, 12:16].unsqueeze(2).to_broadcast([ST, 4, 64]), op=ALU.mult), [accX, cst], [otmp])
                    V(lambda e, ov=ov: e.tensor_tensor(out=ov, in0=ov, in1=otmp[0:ST, :, :], op=ALU.add), [osb, otmp], [osb])
            ST_(oscr[HALF + r0:HALF + r0 + ST, :], osb[0:ST, :], [osb], q='sp')
    S.barrier()
    sper.close()

    with ExitStack() as ph:
        load_g(ph, [0])
        wr = load_w(ph, "wr", w_in, D, IN_W, 1304, IN_W)
        wno = load_w(ph, "wno", w_nsa_o, 512, D)
        wpw = load_w(ph, "wpw", w_pw, 512, D)
        wo = load_w(ph, "wo", w_out, D, D)
        cv = sb(ph, "cv", [128, 12], F32)
        LD(cv[:, :], cvec[:, :], [cv])
        wdw_sb = sb(ph, "wdw_sb", [31, 512], F32)
        LD(wdw_sb[:, :], w_dw[:, :], [wdw_sb])
        wdT = sb(ph, "wdT", [128, 4, 32], F32)
        pw_ = pmm.next()
        for c in range(4):
            PE(lambda e, c=c: e.transpose(out=pw_[:, c * 32:c * 32 + 31], in_=wdw_sb[0:31, c * 128:(c + 1) * 128], identity=identf[0:31, 0:31]), [wdw_sb, identf], [pw_])
        V(lambda e: e.tensor_copy(out=wdT[:, :, 0:31], in_=pw_[:, 0:128].rearrange("p (c j) -> p c j", c=4)[:, :, 0:31]), [pw_], [wdT])
        dg = sb(ph, "dg", [128, 4, 31, 128], BF16)
        for c in range(4):
            for j in range(31):
                eng = V if (j % 2 == 0) else G
                eng(lambda e, c=c, j=j: e.tensor_scalar(out=dg[:, c, j, :], in0=identf[:, :], scalar1=wdT[:, c, j:j + 1], scalar2=None, op0=ALU.mult), [identf, wdT], [dg])
        ones_f = sb(ph, "ones_f", [128, 128], F32)
        G(lambda e: e.memset(ones_f[:, :], 1.0 / 512.0), [], [ones_f])
        gluT_r = Ring([sb(ph, "gluT", [128, 4, 30 + 128], BF16) for _ in range(2)])
        for g_ in gluT_r.bufs:
            G(lambda e, g_=g_: e.memset(g_[:, :, :], 0.0), [], [g_])
        glu = sb(ph, "glu", [128, 512], F32)
        glub = sb(ph, "glub", [128, 512], BF16)
        gm = sb(ph, "gm", [128, 2048], F32)
        osb = sb(ph, "osb2", [128, 512], F32)
        osbb = sb(ph, "osbb", [128, 512], BF16)
        oT = sb(ph, "oT", [128, 4, 128], BF16)
        ynsa = sb(ph, "ynsa", [128, D], F32)
        gtm = sb(ph, "gtm", [128, D], F32)
        cconv = sb(ph, "cconv", [128, 4, 128], F32)
        csq = sb(ph, "csq", [128, 4, 128], F32)
        lnst = sb(ph, "lnst", [128, 4, 128], F32)
        csT = sb(ph, "csT", [128, 4, 128], BF16)
        mg = sb(ph, "mg", [128, D], F32)
        mgb = sb(ph, "mgb", [128, D], BF16)
        mT = sb(ph, "mT", [128, 8, 128], BF16)
        x1 = mg

        def proj_glu(hT, nt):
            def cons(p, g0, gn):
                c = g0
                if c < 512:
                    A(lambda e: e.copy(out=gtm[0:nt, c:c + gn], in_=p[0:nt, 0:gn]), [p], [gtm])
                else:
                    A(lambda e: e.activation(out=gtm[0:nt, c:c + gn], in_=p[0:nt, 0:gn], func=AF.Sigmoid), [p], [gtm])
            linear(hT, nt, 8, wr, 0, 1024, cons)
            V(lambda e: e.tensor_tensor(out=glu[0:nt, :], in0=gtm[0:nt, 0:512], in1=gtm[0:nt, 512:1024], op=ALU.mult), [gtm], [glu])
            V(lambda e: e.tensor_copy(out=glub[0:nt, :], in_=glu[0:nt, :]), [glu], [glub])

        def glu_to_T(gluT, nt, col0):
            p = pT.next()
            for c in range(4):
                PE(lambda e, c=c, p=p: e.transpose(out=p[:, c, 0:nt], in_=glub[0:nt, c * 128:(c + 1) * 128], identity=ident[0:nt, 0:nt]), [glub, ident], [p])
            A(lambda e, p=p: e.copy(out=gluT[:, :, col0:col0 + nt], in_=p[:, 0:4, 0:nt]), [p], [gluT])

        def b2_rest(gluT, xt, hT, nt, o_src, dst_ap):
            def consm(p, g0, gn):
                c = g0 - 1024
                A(lambda e: e.activation(out=gm[0:nt, c:c + gn], in_=p[0:nt, 0:gn], func=AF.Sigmoid), [p], [gm])
            linear(hT, nt, 8, wr, 1024, 1024 + 2048, consm)
            LD(osb[0:nt, :], o_src, [osb])
            V(lambda e: e.tensor_copy(out=osbb[0:nt, :], in_=osb[0:nt, :]), [osb], [osbb])
            transpose_bf(osbb, nt, 4, oT)

            def consn(p, g0, gn):
                V(lambda e: e.tensor_tensor(out=mg[0:nt, g0:g0 + gn], in0=p[0:nt, 0:gn], in1=gm[0:nt, g0:g0 + gn], op=ALU.mult), [p, gm], [mg])
            linear(oT, nt, 4, wno, 0, D, consn)
            for c in range(4):
                p = pmm.next()
                for j in range(31):
                    PE(lambda e, c=c, j=j, p=p: e.matmul(p[:, 0:nt], lhsT=dg[:, c, j, :], rhs=gluT[:, c, j:j + nt], start=(j == 0), stop=(j == 30)), [dg, gluT], [p], inc=(j == 30))
                A(lambda e, c=c, p=p: e.activation(out=cconv[:, c, 0:nt], in_=p[:, 0:nt], func=AF.Identity, bias=cv[:, c:c + 1]), [p, cv], [cconv])
            V(lambda e: e.tensor_tensor(out=csq[:, :, 0:nt], in0=cconv[:, :, 0:nt], in1=cconv[:, :, 0:nt], op=ALU.mult), [cconv], [csq])
            pst = pmm.next()
            for c in range(4):
                PE(lambda e, c=c: e.matmul(pst[:, 0:nt], lhsT=ones_f[:, :], rhs=cconv[:, c, 0:nt], start=(c == 0), stop=(c == 3)), [ones_f, cconv], [pst], inc=(c == 3))
            pst2 = pmm.next()
            for c in range(4):
                PE(lambda e, c=c: e.matmul(pst2[:, 0:nt], lhsT=ones_f[:, :], rhs=csq[:, c, 0:nt], start=(c == 0), stop=(c == 3)), [ones_f, csq], [pst2], inc=(c == 3))
            V(lambda e: e.tensor_copy(out=lnst[:, 0, 0:nt], in_=pst[:, 0:nt]), [pst], [lnst])
            V(lambda e: e.tensor_tensor(out=lnst[:, 1, 0:nt], in0=lnst[:, 0, 0:nt], in1=lnst[:, 0, 0:nt], op=ALU.mult), [lnst], [lnst])
            V(lambda e: e.tensor_tensor(out=lnst[:, 1, 0:nt], in0=pst2[:, 0:nt], in1=lnst[:, 1, 0:nt], op=ALU.subtract), [pst2, lnst], [lnst])
            V(lambda e: e.tensor_scalar(out=lnst[:, 1, 0:nt], in0=lnst[:, 1, 0:nt], scalar1=EPS, scalar2=None, op0=ALU.add), [lnst], [lnst])
            A(lambda e: e.activation(out=lnst[:, 2, 0:nt], in_=lnst[:, 1, 0:nt], func=AF.Sqrt), [lnst], [lnst])
            V(lambda e: e.reciprocal(out=lnst[:, 3, 0:nt], in_=lnst[:, 2, 0:nt]), [lnst], [lnst])
            V(lambda e: e.tensor_tensor(out=cconv[:, :, 0:nt], in0=cconv[:, :, 0:nt], in1=lnst[:, 0:1, 0:nt].to_broadcast([128, 4, nt]), op=ALU.subtract), [cconv, lnst], [cconv])
            V(lambda e: e.tensor_tensor(out=cconv[:, :, 0:nt], in0=cconv[:, :, 0:nt], in1=lnst[:, 3:4, 0:nt].to_broadcast([128, 4, nt]), op=ALU.mult), [cconv, lnst], [cconv])
            for c in range(4):
                V(lambda e, c=c: e.tensor_scalar(out=cconv[:, c, 0:nt], in0=cconv[:, c, 0:nt], scalar1=cv[:, 4 + c:5 + c], scalar2=cv[:, 8 + c:9 + c], op0=ALU.mult, op1=ALU.add), [cconv, cv], [cconv])
            A(lambda e: e.activation(out=csq[:, :, 0:nt], in_=cconv[:, :, 0:nt], func=AF.Sigmoid), [cconv], [csq])
            V(lambda e: e.tensor_tensor(out=csT[:, :, 0:nt], in0=cconv[:, :, 0:nt], in1=csq[:, :, 0:nt], op=ALU.mult), [cconv, csq], [csT])

            def consc(p, g0, gn):
                V(lambda e: e.tensor_tensor(out=ynsa[0:nt, g0:g0 + gn], in0=p[0:nt, 0:gn], in1=gm[0:nt, D + g0:D + g0 + gn], op=ALU.mult), [p, gm], [ynsa])
            linear(csT, nt, 4, wpw, 0, D, consc)
            V(lambda e: e.tensor_tensor(out=mgb[0:nt, :], in0=mg[0:nt, :], in1=ynsa[0:nt, :], op=ALU.add), [mg, ynsa], [mgb])
            transpose_bf(mgb, nt, 8, mT)

            def conso(p, g0, gn):
                V(lambda e: e.tensor_tensor(out=x1[0:nt, g0:g0 + gn], in0=p[0:nt, 0:gn], in1=xt[0:nt, g0:g0 + gn], op=ALU.add), [p, xt], [x1])
            linear(mT, nt, 8, wo, 0, D, conso)
            ST_(dst_ap, x1[0:nt, :], [x1], q='sp')

        def b2_part1(qt, prev_gluT):
            ti = NT_OWN + qt
            xt = xt_r.next()
            LD(xt[:, :], xp[ti * 128:(ti + 1) * 128, :], [xt])
            h = h_r.next()
            rmsnorm(xt, 128, 0, h)
            hT = hT_r.next()
            transpose_bf(h, 128, 8, hT)
            proj_glu(hT, 128)
            gluT = gluT_r.next()
            glu_to_T(gluT, 128, 30)
            if prev_gluT is not None:
                V(lambda e: e.tensor_copy(out=gluT[:, :, 0:30], in_=prev_gluT[:, :, 128:158]), [prev_gluT], [gluT])
            if qt == NT_OWN - 1:
                ST_(pconv[:, :], glu[98:128, :], [glu], q='sp')
            return (gluT, xt, hT)

        st_prev = b2_part1(-1, None)
        st_next = b2_part1(0, st_prev[0])
        for qt in range(NT_OWN):
            st_cur = st_next
            if qt + 1 < NT_OWN:
                st_next = b2_part1(qt + 1, st_cur[0])
            b2_rest(st_cur[0], st_cur[1], st_cur[2], 128, oscr[qt * 128:(qt + 1) * 128, :], x1s[qt * 128:(qt + 1) * 128, :])

        for b in range(SB):
            r0 = b * ST
            gluT = gluT_r.next()
            LD(glu[0:30, :], sconv_in[b, :, :], [glu])
            V(lambda e: e.tensor_copy(out=glub[0:30, :], in_=glu[0:30, :]), [glu], [glub])
            glu_to_T(gluT, 30, 0)
            xt = xt_r.next()
            LD(xt[0:ST, :], xs[r0:r0 + ST, :], [xt])
            h = h_r.next()
            rmsnorm(xt, ST, 0, h)
            hT = hT_r.next()
            transpose_bf(h, ST, 8, hT)
            proj_glu(hT, ST)
            glu_to_T(gluT, ST, 30)
            ST_(sconv[b, 0:30 - ST, :], sconv_in[b, ST:30, :], [], q='sp')
            ST_(sconv[b, 30 - ST:30, :], glu[0:ST, :], [glu], q='sp')
            b2_rest(gluT, xt, hT, ST, oscr[HALF + r0:HALF + r0 + ST, :], x1s[HALF + r0:HALF + r0 + ST, :])
    S.barrier()

    with ExitStack() as ph:
        load_g(ph, [1, 2, 3, 4])
        wxq = load_w(ph, "wxq", w_xq, D, 512)
        wxo = load_w(ph, "wxo", w_xo, 512, D)
        wup = load_w(ph, "wup", w_up, D, 4096, defer=True)
        wdn = load_w(ph, "wdn", w_down, 4096, D, defer=True)
        mkT = sb(ph, "mkT", [128, 4, 256], BF16)
        mva = sb(ph, "mva", [128, 2, 4, 130], BF16)
        G(lambda e: e.memset(mva[:, :, :, 128:130], 1.0), [], [mva])
        mkf = sb(ph, "mkf", [128, 512], F32)
        qx = sb(ph, "qx", [128, 512], BF16)
        mkb = qx
        with ExitStack() as ph2:
            wxk = load_w(ph2, "wxk", w_xk, D, 512)
            wxv = load_w(ph2, "wxv", w_xv, D, 512)
            wup.issue()
            wdn.issue()
            for mt in range(2):
                xt = xt_r.next()
                LD(xt[:, :], memp[mt * 128:(mt + 1) * 128, :], [xt])
                h = h_r.next()
                rmsnorm(xt, 128, 2, h)
                hT = hT_r.next()
                transpose_bf(h, 128, 8, hT)

                def consk(p, g0, gn):
                    A(lambda e: e.copy(out=mkf[:, :], in_=p[:, 0:512]), [p], [mkf])
                linear(hT, 128, 8, wxk, 0, 512, consk)
                ST_(pmk[mt * 128:(mt + 1) * 128, :], mkf[:, :], [mkf], q='sp')
                V(lambda e: e.tensor_copy(out=mkb[:, :], in_=mkf[:, :]), [mkf], [mkb])
                p = pT.next()
                for hd in range(4):
                    PE(lambda e, hd=hd, p=p: e.transpose(out=p[:, hd, :], in_=mkb[:, hd * 128:(hd + 1) * 128], identity=ident[:, :]), [mkb, ident], [p])
                A(lambda e, p=p, mt=mt: e.copy(out=mkT[:, :, mt * 128:(mt + 1) * 128], in_=p[:, 0:4, :]), [p], [mkT])

                def consv(p, g0, gn):
                    A(lambda e: e.copy(out=mkf[:, :], in_=p[:, 0:512]), [p], [mkf])
                linear(hT, 128, 8, wxv, 0, 512, consv)
                ST_(pmv[mt * 128:(mt + 1) * 128, :], mkf[:, :], [mkf], q='sp')
                V(lambda e, mt=mt: e.tensor_copy(out=mva[:, mt, :, 0:128], in_=mkf[:, :].rearrange("p (h d) -> p h d", h=4)), [mkf], [mva])
            S.barrier(only=('pe',))

        qxT = sb(ph, "qxT", [128, 4, 128], BF16)
        PX = sb(ph, "PX", [128, 2, 4, 128], BF16)
        ox = sb(ph, "ox", [128, 512], BF16)
        oxT = sb(ph, "oxT", [128, 4, 128], BF16)
        uT = sb(ph, "uT", [128, 32, 128], BF16)
        ur = mkf
        yo = sb(ph, "yo", [128, D], F32)
        cst = sb(ph, "cst2", [128, 8], F32)

        def cd_pre(src_ap, nt):
            xt = xt_r.next()
            LD(xt[0:nt, :], src_ap, [xt])
            h = h_r.next()
            rmsnorm(xt, nt, 1, h)
            hT = hT_r.next()
            transpose_bf(h, nt, 8, hT)

            def consq(p, g0, gn):
                A(lambda e: e.copy(out=qx[0:nt, :], in_=p[0:nt, 0:512]), [p], [qx])
            linear(hT, nt, 8, wxq, 0, 512, consq)
            transpose_bf(qx, nt, 4, qxT)
            return xt

        def cd_core(nq, col0):
            for mt in range(2):
                p = pmm.next()
                for hd in range(4):
                    PE(lambda e, hd=hd, mt=mt, p=p: e.matmul(p[:, hd * nq:(hd + 1) * nq], lhsT=mkT[:, hd, mt * 128:(mt + 1) * 128], rhs=qxT[:, hd, col0:col0 + nq], start=True, stop=True), [mkT, qxT], [p], inc=(hd == 3))
                A(lambda e, mt=mt, p=p: e.activation(out=PX[:, mt, :, 0:nq], in_=p[:, 0:4 * nq].rearrange("k (h q) -> k h q", h=4), func=AF.Exp, scale=128.0 ** -0.5), [p], [PX])
            for half in range(2):
                pa = pacc.next()
                for hh in range(2):
                    hd = half * 2 + hh
                    for mt in range(2):
                        PE(lambda e, hd=hd, hh=hh, mt=mt, pa=pa: e.matmul(pa[0:nq, hh * 130:hh * 130 + 129], lhsT=PX[:, mt, hd, 0:nq], rhs=mva[:, mt, hd, 0:129], start=(mt == 0), stop=(mt == 1)), [PX, mva], [pa], inc=(mt == 1 and hh == 1))
                pav = pa[0:nq, 0:260].rearrange("p (h n) -> p h n", h=2)
                V(lambda e, pav=pav: e.reciprocal(out=cst[0:nq, 0:2], in_=pav[:, :, 128]), [pa], [cst])
                V(lambda e, pav=pav, half=half: e.tensor_tensor(out=ox[0:nq, half * 256:(half + 1) * 256].rearrange("p (h d) -> p h d", h=2), in0=pav[:, :, 0:128], in1=cst[0:nq, 0:2].unsqueeze(2).to_broadcast([nq, 2, 128]), op=ALU.mult), [pa, cst], [ox])
            transpose_bf(ox, nq, 4, oxT, col0=col0)

        def cd_post(xt, nt):
            x2 = xt

            def conso(p, g0, gn):
                V(lambda e: e.tensor_tensor(out=x2[0:nt, g0:g0 + gn], in0=p[0:nt, 0:gn], in1=xt[0:nt, g0:g0 + gn], op=ALU.add), [p, xt], [x2])
            linear(oxT, nt, 4, wxo, 0, D, conso)
            return x2

        def cd_part1(src_ap, nt):
            xt = cd_pre(src_ap, nt)
            cd_core(nt, 0)
            return cd_post(xt, nt)

        def cd_part2(x2, nt, dst_ap):
            x3 = x2
            h2 = h_r.next()
            rmsnorm(x2, nt, 3, h2)
            hT2 = hT_r.next()
            transpose_bf(h2, nt, 8, hT2)
            for c4 in range(8):
                p = pmm.next()
                for cc in range(4):
                    c = c4 * 4 + cc
                    for i in range(8):
                        PE(lambda e, c=c, cc=cc, i=i, p=p: e.matmul(p[:, cc * nt:(cc + 1) * nt], lhsT=wup[:, i, c * 128:(c + 1) * 128], rhs=hT2[:, i, 0:nt], start=(i == 0), stop=(i == 7)), [wup.ktoks[i], hT2], [p], inc=(i == 7 and cc == 3))
                A(lambda e, p=p: e.activation(out=ur[:, 0:4 * nt], in_=p[:, 0:4 * nt], func=AF.Relu), [p], [ur])
                V(lambda e, c4=c4: e.tensor_tensor(out=uT[:, c4 * 4:(c4 + 1) * 4, 0:nt], in0=ur[:, 0:4 * nt].rearrange("p (c q) -> p c q", c=4), in1=ur[:, 0:4 * nt].rearrange("p (c q) -> p c q", c=4), op=ALU.mult), [ur], [uT])

            def consd(p, g0, gn):
                V(lambda e: e.tensor_tensor(out=x3[0:nt, g0:g0 + gn], in0=p[0:nt, 0:gn], in1=x2[0:nt, g0:g0 + gn], op=ALU.add), [p, x2], [x3])
            linear(uT, nt, 32, wdn, 0, D, consd)
            rmsnorm(x3, nt, 4, yo)
            ST_(dst_ap, yo[0:nt, :], [yo], q='sp')

        x2_next = cd_part1(x1s[0:128, :], 128)
        for qt in range(NT_OWN):
            x2_cur = x2_next
            if qt + 1 < NT_OWN:
                x2_next = cd_part1(x1s[(qt + 1) * 128:(qt + 2) * 128, :], 128)
            cd_part2(x2_cur, 128, yp[qt * 128:(qt + 1) * 128, :])

        NS = SB * ST
        xts = cd_pre(x1s[HALF:HALF + NS, :], NS)
        for b in range(SB):
            for mt in range(2):
                LD(mkf[:, :], cmk[b, mt * 128:(mt + 1) * 128, :], [mkf])
                V(lambda e: e.tensor_copy(out=mkb[:, :], in_=mkf[:, :]), [mkf], [mkb])
                p = pT.next()
                for hd in range(4):
                    PE(lambda e, hd=hd, p=p: e.transpose(out=p[:, hd, :], in_=mkb[:, hd * 128:(hd + 1) * 128], identity=ident[:, :]), [mkb, ident], [p])
                A(lambda e, p=p, mt=mt: e.copy(out=mkT[:, :, mt * 128:(mt + 1) * 128], in_=p[:, 0:4, :]), [p], [mkT])
                LD(mkf[:, :], cmv[b, mt * 128:(mt + 1) * 128, :], [mkf])
                V(lambda e, mt=mt: e.tensor_copy(out=mva[:, mt, :, 0:128], in_=mkf[:, :].rearrange("p (h d) -> p h d", h=4)), [mkf], [mva])
            cd_core(ST, b * ST)
        cd_part2(cd_post(xts, NS), NS, ys[:, :])

    S.finish()
    wk.close()
    glob.close()
    return nc


def _rope_tab(pos):
    half = 8
    inv = (500000.0 ** (-np.arange(half, dtype=np.float32) / half)).astype(np.float32)
    ang = pos.astype(np.float32)[:, None] * inv[None, :]
    return np.concatenate([np.cos(ang), np.sin(ang)], axis=1).astype(np.float32)


def _core_consts(half):
    c = {}
    off = 0 if half == 1 else -HALF
    lpos = np.arange(S_FULL)
    gpos = lpos + off
    cs = _rope_tab(np.maximum(gpos, 0))
    c["c_cs"] = np.ascontiguousarray(cs.reshape(NT_ALL, 128, 16).transpose(1, 0, 2))
    c["c_valid"] = np.ascontiguousarray((gpos >= 0).astype(np.float32).reshape(NT_ALL, 128).T)
    t = gpos[HALF:]
    n = np.arange(64)
    gn = n + (0 if half == 1 else -32)
    elig = (gn[None, :] >= 0) & (gn[None, :] * 64 <= t[:, None])
    cur = t // 64
    forced = (gn[None, :] == 0) | (gn[None, :] == cur[:, None]) | (gn[None, :] == cur[:, None] - 1)
    A = (elig & ~forced).astype(np.float32)
    B = np.where(~elig, -1.0, np.where(forced, 1.0e4, 0.0)).astype(np.float32)
    c["c_A"] = np.ascontiguousarray(A.reshape(NT_OWN, 128, 64).transpose(1, 0, 2))
    c["c_B"] = np.ascontiguousarray(B.reshape(NT_OWN, 128, 64).transpose(1, 0, 2))
    cl = np.arange(256)
    gc = cl + (0 if half == 1 else -128)
    cm = (gc[:, None] >= 0) & (cl[:, None] <= 254) & (16 * gc[:, None] + 31 <= t[None, :])
    c["c_cmask"] = np.ascontiguousarray(cm.astype(np.float32).reshape(2, 128, HALF).transpose(1, 0, 2))
    start = cl[:, None] * 16
    sel0 = n[None, :] * 64
    ov = np.clip(np.minimum(start + 32, sel0 + 64) - np.maximum(start, sel0), 0, None) / 32.0
    c["c_M"] = np.ascontiguousarray(ov.astype(np.float32).reshape(2, 128, 64).transpose(1, 0, 2))
    return c


def _common_consts():
    c = {}
    c["c_ident"] = np.eye(128, dtype=np.float32)
    key = np.arange(4096)
    c["c_E"] = (key[None, :] // 64 == np.arange(64)[:, None]).astype(np.float32)
    kk = np.arange(128)[:, None]
    qq = np.arange(128)[None, :]
    c["c_tri"] = np.ascontiguousarray(np.stack([(kk <= qq), (kk > qq)], axis=1).astype(np.float32))
    c["c_pidx"] = np.arange(128, dtype=np.float32)[:, None]
    c["c_c8"] = np.ascontiguousarray(np.broadcast_to(np.arange(8, dtype=np.float32)[None, :], (NPAGE, 8)))
    tpos = PAST + np.arange(ST)
    c["s_cs"] = _rope_tab(tpos)
    n = np.arange(192)
    elig = (n[None, :] <= 128) & (n[None, :] * 64 <= tpos[:, None])
    cur = tpos // 64
    forced = (n[None, :] == 0) | (n[None, :] == cur[:, None]) | (n[None, :] == cur[:, None] - 1)
    c["s_A"] = (elig & ~forced).astype(np.float32)
    c["s_B"] = np.where(~elig, -1.0, np.where(forced, 1.0e4, 0.0)).astype(np.float32)
    cl = np.arange(512)
    c["s_cmask"] = np.ascontiguousarray((cl <= 510).astype(np.float32).reshape(4, 128).T)
    start = cl[:, None] * 16
    sel0 = n[None, :] * 64
    ov = np.clip(np.minimum(start + 32, sel0 + 64) - np.maximum(start, sel0), 0, None) / 32.0
    ov[:, 129:] = 0
    c["s_M"] = np.ascontiguousarray(ov.astype(np.float32).reshape(4, 128, 192).transpose(1, 0, 2))
    r = np.arange(640)
    kpos = np.where(r < 512, PAST - 512 + r, PAST + (r - 512))
    kvalid = r < 512 + ST
    dt = tpos[None, :] - kpos[:, None]
    wm = kvalid[:, None] & (dt >= 0) & (dt < 512)
    c["s_wmask"] = np.ascontiguousarray(wm.astype(np.float32).reshape(5, 128, ST).transpose(1, 0, 2))
    c["s_wvalid"] = np.ascontiguousarray(kvalid.astype(np.float32).reshape(5, 128).T)
    r2 = np.arange(128)
    c["s_lmask"] = ((r2[:, None] < ST) & (r2[:, None] <= np.arange(ST)[None, :])).astype(np.float32)
    r3 = np.arange(66 * 128)
    c["s_valid"] = np.ascontiguousarray((r3 < PAST + ST).astype(np.float32).reshape(66, 128).T)
    return c


_PROG = {}


def kernel(x_prompt, x_sample, mem_prompt, cache_k_cmp, cache_v_cmp, cache_k_slc, cache_v_slc, cache_k_win,
           cache_v_win, state_conv, cache_mem_k, cache_mem_v, page_table, norm_mix, w_in, cmp_pos_k, cmp_pos_v,
           w_ck1, w_ck2, w_cv1, w_cv2, w_nsa_o, w_dw, b_dw, conv_ln_g, conv_ln_b, w_pw, w_out, norm_x, norm_mem,
           w_xq, w_xk, w_xv, w_xo, norm_ff, w_up, w_down, norm_final):
    f = lambda a: np.ascontiguousarray(np.asarray(a, dtype=np.float32))
    if "nc" not in _PROG:
        _PROG["nc"] = build_program()
    nc = _PROG["nc"]
    common = _common_consts()
    shared = {
        "w_in": f(w_in[0]), "cpos_k": f(cmp_pos_k[0]), "cpos_v": f(cmp_pos_v[0]),
        "w_ck1": f(w_ck1[0]), "w_cv1": f(w_cv1[0]), "w_ck2": f(w_ck2[0]), "w_cv2": f(w_cv2[0]),
        "w_nsa_o": f(w_nsa_o[0]), "w_dw": f(w_dw[0]), "w_pw": f(w_pw[0]), "w_out": f(w_out[0]),
        "w_xq": f(w_xq[0]), "w_xk": f(w_xk[0]), "w_xv": f(w_xv[0]), "w_xo": f(w_xo[0]),
        "w_up": f(w_up[0]), "w_down": f(w_down[0]),
        "cvec": np.ascontiguousarray(np.stack([f(b_dw[0]).reshape(4, 128), f(conv_ln_g[0]).reshape(4, 128), f(conv_ln_b[0]).reshape(4, 128)], 0).reshape(12, 128).T),
        "gvec": np.ascontiguousarray(np.broadcast_to(np.stack([f(norm_mix[0]), f(norm_x[0]), f(norm_mem[0]), f(norm_ff[0]), f(norm_final)], 0)[None], (128, 5, D))),
        "pk_cmp": f(cache_k_cmp[0]).reshape(2560 * 8, 2048), "pv_cmp": f(cache_v_cmp[0]).reshape(2560 * 8, 2048),
        "pk_slc": f(cache_k_slc[0]).reshape(2560 * 8, 2048), "pv_slc": f(cache_v_slc[0]).reshape(2560 * 8, 2048),
    }
    shared.update(common)
    cc = [_core_consts(0), _core_consts(1)]
    xpr = f(x_prompt)
    in_maps = []
    for core in range(8):
        b, half = core // 2, core % 2
        m = dict(shared)
        m.update(cc[half])
        xp_ = np.zeros((S_FULL, D), np.float32)
        if half == 1:
            xp_[:] = xpr[b]
        else:
            xp_[HALF:] = xpr[b, :HALF]
        m["xp"] = xp_
        m["memp"] = f(mem_prompt[b])
        sl = slice(core * SB, (core + 1) * SB)
        m["xs"] = f(x_sample[sl]).reshape(SB * ST, D)
        m["ckw"] = f(cache_k_win[0, sl]).reshape(SB, 512, 128)
        m["cvw"] = f(cache_v_win[0, sl]).reshape(SB, 512, 128)
        m["sconv_in"] = f(state_conv[0, sl])
        m["cmk"] = f(cache_mem_k[0, sl]).reshape(SB, 256, 512)
        m["cmv"] = f(cache_mem_v[0, sl]).reshape(SB, 256, 512)
        m["ptab"] = np.ascontiguousarray(np.asarray(page_table[sl], dtype=np.int32))
        in_maps.append(m)
    res = run_bass_kernel_spmd(nc, in_maps, core_ids=list(range(8))).results

    B4 = 4
    y_prompt = np.zeros((B4, S_FULL, D), np.float32)
    pst = [np.zeros((1, B4, S_FULL, 2, 64), np.float32) for _ in range(4)]
    pwin = [np.zeros((1, B4, 512, 2, 64), np.float32) for _ in range(2)]
    p_conv = np.zeros((1, B4, 30, 512), np.float32)
    p_mk = np.zeros((1, B4, 256, 4, 128), np.float32)
    p_mv = np.zeros((1, B4, 256, 4, 128), np.float32)
    y_sample = np.zeros((32, ST, D), np.float32)
    sst = [np.zeros((1, 32, ST, 2, 64), np.float32) for _ in range(4)]
    swin = [np.zeros((1, 32, 512, 2, 64), np.float32) for _ in range(2)]
    s_conv = np.zeros((1, 32, 30, 512), np.float32)
    for core in range(8):
        b, half = core // 2, core % 2
        r = res[core]
        ts = slice(half * HALF, (half + 1) * HALF)
        y_prompt[b, ts] = r["yp"]
        for i in range(4):
            pst[i][0, b, ts] = r["okv"][i].reshape(HALF, 2, 64)
        if half == 1:
            for i in range(2):
                pwin[i][0, b] = r["okv"][4 + i][HALF - 512:].reshape(512, 2, 64)
            p_conv[0, b] = r["pconv"]
            p_mk[0, b] = r["pmk"].reshape(256, 4, 128)
            p_mv[0, b] = r["pmv"].reshape(256, 4, 128)
        sl = slice(core * SB, (core + 1) * SB)
        y_sample[sl] = r["ys"].reshape(SB, ST, D)
        for i in range(4):
            sst[i][0, sl] = r["skv"][i].reshape(SB, ST, 2, 64)
        swin[0][0, sl] = r["swk"].reshape(SB, 512, 2, 64)
        swin[1][0, sl] = r["swv"].reshape(SB, 512, 2, 64)
        s_conv[0, sl] = r["sconv"]
    return (y_prompt, y_sample, pst[0], pst[1], pst[2], pst[3], pwin[0], pwin[1], p_conv, p_mk, p_mv,
            sst[0], sst[1], sst[2], sst[3], swin[0], swin[1], s_conv)
```
